# Optimizing a Trainium2 kernel written in Bass

```python
import math
import jax, jax.numpy as jnp
from jax import lax
import numpy as np

D_MODEL = 2048
BATCH = 4
SEQ = 8192
DEPTH = 1

N_HEADS_A = 16
D_LAT = 256
D_HEAD_A = 128
D_ATTN = N_HEADS_A * D_HEAD_A
N_HEADS_IDX = 16
D_IDX = 64
TOPK_MAX = 256
Q_BLOCK = 128
D_RNN = 2048
N_BLK_RNN = 16
D_BLK_RNN = D_RNN // N_BLK_RNN
CONV_W = 4
LRU_C = 8.0
N_BUCKETS = 32
MAX_DIST = 128
ALPHA = (2 * DEPTH) ** 0.25
BETA = (8 * DEPTH) ** -0.25
LN_EPS = 1e-5

IN_SIZES = (N_HEADS_A * D_LAT, D_LAT, D_ATTN, N_HEADS_IDX * D_IDX, D_IDX, N_HEADS_IDX,
            D_RNN, D_RNN, D_MODEL, D_MODEL)
D_IN = sum(IN_SIZES)
SPLIT_POINTS = tuple(int(v) for v in np.cumsum(IN_SIZES)[:-1])

kernel_name = "hybrid_dsa_rglru_gated_deepnorm"


def _layernorm(x, g, b):
    xf = x.astype(jnp.float32)
    mu = jnp.mean(xf, axis=-1, keepdims=True)
    var = jnp.mean(jnp.square(xf - mu), axis=-1, keepdims=True)
    return ((xf - mu) * lax.rsqrt(var + LN_EPS)).astype(x.dtype) * g + b


def _rmsnorm(x, g):
    xf = x.astype(jnp.float32)
    return (xf * lax.rsqrt(jnp.mean(jnp.square(xf), axis=-1, keepdims=True) + LN_EPS)).astype(x.dtype) * g


def _t5_bucket(dist):
    n = jnp.maximum(dist, 0)
    max_exact = N_BUCKETS // 2
    nf = jnp.maximum(n, 1).astype(jnp.float32)
    large = max_exact + (jnp.log(nf / max_exact) / math.log(MAX_DIST / max_exact)
                         * (N_BUCKETS - max_exact)).astype(jnp.int32)
    large = jnp.minimum(large, N_BUCKETS - 1)
    return jnp.where(n < max_exact, n, large)


def _sparse_attention(q_lat, c_kv, q_idx, k_idx, w_idx, rel_bias, w_uv):
    B, T = q_lat.shape[:2]
    K = min(TOPK_MAX, T // 4)
    nb = T // Q_BLOCK
    key_pos = jnp.arange(T, dtype=jnp.int32)
    idx_scale = (D_IDX ** -0.5) * (N_HEADS_IDX ** -0.5)
    att_scale = D_LAT ** -0.5

    def blockify(a):
        return jnp.moveaxis(a.reshape((B, nb, Q_BLOCK) + a.shape[2:]), 1, 0)

    def one_block(args):
        qb, qib, wb, blk = args
        q_pos = blk * Q_BLOCK + jnp.arange(Q_BLOCK, dtype=jnp.int32)
        s = jax.nn.relu(jnp.einsum('bqhd,bsd->bqhs', qib, k_idx).astype(jnp.float32))
        score = jnp.einsum('bqhs,bqh->bqs', s, wb.astype(jnp.float32)) * idx_scale
        causal = key_pos[None, :] <= q_pos[:, None]
        score = jnp.where(causal[None], score, -jnp.inf)
        _, sel = lax.top_k(score, K)
        c_sel = jax.vmap(lambda c_b, i_b: c_b[i_b])(c_kv, sel)
        logits = jnp.einsum('bqhd,bqkd->bqhk', qb, c_sel).astype(jnp.float32) * att_scale
        dist = q_pos[None, :, None] - sel
        bias = rel_bias[_t5_bucket(dist)]
        logits = logits + jnp.moveaxis(bias, -1, 2).astype(jnp.float32)
        logits = jnp.where((dist >= 0)[:, :, None, :], logits, -jnp.inf)
        p = jax.nn.softmax(logits, axis=-1).astype(c_sel.dtype)
        o = jnp.einsum('bqhk,bqkd->bqhd', p, c_sel)
        o = jnp.einsum('bqhd,hde->bqhe', o, w_uv)
        return o.reshape(B, Q_BLOCK, D_ATTN)

    out = lax.map(one_block, (blockify(q_lat), blockify(q_idx), blockify(w_idx),
                              jnp.arange(nb, dtype=jnp.int32)))
    return jnp.moveaxis(out, 0, 1).reshape(B, T, D_ATTN)


def _rglru(xr, conv_w, conv_b, w_gate_a, b_gate_a, w_gate_x, b_gate_x, lru_lambda):
    B, T, _ = xr.shape
    xp = jnp.pad(xr, ((0, 0), (CONV_W - 1, 0), (0, 0)))
    xc = conv_b + sum(conv_w[k] * xp[:, k:k + T] for k in range(CONV_W))
    xb = xc.reshape(B, T, N_BLK_RNN, D_BLK_RNN)
    r = jax.nn.sigmoid(jnp.einsum('btnd,nde->btne', xb, w_gate_a).reshape(B, T, D_RNN) + b_gate_a)
    i = jax.nn.sigmoid(jnp.einsum('btnd,nde->btne', xb, w_gate_x).reshape(B, T, D_RNN) + b_gate_x)
    log_a = -LRU_C * r.astype(jnp.float32) * jax.nn.softplus(-lru_lambda.astype(jnp.float32))
    a = jnp.exp(log_a)
    mult = jnp.sqrt(-jnp.expm1(2.0 * log_a))
    mult = jnp.where(jnp.arange(T)[None, :, None] == 0, 1.0, mult)
    b = mult * (i * xc).astype(jnp.float32)

    def combine(left, right):
        a1, b1 = left
        a2, b2 = right
        return a1 * a2, a2 * b1 + b2

    _, h = lax.associative_scan(combine, (a, b), axis=1)
    return h.astype(xr.dtype)


def setup_inputs(seed: int = 0) -> dict:
    key = jax.random.key(seed)
    ks = jax.random.split(key, 18)
    f32 = jnp.float32
    nrm = lambda k, shape, s: jax.random.normal(k, shape, f32) * s
    u = jax.random.uniform(ks[9], (DEPTH, D_RNN), f32, 0.9, 0.999)
    p = u ** (1.0 / LRU_C)
    lru_lambda = jnp.log(p) - jnp.log1p(-p)
    return {
        "x": jax.random.normal(ks[0], (BATCH, SEQ, D_MODEL), f32),
        "w_in": nrm(ks[1], (DEPTH, D_MODEL, D_IN), D_MODEL ** -0.5),
        "kv_norm_g": 1.0 + nrm(ks[2], (DEPTH, D_LAT), 0.02),
        "w_uv": nrm(ks[3], (DEPTH, N_HEADS_A, D_LAT, D_HEAD_A), D_LAT ** -0.5),
        "w_branch_a": nrm(ks[4], (DEPTH, D_ATTN, D_MODEL), BETA * D_ATTN ** -0.5),
        "conv_w": nrm(ks[5], (DEPTH, CONV_W, D_RNN), CONV_W ** -0.5),
        "conv_b": nrm(ks[6], (DEPTH, D_RNN), 0.02),
        "w_gate_a": nrm(ks[7], (DEPTH, N_BLK_RNN, D_BLK_RNN, D_BLK_RNN), D_BLK_RNN ** -0.5),
        "b_gate_a": nrm(ks[8], (DEPTH, D_RNN), 0.02),
        "w_gate_x": nrm(ks[10], (DEPTH, N_BLK_RNN, D_BLK_RNN, D_BLK_RNN), D_BLK_RNN ** -0.5),
        "b_gate_x": nrm(ks[11], (DEPTH, D_RNN), 0.02),
        "lru_lambda": lru_lambda,
        "w_branch_b": nrm(ks[12], (DEPTH, D_RNN, D_MODEL), BETA * D_RNN ** -0.5),
        "rel_bias": nrm(ks[13], (N_BUCKETS, N_HEADS_A), 0.5),
        "w_out": nrm(ks[14], (DEPTH, D_MODEL, D_MODEL), BETA * D_MODEL ** -0.5),
        "ln_g": 1.0 + nrm(ks[15], (DEPTH, D_MODEL), 0.02),
        "ln_b": nrm(ks[16], (DEPTH, D_MODEL), 0.02),
    }


def reference(x, w_in, kv_norm_g, w_uv, w_branch_a, conv_w, conv_b, w_gate_a, b_gate_a,
              w_gate_x, b_gate_x, lru_lambda, w_branch_b, rel_bias, w_out, ln_g, ln_b):
    B, T, _ = x.shape
    for l in range(DEPTH):
        proj = jnp.einsum('btd,dc->btc', x, w_in[l])
        (q_lat, c_kv, attn_gate, q_idx, k_idx, w_idx,
         x_rnn, rnn_gate, g_a, g_b) = jnp.split(proj, SPLIT_POINTS, axis=-1)
        q_lat = q_lat.reshape(B, T, N_HEADS_A, D_LAT)
        c_kv = _rmsnorm(c_kv, kv_norm_g[l])
        q_idx = q_idx.reshape(B, T, N_HEADS_IDX, D_IDX)
        attn = _sparse_attention(q_lat, c_kv, q_idx, k_idx, w_idx, rel_bias, w_uv[l])
        branch_a = jnp.einsum('btc,cd->btd', attn * jax.nn.silu(attn_gate), w_branch_a[l])
        h = _rglru(x_rnn, conv_w[l], conv_b[l], w_gate_a[l], b_gate_a[l],
                   w_gate_x[l], b_gate_x[l], lru_lambda[l])
        branch_b = jnp.einsum('btc,cd->btd', h * jax.nn.silu(rnn_gate), w_branch_b[l])
        merged = jax.nn.sigmoid(g_a) * branch_a + jax.nn.sigmoid(g_b) * branch_b
        sub = jnp.einsum('btd,de->bte', merged, w_out[l])
        x = _layernorm(ALPHA * x + sub, ln_g[l], ln_b[l])
    return x
```

```python
import math
from contextlib import ExitStack

import numpy as np
import concourse.bass as bass
import concourse.mybir as mybir
from concourse.bass_utils import run_bass_kernel_spmd

F32 = mybir.dt.float32
BF16 = mybir.dt.bfloat16
U8 = mybir.dt.uint8
AF = mybir.ActivationFunctionType
ALU = mybir.AluOpType
AX = mybir.AxisListType

D = 2048
T = 8192
NT = 16
NP = 8
TOPK = 256
NIT = 24
ALPHA = 2.0 ** 0.25
EPS = 1e-5
NEG = -1.0e30
QL, AG, QI, XR, RG, GA, GB, WA, WB = 0, 32, 48, 56, 72, 88, 104, 120, 136
NCH = 152
COL = {QL: 0, AG: 4352, QI: 6400, XR: 7504, RG: 9552, GA: 11600, GB: 13648}


class _Op:
    __slots__ = ("eng", "fn", "waits", "idx", "inc", "semval", "dma_sem", "dma_val")

    def __init__(self, eng, fn):
        self.eng = eng
        self.fn = fn
        self.waits = []
        self.idx = -1
        self.inc = False
        self.semval = 0
        self.dma_sem = None
        self.dma_val = 0


class Sched:
    ENG = ("pe", "act", "dve", "pool", "sp")
    SAME = ("act", "dve", "pool")

    def __init__(self, nc, stack):
        self.nc = nc
        self.stack = stack
        self.ops = {e: [] for e in self.ENG}
        self.last_w = {}
        self.readers = {}
        self.dma_sems = {}
        self.final_tokens = []
        self.reg_mode = {}
        self.reg_cur = {}
        self.reg_fence = {}

    def _dma_sem(self, name):
        if name not in self.dma_sems:
            h = self.stack.enter_context(self.nc.semaphore("d_" + name))
            self.dma_sems[name] = [h, 0]
        return self.dma_sems[name]

    @staticmethod
    def _compress(toks):
        be = {}
        bd = {}
        for t in toks:
            if t[0] == "e":
                p = t[1]
                if p.eng not in be or be[p.eng].idx < p.idx:
                    be[p.eng] = p
            else:
                if bd.get(t[1], 0) < t[2]:
                    bd[t[1]] = t[2]
        return [("e", p) for p in be.values()] + [("d", k, v) for k, v in bd.items()]

    def _deps(self, reads, writes, region):
        toks = []
        for k in reads:
            t = self.last_w.get(k)
            if t is not None:
                toks.append(t)
        for k in writes:
            t = self.last_w.get(k)
            if t is not None:
                toks.append(t)
            toks.extend(self.readers.get(k, ()))
        if region is not None:
            r, mode = region
            if self.reg_mode.get(r) != mode:
                self.reg_fence[r] = self._compress(self.reg_cur.get(r, []) + self.reg_fence.get(r, []))
                self.reg_cur[r] = []
                self.reg_mode[r] = mode
            toks.extend(self.reg_fence.get(r, ()))
        return self._compress(toks)

    def _commit(self, tok, reads, writes, region):
        for k in reads:
            lst = self.readers.setdefault(k, [])
            lst.append(tok)
            if len(lst) > 12:
                self.readers[k] = self._compress(lst)
        for k in writes:
            self.last_w[k] = tok
            self.readers[k] = []
        if region is not None:
            lst = self.reg_cur.setdefault(region[0], [])
            lst.append(tok)
            if len(lst) > 12:
                self.reg_cur[region[0]] = self._compress(lst)

    @staticmethod
    def _excl(reads, writes):
        r = [k for k in reads if not (isinstance(k, tuple) and k[0] == "ps")]
        w = list(writes) + [k for k in reads if isinstance(k, tuple) and k[0] == "ps" and k not in writes]
        return r, w

    def op(self, eng, fn, reads=(), writes=(), region=None):
        reads, writes = self._excl(reads, writes)
        o = _Op(eng, fn)
        o.idx = len(self.ops[eng])
        o.waits = self._deps(reads, writes, region)
        self.ops[eng].append(o)
        self._commit(("e", o), reads, writes, region)
        return o

    def dma(self, eng, fn, sem, reads=(), writes=(), region=None, final=False):
        o = _Op(eng, fn)
        o.idx = len(self.ops[eng])
        o.waits = self._deps(reads, writes, region)
        s = self._dma_sem(sem)
        s[1] += 16
        o.dma_sem = sem
        o.dma_val = s[1]
        self.ops[eng].append(o)
        tok = ("d", sem, s[1])
        self._commit(tok, reads, writes, region)
        if final:
            self.final_tokens.append(tok)
        return o

    def dma_group(self, eng, fns, sem, keys):
        s = self._dma_sem(sem)
        for fn in fns:
            o = _Op(eng, fn)
            o.idx = len(self.ops[eng])
            o.waits = []
            s[1] += 16
            o.dma_sem = sem
            o.dma_val = s[1]
            self.ops[eng].append(o)
        tok = ("d", sem, s[1])
        for k in keys:
            self.last_w[k] = tok
            self.readers[k] = []

    def emit(self, final_eng="sp"):
        nc = self.nc
        fo = _Op(final_eng, None)
        fo.idx = len(self.ops[final_eng])
        fo.waits = list(self.final_tokens) + [("d", k, v[1]) for k, v in self.dma_sems.items()]
        for e in self.ENG:
            if e != final_eng and self.ops[e]:
                fo.waits.append(("e", self.ops[e][-1]))
        self.ops[final_eng].append(fo)
        for e in self.ENG:
            for o in self.ops[e]:
                for t in o.waits:
                    if t[0] == "e":
                        p = t[1]
                        if p.eng != o.eng or o.eng in self.SAME:
                            p.inc = True
        esem = {}
        for e in self.ENG:
            esem[e] = self.stack.enter_context(nc.semaphore("e_" + e))
            c = 0
            for o in self.ops[e]:
                if o.inc:
                    c += 1
                    o.semval = c
        block = self.stack.enter_context(nc.Block())

        def run(e, eng):
            seen_e = {x: -1 for x in self.ENG}
            seen_d = {}
            for o in self.ops[e]:
                for t in o.waits:
                    if t[0] == "e":
                        p = t[1]
                        if p.eng == e and e not in self.SAME:
                            continue
                        if p.idx > seen_e[p.eng]:
                            eng.wait_ge(esem[p.eng], p.semval)
                            seen_e[p.eng] = p.idx
                    else:
                        if seen_d.get(t[1], 0) < t[2]:
                            eng.wait_ge(self.dma_sems[t[1]][0], t[2])
                            seen_d[t[1]] = t[2]
                if o.fn is None:
                    continue
                ins = o.fn(eng)
                if o.dma_sem is not None:
                    ins.then_inc(self.dma_sems[o.dma_sem][0], 16)
                elif o.inc:
                    ins.then_inc(esem[e], 1)

        @block.tensor
        def _(eng):
            run("pe", eng)

        @block.scalar
        def _(eng):
            run("act", eng)

        @block.vector
        def _(eng):
            run("dve", eng)

        @block.gpsimd
        def _(eng):
            run("pool", eng)

        @block.sync
        def _(eng):
            run("sp", eng)


class _Stop(Exception):
    pass


def build_nc(n_pairs=NP, dbg=False, stop=None):
    def stage(name):
        if stop is not None and name == stop:
            raise _Stop()

    nc = bass.Bass("TRN2", target_bir_lowering=False)

    def din(name, shape, dt=F32):
        return nc.dram_tensor(name, list(shape), dt, kind="ExternalInput").ap()

    xT_d = din("xT", [NT, 128, 16 * 512])
    xTo_d = din("xTo", [NP, 128, 16 * 512])
    xtok_d = din("xtok", [NP, 4, 128, D])
    wfm_d = din("wfm", [NCH, 128, 16 * 128])
    wtok_d = din("wtok", [4, 128, 4 * 400])
    wout_d = din("wout", [8, 128, 16 * 256])
    wuv_d = din("wuv", [128, 32 * 128])
    wg_d = din("wg", [128, 32 * 128])
    cvec_d = din("cvec", [128, 8 * 16])
    kvg_d = din("kvg", [128, 256])
    lng_d = din("lng", [128, D])
    lnb_d = din("lnb", [128, D])
    biasS_d = din("biasS", [128, 4 * 16 * 128])
    b31_d = din("b31", [128, 16 * 128])
    sel_d = din("sel", [128, 2])
    cbias_d = din("cbias", [4, 128, 1024])
    pow2_d = din("pow2", [128, NIT])
    y_d = nc.dram_tensor("y", [NP, 4, 128, D], F32, kind="ExternalOutput").ap()
    ckv_d = nc.dram_tensor("ckv_s", [64, 128, 256], BF16, kind="Internal").ap()
    ckvT_d = nc.dram_tensor("ckvT_s", [128, 2, T], BF16, kind="Internal").ap()
    kixT_d = nc.dram_tensor("kixT_s", [128, T], BF16, kind="Internal").ap()
    wfm_b = nc.dram_tensor("wfm_bf", [NCH, 128, 16 * 128], BF16, kind="Internal").ap()
    wout_b = nc.dram_tensor("wout_bf", [8, 128, 16 * 256], BF16, kind="Internal").ap()
    dbg_out = {}
    if dbg:
        for nm, shp in (("d_ckv", [128, 256]), ("d_sc", [128, 1024]), ("d_thr", [128, 4]),
                        ("d_hg", [128, 16 * 512]), ("d_mrg", [128, 16 * 512]), ("d_ga", [128, 16 * 512])):
            dbg_out[nm] = nc.dram_tensor(nm, shp, F32, kind="ExternalOutput").ap()

    st = ExitStack()
    with st:
        def sb(name, shape, dt):
            return st.enter_context(nc.sbuf_tensor("s_" + name, list(shape), dt))

        S = Sched(nc, st)
        biasS = sb("biasS", [128, 4, 16, 128], BF16)
        cvec = sb("cvec", [128, 8, 16], F32)
        clam = sb("clam", [128, 16], F32)
        hclam = sb("hclam", [128, 16], F32)
        hba = sb("hba", [128, 16], F32)
        hbx = sb("hbx", [128, 16], F32)
        kvg = sb("kvg", [128, 256], F32)
        selt = sb("selt", [128, 2], F32)
        pow2 = sb("pow2", [128, NIT], F32)
        mhalf = sb("mhalf", [128, 1], F32)
        ident = sb("ident", [128, 128], BF16)
        ones = sb("ones", [128, 128], BF16)
        tail = sb("tail", [128, 16, 3], F32)
        hcar = sb("hcar", [128, 16], F32)
        xT = sb("xT", [128, 16, 512], BF16)
        NW = 4
        wst = [sb("wst%d" % i, [128, 16 * 128], BF16) for i in range(NW)]
        wgb = [sb("wgb%d" % i, [128, 2, 128], BF16) for i in range(2)]
        wuvb = [sb("wuvb%d" % i, [128, 8, 128], BF16) for i in range(2)]
        merged = sb("merged", [128, 16, 512], BF16)
        hg = sb("hg", [128, 16, 512], BF16)
        maskT = sb("maskT", [128, 64, 128], U8)
        qiT = sb("qiT", [128, 8, 256], BF16)
        REG = sb("REG", [128, 16384], F32)
        kix = sb("kix", [128, 512], BF16)
        ckvc = sb("ckvc", [128, 4, 256], BF16)
        ckvTc = sb("ckvTc", [128, 2, 512], BF16)
        Pb = [sb("Pb%d" % i, [128, 512], BF16) for i in range(2)]
        Pm = [sb("Pm%d" % i, [128, 4, 128], BF16) for i in range(2)]
        onT = sb("onT", [128, 2, 512], BF16)
        rden = sb("rden", [128, 512], F32)
        itmp = [sb("itmp%d" % i, [128, 512], F32) for i in range(2)]
        mk = sb("mk", [128, 512], BF16)
        ckv_tok = [sb("ckvtok%d" % i, [128, 256], BF16) for i in range(2)]
        kix_tok = [sb("kixtok%d" % i, [128, 128], BF16) for i in range(2)]
        ckvT_st = sb("ckvT_st", [128, 2, 512], BF16)
        kixT_st = sb("kixT_st", [128, 512], BF16)
        absw = sb("absw", [128, 4, 16], F32)
        sgnw = sb("sgnw", [128, 4, 16], F32)
        sm = sb("sm", [128, 64], F32)
        ft = [sb("ft%d" % i, [128, 512], F32) for i in range(3)]
        cbt = sb("cbt", [128, 1024], F32)
        bst = sb("bst", [128, 4, 6], F32)
        ps = st.enter_context(nc.psum_tensor("ps", [128, 8, 512], F32))

        scores = REG[:, 0:8192]
        qT = REG[:, 8192:12288].bitcast(BF16)
        ysub = REG[:, 0:8192]
        wo = [REG[:, 8192 + i * 2048: 8192 + (i + 1) * 2048].bitcast(BF16) for i in range(2)]
        lng = REG[:, 12288:14336]
        lnb = REG[:, 14336:16384]

        def rv(i):
            return REG[:, i * 1024:(i + 1) * 1024]

        gpc = [0]

        def gp():
            b = 5 + gpc[0] % 3
            gpc[0] += 1
            return b

        wc = [0]

        def wslot():
            s = wc[0] % NW
            wc[0] += 1
            return s

        def psb(bank):
            return ps[:, bank, :]

        S.dma("sp", lambda e: e.dma_start(out=cvec[:].rearrange("p a b -> p (a b)"), in_=cvec_d), "c0", writes=["cvec"])
        S.dma("sp", lambda e: e.dma_start(out=kvg[:], in_=kvg_d), "c1", writes=["kvg"])
        S.dma("sp", lambda e: e.dma_start(out=selt[:], in_=sel_d), "c2", writes=["selt"])
        S.dma("sp", lambda e: e.dma_start(out=pow2[:], in_=pow2_d), "c3", writes=["pow2"])
        S.dma("sp", lambda e: e.dma_start(out=REG[:, 0:8192], in_=biasS_d), "c4", writes=["ibias"], region=("R", "init"))
        S.dma("sp", lambda e: e.dma_start(out=REG[:, 8192:10240], in_=b31_d), "c5", writes=["ib31"], region=("R", "init"))
        for s_ in range(4):
            S.op("dve", lambda e, s_=s_: e.tensor_tensor(out=REG[:, s_ * 2048:(s_ + 1) * 2048], in0=REG[:, s_ * 2048:(s_ + 1) * 2048],
                                                          in1=REG[:, 8192:10240], op=ALU.subtract),
                 reads=["ib31", "ibias"], writes=[("ibs", s_)], region=("R", "init"))
            S.op("dve", lambda e, s_=s_: e.tensor_scalar(out=biasS[:, s_, :, :].rearrange("p a b -> p (a b)"),
                                                          in0=REG[:, s_ * 2048:(s_ + 1) * 2048], scalar1=16.0, scalar2=None, op0=ALU.mult),
                 reads=[("ibs", s_)], writes=["biasS"], region=("R", "init"))
        S.op("pool", lambda e: e.memset(ones[:], 1.0), writes=["ones"])
        S.op("pool", lambda e: e.memset(mhalf[:], -0.5), writes=["mhalf"])
        S.op("pool", lambda e: e.memset(tail[:], 0.0), writes=["tail"])
        S.op("pool", lambda e: e.memset(hcar[:], 0.0), writes=["hcar"])
        S.op("pool", lambda e: e.affine_select(out=ident[:], in_=ones[:], pattern=[[-1, 128]], compare_op=ALU.is_equal,
                                                fill=0.0, base=0, channel_multiplier=1),
             reads=["ones"], writes=["ident"])
        S.op("act", lambda e: e.activation(out=clam[:], in_=cvec[:, 7, :], func=AF.Exp, scale=-1.0), reads=["cvec"], writes=["clam"])
        S.op("act", lambda e: e.activation(out=clam[:], in_=clam[:], func=AF.Ln, bias=1.0, scale=1.0), reads=["clam"], writes=["clam"])
        S.op("dve", lambda e: e.tensor_scalar(out=clam[:], in0=clam[:], scalar1=-8.0, scalar2=None, op0=ALU.mult), reads=["clam"], writes=["clam"])
        S.op("dve", lambda e: e.tensor_scalar(out=hclam[:], in0=clam[:], scalar1=0.5, scalar2=None, op0=ALU.mult), reads=["clam"], writes=["hclam"])
        S.op("dve", lambda e: e.tensor_scalar(out=hba[:], in0=cvec[:, 5, :], scalar1=0.5, scalar2=None, op0=ALU.mult), reads=["cvec"], writes=["hba"])
        S.op("dve", lambda e: e.tensor_scalar(out=hbx[:], in0=cvec[:, 6, :], scalar1=0.5, scalar2=None, op0=ALU.mult), reads=["cvec"], writes=["hbx"])

        for gname, base, n in (("XR", XR, 16), ("RG", RG, 16), ("WB", WB, 16), ("GB", GB, 16), ("QI", QI, 8), ("QL", QL, 32),
                               ("AG", AG, 16), ("WA", WA, 16), ("GA", GA, 16)):
            S.dma_group("pool", [(lambda e, c=c: e.dma_start(max_dma_last_dim=4096, out=wfm_b[c], in_=wfm_d[c])) for c in range(base, base + n)],
                        "cv" + gname, [("wb", c) for c in range(base, base + n)])
        S.dma_group("pool", [(lambda e, c=c: e.dma_start(max_dma_last_dim=4096, out=wout_b[c], in_=wout_d[c])) for c in range(8)],
                    "cvWO", [("wob", c) for c in range(8)])

        def load_w(chunk):
            s = wslot()
            S.dma("pool", lambda e: e.dma_start(out=wst[s][:], in_=wfm_b[chunk]), "w%d" % s, reads=[("wb", chunk)], writes=[("wst", s)])
            return s

        def proj_fm(chunk, rhs_fn, rhs_keys, ncol, bank=None):
            s = load_w(chunk)
            if bank is None:
                bank = gp()

            def f(e):
                ins = None
                for k in range(16):
                    ins = e.matmul(ps[:, bank, 0:ncol], lhsT=wst[s][:, k * 128:(k + 1) * 128], rhs=rhs_fn(k),
                                   start=(k == 0), stop=(k == 15))
                return ins
            S.op("pe", f, reads=[("wst", s)] + list(rhs_keys), writes=[("ps", bank)])
            return bank

        def xT_rhs(k):
            return xT[:, k, :]

        def sig_half(bank, ncol, out_t, key, bias=None, rkeys=()):
            if bias is None:
                S.op("act", lambda e: e.activation(out=out_t, in_=ps[:, bank, 0:ncol], func=AF.Tanh, scale=0.5),
                     reads=[("ps", bank)], writes=[key])
            else:
                S.op("act", lambda e: e.activation(out=out_t, in_=ps[:, bank, 0:ncol], func=AF.Tanh, bias=bias, scale=0.5),
                     reads=[("ps", bank)] + list(rkeys), writes=[key])

        def fullseq(i, half, m):
            RM = ("R", "rnn%d" % i)
            S.dma("pool", lambda e: e.dma_start(max_dma_last_dim=4096, out=xT[:].rearrange("p a b -> p (a b)"), in_=xT_d[i]), "xT", writes=["xT"])
            stage('s1')
            for piece in range(4):
                s = wslot()
                S.dma("pool", lambda e, s=s, piece=piece: e.dma_start(max_dma_last_dim=4096, out=wst[s][:, 0:1600], in_=wtok_d[piece]), "w%d" % s,
                      writes=[("wst", s)])
                for tb in range(4):
                    def f(e, s=s, piece=piece, tb=tb):
                        ins = None
                        for kk in range(4):
                            k = piece * 4 + kk
                            ins = e.matmul(ps[:, tb, 0:400], lhsT=xT[:, k, tb * 128:(tb + 1) * 128],
                                           rhs=wst[s][:, kk * 400:(kk + 1) * 400], start=(k == 0), stop=(k == 15))
                        return ins
                    S.op("pe", f, reads=[("wst", s), "xT"], writes=[("ps", tb)])
            stage('s2')
            for tb in range(4):
                pb = tb % 2
                ss = sm[:, tb:tb + 1]
                rs = sm[:, 4 + tb:5 + tb]
                S.op("act", lambda e, tb=tb, ss=ss: e.activation(out=ft[0][:, 0:256], in_=ps[:, tb, 0:256], func=AF.Square, accum_out=ss),
                     reads=[("ps", tb)], writes=[("ft", 0), ("sm", tb)])
                S.op("dve", lambda e, ss=ss: e.tensor_scalar(out=ss, in0=ss, scalar1=1.0 / 256.0, scalar2=EPS, op0=ALU.mult, op1=ALU.add),
                     reads=[("sm", tb)], writes=[("sm", tb)])
                stage('s3_%d' % tb)
                S.op("pool", lambda e, ss=ss, rs=rs: e.tensor_tensor(out=rs, in0=ss, in1=mhalf[:], op=ALU.pow),
                     reads=[("sm", tb), "mhalf"], writes=[("sm", 4 + tb)])
                S.op("dve", lambda e, tb=tb, rs=rs, pb=pb: e.scalar_tensor_tensor(out=ckv_tok[pb][:], in0=ps[:, tb, 0:256], scalar=rs,
                                                                                 in1=kvg[:], op0=ALU.mult, op1=ALU.mult),
                     reads=[("ps", tb), ("sm", 4 + tb), "kvg"], writes=[("ckvtok", pb)])
                S.op("act", lambda e, tb=tb, pb=pb: e.activation(out=kix_tok[pb][:], in_=ps[:, tb, 256:384], func=AF.Copy),
                     reads=[("ps", tb)], writes=[("kixtok", pb)])
                if half == 1 or True:
                    pass
                if dbg and i == 0 and tb == 0:
                    S.op("act", lambda e: e.activation(out=ft[1][:, 0:256], in_=ckv_tok[0][:], func=AF.Copy),
                         reads=[("ckvtok", 0)], writes=[("ft", 1)])
                    S.dma("sp", lambda e: e.dma_start(out=dbg_out["d_ckv"], in_=ft[1][:, 0:256]), "dbg0", reads=[("ft", 1)], final=True)
                stage('s4_%d' % tb)
                S.dma("sp", lambda e, tb=tb, pb=pb: e.dma_start(out=ckv_d[i * 4 + tb], in_=ckv_tok[pb][:]), "skv%d" % pb,
                      reads=[("ckvtok", pb)], writes=[("ckv_d", i)])
                stage('s5_%d' % tb)
                tbk = gp()
                pv = ps[:, tbk, :].bitcast(BF16)

                def ftr(e, pb=pb, pv=pv):
                    e.transpose(out=pv[:, 0:128], in_=ckv_tok[pb][:, 0:128], identity=ident[:])
                    e.transpose(out=pv[:, 128:256], in_=ckv_tok[pb][:, 128:256], identity=ident[:])
                    return e.transpose(out=pv[:, 256:384], in_=kix_tok[pb][:], identity=ident[:])
                S.op("pe", ftr, reads=[("ckvtok", pb), ("kixtok", pb), "ident"], writes=[("ps", tbk)])
                stage('s6_%d' % tb)
                S.op("act", lambda e, tb=tb, pv=pv: e.activation(out=ckvT_st[:, :, tb * 128:(tb + 1) * 128],
                                                               in_=pv[:, 0:256].rearrange("p (a b) -> p a b", a=2), func=AF.Copy),
                     reads=[("ps", tbk)], writes=["ckvT_st"])
                S.op("dve", lambda e, tb=tb, pv=pv: e.tensor_copy(out=kixT_st[:, tb * 128:(tb + 1) * 128], in_=pv[:, 256:384]),
                     reads=[("ps", tbk)], writes=["kixT_st"])
                stage('s7_%d' % tb)
            stage('s8')
            S.dma("sp", lambda e: e.dma_start(out=ckvT_d[:, :, i * 512:(i + 1) * 512], in_=ckvT_st[:]), "skvT",
                  reads=["ckvT_st"], writes=[("ckvT_d", i)])
            S.dma("sp", lambda e: e.dma_start(out=kixT_d[:, i * 512:(i + 1) * 512], in_=kixT_st[:]), "skix",
                  reads=["kixT_st"], writes=[("kixT_d", i)])

            stage('tokproj')
            for c in range(16):
                par = c % 2
                o = par * 8
                xr = rv(o + 0)[:, 0:515]
                xc = rv(o + 1)[:, 0:512]
                xcb = rv(o + 1)[:, 512:768].bitcast(BF16)
                thr_ = rv(o + 2)[:, 0:512]
                thi = rv(o + 2)[:, 512:1024]
                a_ = rv(o + 3)[:, 0:512]
                a2 = rv(o + 3)[:, 512:1024]
                b_ = rv(o + 4)[:, 0:512]
                hh = rv(o + 4)[:, 512:1024]
                kk_ = lambda n: ("rt", par, n)
                bank = proj_fm(XR + c, xT_rhs, ["xT"], 512)
                S.op("dve", lambda e, xr=xr, c=c: e.tensor_copy(out=xr[:, 0:3], in_=tail[:, c, :]), reads=["tail"], writes=[kk_("xr0")], region=RM)
                S.op("act", lambda e, xr=xr, bank=bank: e.activation(out=xr[:, 3:515], in_=ps[:, bank, :], func=AF.Copy),
                     reads=[("ps", bank)], writes=[kk_("xr")], region=RM)
                S.op("dve", lambda e, xr=xr, c=c: e.tensor_copy(out=tail[:, c, :], in_=xr[:, 512:515]), reads=[kk_("xr")], writes=["tail"], region=RM)
                S.op("dve", lambda e, xr=xr, xc=xc, c=c: e.tensor_scalar(out=xc, in0=xr[:, 0:512], scalar1=cvec[:, 0, c:c + 1],
                                                                         scalar2=cvec[:, 4, c:c + 1], op0=ALU.mult, op1=ALU.add),
                     reads=[kk_("xr"), kk_("xr0"), "cvec"], writes=[kk_("xc")], region=RM)
                for k in range(1, 4):
                    S.op("dve", lambda e, xr=xr, xc=xc, c=c, k=k: e.scalar_tensor_tensor(out=xc, in0=xr[:, k:k + 512], scalar=cvec[:, k, c:c + 1],
                                                                                      in1=xc, op0=ALU.mult, op1=ALU.add),
                         reads=[kk_("xr"), kk_("xr0"), kk_("xc")], writes=[kk_("xc")], region=RM)
                S.op("act", lambda e, xc=xc, xcb=xcb: e.activation(out=xcb, in_=xc, func=AF.Copy), reads=[kk_("xc")], writes=[kk_("xcb")], region=RM)
                gs = c % 2
                S.dma("pool", lambda e, gs=gs, c=c: e.dma_start(max_dma_last_dim=4096, out=wgb[gs][:], in_=wg_d.rearrange("p (g n e) -> p g n e", g=2, n=16)[:, :, c, :]),
                      "wg%d" % gs, writes=[("wgb", gs)])
                br = gp()
                S.op("pe", lambda e, gs=gs, br=br, xcb=xcb: e.matmul(ps[:, br, :], lhsT=wgb[gs][:, 0, :], rhs=xcb, start=True, stop=True),
                     reads=[("wgb", gs), kk_("xcb")], writes=[("ps", br)], region=RM)
                bi = gp()
                S.op("pe", lambda e, gs=gs, bi=bi, xcb=xcb: e.matmul(ps[:, bi, :], lhsT=wgb[gs][:, 1, :], rhs=xcb, start=True, stop=True),
                     reads=[("wgb", gs), kk_("xcb")], writes=[("ps", bi)], region=RM)
                S.op("act", lambda e, br=br, thr_=thr_, c=c: e.activation(out=thr_, in_=ps[:, br, :], func=AF.Tanh, bias=hba[:, c:c + 1], scale=0.5),
                     reads=[("ps", br), "hba"], writes=[kk_("thr")], region=RM)
                S.op("act", lambda e, bi=bi, thi=thi, c=c: e.activation(out=thi, in_=ps[:, bi, :], func=AF.Tanh, bias=hbx[:, c:c + 1], scale=0.5),
                     reads=[("ps", bi), "hbx"], writes=[kk_("thi")], region=RM)
                S.op("act", lambda e, thr_=thr_, a_=a_, c=c: e.activation(out=a_, in_=thr_, func=AF.Exp, bias=hclam[:, c:c + 1], scale=hclam[:, c:c + 1]),
                     reads=[kk_("thr"), "hclam"], writes=[kk_("a")], region=RM)
                S.op("act", lambda e, thr_=thr_, a2=a2, c=c: e.activation(out=a2, in_=thr_, func=AF.Exp, bias=clam[:, c:c + 1], scale=clam[:, c:c + 1]),
                     reads=[kk_("thr"), "clam"], writes=[kk_("a2")], region=RM)
                S.op("act", lambda e, a2=a2: e.activation(out=a2, in_=a2, func=AF.Sqrt, bias=1.0, scale=-1.0),
                     reads=[kk_("a2")], writes=[kk_("a2")], region=RM)
                S.op("dve", lambda e, thi=thi, xc=xc, b_=b_: e.scalar_tensor_tensor(out=b_, in0=thi, scalar=1.0, in1=xc, op0=ALU.add, op1=ALU.mult),
                     reads=[kk_("thi"), kk_("xc")], writes=[kk_("b")], region=RM)
                if i == 0:
                    S.op("pool", lambda e, a2=a2: e.memset(a2[:, 0:1], 1.0), reads=[kk_("a2")], writes=[kk_("a2")], region=RM)
                S.op("dve", lambda e, b_=b_, a2=a2: e.scalar_tensor_tensor(out=b_, in0=b_, scalar=0.5, in1=a2, op0=ALU.mult, op1=ALU.mult),
                     reads=[kk_("b"), kk_("a2")], writes=[kk_("b")], region=RM)
                S.op("dve", lambda e, hh=hh, a_=a_, b_=b_, c=c: e.tensor_tensor_scan(out=hh, data0=a_, data1=b_, initial=hcar[:, c:c + 1],
                                                                                  op0=ALU.mult, op1=ALU.add),
                     reads=[kk_("a"), kk_("b"), "hcar"], writes=[kk_("h")], region=RM)
                S.op("dve", lambda e, hh=hh, c=c: e.tensor_copy(out=hcar[:, c:c + 1], in_=hh[:, 511:512]), reads=[kk_("h")], writes=["hcar"], region=RM)
                if half == 0:
                    S.op("dve", lambda e, hh=hh, c=c: e.tensor_scalar(out=hg[:, c, :], in0=hh, scalar1=selt[:, 0:1], scalar2=None, op0=ALU.mult),
                         reads=[kk_("h"), "selt"], writes=[("hg", c)], region=RM)
                else:
                    S.op("dve", lambda e, hh=hh, c=c: e.scalar_tensor_tensor(out=hg[:, c, :], in0=hh, scalar=selt[:, 1:2], in1=hg[:, c, :],
                                                                          op0=ALU.mult, op1=ALU.add),
                         reads=[kk_("h"), "selt", ("hg", c)], writes=[("hg", c)], region=RM)

        def own(m):
            stage('rnn')
            RA = ("R", "att%d" % m)
            RO = ("R", "out%d" % m)
            S.dma("pool", lambda e: e.dma_start(max_dma_last_dim=4096, out=xT[:].rearrange("p a b -> p (a b)"), in_=xTo_d[m]), "xT", writes=["xT"])
            for c in range(16):
                bank = proj_fm(RG + c, xT_rhs, ["xT"], 512)
                t0 = ft[c % 2]
                S.op("act", lambda e, bank=bank, t0=t0: e.activation(out=t0[:], in_=ps[:, bank, :], func=AF.Tanh, scale=0.5),
                     reads=[("ps", bank)], writes=[("ft", c % 2)])
                S.op("dve", lambda e, bank=bank, t0=t0: e.scalar_tensor_tensor(out=t0[:], in0=t0[:], scalar=1.0, in1=ps[:, bank, :], op0=ALU.add, op1=ALU.mult),
                     reads=[("ps", bank), ("ft", c % 2)], writes=[("ft", c % 2)])
                S.op("dve", lambda e, t0=t0, c=c: e.scalar_tensor_tensor(out=hg[:, c, :], in0=t0[:], scalar=0.5, in1=hg[:, c, :], op0=ALU.mult, op1=ALU.mult),
                     reads=[("ft", c % 2), ("hg", c)], writes=[("hg", c)])
            if dbg and m == 0:
                for c in range(16):
                    S.op("act", lambda e, c=c: e.activation(out=ft[2][:], in_=hg[:, c, :], func=AF.Copy), reads=[("hg", c)], writes=[("ft", 2)])
                    S.dma("sp", lambda e, c=c: e.dma_start(out=dbg_out["d_hg"][:, c * 512:(c + 1) * 512], in_=ft[2][:]), "dbg1", reads=[("ft", 2)], final=True)
            hgk = [("hg", c) for c in range(16)]
            for jc in range(16):
                ba = proj_fm(WB + jc, lambda k: hg[:, k, :], hgk, 512)
                bb = proj_fm(GB + jc, xT_rhs, ["xT"], 512)
                t0 = ft[jc % 2]
                S.op("act", lambda e, bb=bb, t0=t0: e.activation(out=t0[:], in_=ps[:, bb, :], func=AF.Tanh, scale=0.5),
                     reads=[("ps", bb)], writes=[("ft", jc % 2)])
                S.op("dve", lambda e, ba=ba, t0=t0: e.scalar_tensor_tensor(out=t0[:], in0=t0[:], scalar=1.0, in1=ps[:, ba, :], op0=ALU.add, op1=ALU.mult),
                     reads=[("ps", ba), ("ft", jc % 2)], writes=[("ft", jc % 2)])
                S.op("act", lambda e, t0=t0, jc=jc: e.activation(out=merged[:, jc, :], in_=t0[:], func=AF.Copy, scale=0.5),
                     reads=[("ft", jc % 2)], writes=[("mrg", jc)])

            stage('ownb')
            nkb_pair = 8 * m
            for hf in range(2):
                tsl = slice(hf * 256, (hf + 1) * 256)
                for c in range(8):
                    bank = proj_fm(QI + c, lambda k, tsl=tsl: xT[:, k, tsl], ["xT"], 256)
                    S.op("act", lambda e, bank=bank, c=c: e.activation(out=qiT[:, c, :], in_=ps[:, bank, 0:256], func=AF.Copy),
                         reads=[("ps", bank)], writes=["qiT"])
                for c in range(32):
                    bank = proj_fm(QL + c, lambda k, tsl=tsl: xT[:, k, tsl], ["xT"], 256)
                    if c % 2 == 0:
                        S.op("act", lambda e, bank=bank, c=c: e.activation(out=qT[:, c * 256:(c + 1) * 256], in_=ps[:, bank, 0:256], func=AF.Copy),
                             reads=[("ps", bank)], writes=["qT"], region=RA)
                    else:
                        S.op("dve", lambda e, bank=bank, c=c: e.tensor_copy(out=qT[:, c * 256:(c + 1) * 256], in_=ps[:, bank, 0:256]),
                             reads=[("ps", bank)], writes=["qT"], region=RA)
                for q2 in range(2):
                    qb = hf * 2 + q2
                    s = wslot()
                    bw = gp()
                    for piece in range(4):
                        if piece > 0:
                            s = wslot()
                        S.dma("pool", lambda e, s=s, piece=piece: e.dma_start(max_dma_last_dim=4096, out=wst[s][:, 0:1600], in_=wtok_d[piece]), "w%d" % s, writes=[("wst", s)])

                        def f(e, s=s, piece=piece, qb=qb, bw=bw):
                            ins = None
                            for kk in range(4):
                                k = piece * 4 + kk
                                ins = e.matmul(ps[:, bw, 0:16], lhsT=xT[:, k, qb * 128:(qb + 1) * 128],
                                               rhs=wst[s][:, kk * 400 + 384:kk * 400 + 400], start=(k == 0), stop=(k == 15))
                            return ins
                        S.op("pe", f, reads=[("wst", s), "xT"], writes=[("ps", bw)])
                    S.op("act", lambda e, bw=bw, qb=qb: e.activation(out=absw[:, qb, :], in_=ps[:, bw, 0:16], func=AF.Abs),
                         reads=[("ps", bw)], writes=["absw"])
                    S.op("act", lambda e, bw=bw, qb=qb: e.activation(out=sgnw[:, qb, :], in_=ps[:, bw, 0:16], func=AF.Sign),
                         reads=[("ps", bw)], writes=["sgnw"])

                for q2 in range(2):
                    qb = hf * 2 + q2
                    nkb = nkb_pair + 5 + qb
                    nk = nkb * 128
                    nkc = (nkb + 3) // 4
                    qsl = slice(q2 * 128, (q2 + 1) * 128)
                    for kc in range(nkc):
                        w = min(512, nk - kc * 512)
                        S.dma("sp", lambda e, kc=kc, w=w: e.dma_start(out=kix[:, 0:w], in_=kixT_d[:, kc * 512:kc * 512 + w]), "kix",
                              reads=[("kixT_d", t_) for t_ in range(kc, min(kc + 1, 2 * m + 2))], writes=["kix"])
                        accb = gp()
                        for h in range(16):
                            c = h // 2
                            po = (h % 2) * 64
                            zb = gp()
                            if zb == accb:
                                zb = gp()
                            S.op("pe", lambda e, zb=zb, c=c, po=po, w=w, qsl=qsl: e.matmul(ps[:, zb, 0:w], lhsT=qiT[po:po + 64, c, qsl], rhs=kix[po:po + 64, 0:w],
                                                                                start=True, stop=True),
                                 reads=["qiT", "kix"], writes=[("ps", zb)])
                            it = itmp[h % 2]
                            S.op("act", lambda e, zb=zb, it=it, h=h, w=w, qb=qb: e.activation(out=it[:, 0:w], in_=ps[:, zb, 0:w], func=AF.Relu,
                                                                                          scale=absw[:, qb, h:h + 1]),
                                 reads=[("ps", zb), "absw"], writes=[("itmp", h % 2)])
                            if h == 0:
                                S.op("dve", lambda e, it=it, accb=accb, w=w, qb=qb: e.tensor_scalar(out=ps[:, accb, 0:w], in0=it[:, 0:w],
                                                                                                scalar1=sgnw[:, qb, 0:1], scalar2=None, op0=ALU.mult),
                                     reads=[("itmp", 0), "sgnw"], writes=[("ps", accb)])
                            else:
                                S.op("dve", lambda e, it=it, accb=accb, w=w, h=h, qb=qb: e.scalar_tensor_tensor(out=ps[:, accb, 0:w], in0=it[:, 0:w],
                                                                                                          scalar=sgnw[:, qb, h:h + 1], in1=ps[:, accb, 0:w],
                                                                                                          op0=ALU.mult, op1=ALU.add),
                                     reads=[("itmp", h % 2), "sgnw", ("ps", accb)], writes=[("ps", accb)])
                        S.op("act", lambda e, accb=accb, kc=kc, w=w: e.activation(out=scores[:, kc * 512:kc * 512 + w], in_=ps[:, accb, 0:w], func=AF.Copy),
                             reads=[("ps", accb)], writes=["scores"], region=RA)
                    stage('indexer')
                    am = sm[:, 8:9]
                    lo = sm[:, 9:10]
                    mid = sm[:, 10:11]
                    cnt = sm[:, 11:12]
                    dl = sm[:, 12:13]
                    wt = sm[:, 16:16 + NIT]
                    S.op("dve", lambda e, nk=nk, am=am: e.tensor_reduce(out=am, in_=scores[:, 0:nk], axis=AX.X, op=ALU.max, apply_absolute_value=True),
                         reads=["scores"], writes=["am"], region=RA)
                    S.op("dve", lambda e, am=am: e.tensor_scalar(out=am, in0=am, scalar1=1.0, scalar2=None, op0=ALU.add), reads=["am"], writes=["am"])
                    S.op("dve", lambda e, am=am, lo=lo: e.tensor_scalar(out=lo, in0=am, scalar1=-1.0, scalar2=None, op0=ALU.mult), reads=["am"], writes=["lo"])
                    S.op("dve", lambda e, am=am, wt=wt: e.tensor_scalar(out=wt, in0=pow2[:], scalar1=am, scalar2=None, op0=ALU.mult),
                         reads=["am", "pow2"], writes=["wt"])
                    S.dma("sp", lambda e, qb=qb: e.dma_start(out=cbt[:], in_=cbias_d[qb]), "cbt", writes=["cbt"])
                    cw = (5 + qb) * 128
                    S.op("dve", lambda e, cw=cw: e.tensor_tensor(out=scores[:, nkb_pair * 128:nkb_pair * 128 + cw],
                                                                in0=scores[:, nkb_pair * 128:nkb_pair * 128 + cw], in1=cbt[:, 0:cw], op=ALU.add),
                         reads=["scores", "cbt", "am"], writes=["scores"], region=RA)
                    junk = maskT[:].rearrange("p a b -> p (a b)")
                    for it_ in range(NIT):
                        S.op("dve", lambda e, it_=it_, lo=lo, mid=mid, wt=wt: e.tensor_tensor(out=mid, in0=lo, in1=wt[:, it_:it_ + 1], op=ALU.add),
                             reads=["lo", "wt"], writes=["mid"])
                        S.op("dve", lambda e, nk=nk, mid=mid, cnt=cnt: e.tensor_scalar(out=junk[:, 0:nk], in0=scores[:, 0:nk], scalar1=mid, scalar2=None,
                                                                                    op0=ALU.is_ge, op1=ALU.add, accum_out=cnt),
                             reads=["scores", "mid"], writes=["maskT", "cnt"], region=RA)
                        S.op("dve", lambda e, it_=it_, cnt=cnt, dl=dl, wt=wt: e.scalar_tensor_tensor(out=dl, in0=cnt, scalar=TOPK - 0.5, in1=wt[:, it_:it_ + 1],
                                                                                             op0=ALU.is_ge, op1=ALU.mult),
                             reads=["cnt", "wt"], writes=["dl"])
                        S.op("dve", lambda e, lo=lo, dl=dl: e.tensor_tensor(out=lo, in0=lo, in1=dl, op=ALU.add), reads=["lo", "dl"], writes=["lo"])
                    if dbg and m == 0:
                        S.dma("sp", lambda e, qb=qb, lo=lo: e.dma_start(out=dbg_out["d_thr"][:, qb:qb + 1], in_=lo, allow_slow_non_contiguous=True), "dbg2", reads=["lo"], final=True)
                        if qb == 3:
                            S.dma("sp", lambda e: e.dma_start(out=dbg_out["d_sc"], in_=scores[:, 0:1024]), "dbg3", reads=["scores"], region=RA, final=True)
                    stage('bisect')
                    for kc in range(nkc):
                        w = min(512, nk - kc * 512)
                        nb = w // 128
                        S.op("dve", lambda e, kc=kc, w=w, lo=lo: e.tensor_scalar(out=mk[:, 0:w], in0=scores[:, kc * 512:kc * 512 + w], scalar1=lo, scalar2=None,
                                                                              op0=ALU.is_ge),
                             reads=["scores", "lo"], writes=["mk"], region=RA)
                        tbk = gp()
                        pv = ps[:, tbk, :].bitcast(BF16)

                        def ftr(e, pv=pv, nb=nb):
                            ins = None
                            for j in range(nb):
                                ins = e.transpose(out=pv[:, j * 128:(j + 1) * 128], in_=mk[:, j * 128:(j + 1) * 128], identity=ident[:])
                            return ins
                        S.op("pe", ftr, reads=["mk", "ident"], writes=[("ps", tbk)])
                        S.op("act", lambda e, kc=kc, w=w, nb=nb, pv=pv: e.activation(out=maskT[:, kc * 4:kc * 4 + nb, :].rearrange("p a b -> p (a b)"),
                                                                                  in_=pv[:, 0:w], func=AF.Copy),
                             reads=[("ps", tbk)], writes=["maskT"])
                    stage('mask')
                    for hq in range(4):
                        ws_ = hq % 2
                        S.dma("pool", lambda e, ws_=ws_, hq=hq: e.dma_start(max_dma_last_dim=4096, out=wuvb[ws_][:].rearrange("p a b -> p (a b)"),
                                                                             in_=wuv_d[:, hq * 1024:(hq + 1) * 1024]), "wuv%d" % ws_, writes=[("wuvb", ws_)])
                        qv = qT.rearrange("p (c t) -> p c t", c=32)
                        for kc in range(nkc):
                            w = min(512, nk - kc * 512)
                            nb = w // 128
                            tl = [t_ for t_ in range(kc, min(kc + 1, 2 * m + 2))]
                            S.dma("sp", lambda e, kc=kc, nb=nb: e.dma_start(out=ckvc[:, 0:nb, :], in_=ckv_d[kc * 4:kc * 4 + nb].rearrange("b s d -> s b d")),
                                  "ckvc", reads=[("ckv_d", t_) for t_ in tl], writes=["ckvc"])
                            S.dma("sp", lambda e, kc=kc, w=w: e.dma_start(out=ckvTc[:, :, 0:w], in_=ckvT_d[:, :, kc * 512:kc * 512 + w]),
                                  "ckvTc", reads=[("ckvT_d", t_) for t_ in tl], writes=["ckvTc"])
                            for j in range(nb):
                                kb = kc * 4 + j
                                rel = kb - nkb_pair
                                slot = {qb - 1: 0, qb: 1, qb + 3: 2, qb + 4: 3}.get(rel)
                                qkb = 3 + (kb % 2)
                                pp = kb % 2

                                def fqk(e, j=j, qkb=qkb, slot=slot, hq=hq, qsl=qsl, qv=qv):
                                    ins = None
                                    for k in range(2):
                                        ins = e.matmul(ps[:, qkb, :], lhsT=ckvTc[:, k, j * 128:(j + 1) * 128],
                                                       rhs=qv[:, hq * 8 + k:hq * 8 + 8:2, qsl], start=(k == 0), stop=(k == 1 and slot is None))
                                    if slot is not None:
                                        ins = e.matmul(ps[:, qkb, :], lhsT=ident[:], rhs=biasS[:, slot, hq * 4:hq * 4 + 4, :], start=False, stop=True)
                                    return ins
                                S.op("pe", fqk, reads=["ckvTc", "qT", "ident", "biasS"], writes=[("ps", qkb)], region=RA)
                                S.op("act", lambda e, qkb=qkb, pp=pp: e.activation(out=Pb[pp][:], in_=ps[:, qkb, :], func=AF.Exp, scale=1.0 / 16.0),
                                     reads=[("ps", qkb)], writes=[("Pb", pp)])
                                S.op("pool", lambda e, pp=pp, kb=kb: e.tensor_tensor(out=Pm[pp][:], in0=Pb[pp][:].rearrange("p (a b) -> p a b", a=4),
                                                                                 in1=maskT[:, kb:kb + 1, :].to_broadcast([128, 4, 128]), op=ALU.mult),
                                     reads=[("Pb", pp), "maskT"], writes=[("Pm", pp)])

                                def fpv(e, j=j, pp=pp, kb=kb, nkb=nkb):
                                    rhs = Pm[pp][:].rearrange("p a b -> p (a b)")
                                    e.matmul(ps[:, 0, :], lhsT=ckvc[:, j, 0:128], rhs=rhs, start=(kb == 0), stop=(kb == nkb - 1))
                                    e.matmul(ps[:, 1, :], lhsT=ckvc[:, j, 128:256], rhs=rhs, start=(kb == 0), stop=(kb == nkb - 1))
                                    return e.matmul(ps[:, 2, :], lhsT=ones[:], rhs=rhs, start=(kb == 0), stop=(kb == nkb - 1))
                                S.op("pe", fpv, reads=["ckvc", ("Pm", pp), "ones"], writes=[("ps", 0), ("ps", 1), ("ps", 2)])
                        S.op("dve", lambda e: e.reciprocal(out=rden[:], in_=ps[:, 2, :]), reads=[("ps", 2)], writes=["rden"])
                        for k in range(2):
                            S.op("dve", lambda e, k=k: e.tensor_tensor(out=onT[:, k, :], in0=ps[:, k, :], in1=rden[:], op=ALU.mult),
                                 reads=[("ps", k), "rden"], writes=["onT"])
                        ub = gp()

                        def fuv(e, ub=ub, ws_=ws_):
                            ins = None
                            for hh_ in range(4):
                                for k in range(2):
                                    ins = e.matmul(ps[:, ub, hh_ * 128:(hh_ + 1) * 128], lhsT=wuvb[ws_][:, hh_ * 2 + k, :],
                                                   rhs=onT[:, k, hh_ * 128:(hh_ + 1) * 128], start=(k == 0), stop=(k == 1))
                            return ins
                        S.op("pe", fuv, reads=["onT", ("wuvb", ws_)], writes=[("ps", ub)])
                        S.op("act", lambda e, ub=ub, hq=hq, qb=qb: e.activation(out=hg[:, hq * 4:hq * 4 + 4, qb * 128:(qb + 1) * 128],
                                                                              in_=ps[:, ub, :].rearrange("p (a b) -> p a b", a=4), func=AF.Copy),
                             reads=[("ps", ub)], writes=[("hg", hq * 4 + x) for x in range(4)])
            stage('attn')
            for c in range(16):
                bank = proj_fm(AG + c, xT_rhs, ["xT"], 512)
                t0 = ft[c % 2]
                S.op("act", lambda e, bank=bank, t0=t0: e.activation(out=t0[:], in_=ps[:, bank, :], func=AF.Tanh, scale=0.5),
                     reads=[("ps", bank)], writes=[("ft", c % 2)])
                S.op("dve", lambda e, bank=bank, t0=t0: e.scalar_tensor_tensor(out=t0[:], in0=t0[:], scalar=1.0, in1=ps[:, bank, :], op0=ALU.add, op1=ALU.mult),
                     reads=[("ps", bank), ("ft", c % 2)], writes=[("ft", c % 2)])
                S.op("dve", lambda e, t0=t0, c=c: e.scalar_tensor_tensor(out=hg[:, c, :], in0=t0[:], scalar=0.5, in1=hg[:, c, :], op0=ALU.mult, op1=ALU.mult),
                     reads=[("ft", c % 2), ("hg", c)], writes=[("hg", c)])
            if dbg and m == 0:
                for c in range(16):
                    S.op("act", lambda e, c=c: e.activation(out=ft[2][:], in_=hg[:, c, :], func=AF.Copy), reads=[("hg", c)], writes=[("ft", 2)])
                    S.dma("sp", lambda e, c=c: e.dma_start(out=dbg_out["d_ga"][:, c * 512:(c + 1) * 512], in_=ft[2][:]), "dbg4", reads=[("ft", 2)], final=True)
            stage('gate')
            for jc in range(16):
                ba = proj_fm(WA + jc, lambda k: hg[:, k, :], hgk, 512)
                bb = proj_fm(GA + jc, xT_rhs, ["xT"], 512)
                t0 = ft[jc % 2]
                S.op("act", lambda e, bb=bb, t0=t0: e.activation(out=t0[:], in_=ps[:, bb, :], func=AF.Tanh, scale=0.5),
                     reads=[("ps", bb)], writes=[("ft", jc % 2)])
                S.op("dve", lambda e, ba=ba, t0=t0: e.scalar_tensor_tensor(out=t0[:], in0=t0[:], scalar=1.0, in1=ps[:, ba, :], op0=ALU.add, op1=ALU.mult),
                     reads=[("ps", ba), ("ft", jc % 2)], writes=[("ft", jc % 2)])
                S.op("dve", lambda e, t0=t0, jc=jc: e.scalar_tensor_tensor(out=merged[:, jc, :], in0=t0[:], scalar=0.5, in1=merged[:, jc, :], op0=ALU.mult, op1=ALU.add),
                     reads=[("ft", jc % 2), ("mrg", jc)], writes=[("mrg", jc)])
            if dbg and m == 0:
                for c in range(16):
                    S.op("act", lambda e, c=c: e.activation(out=ft[2][:], in_=merged[:, c, :], func=AF.Copy), reads=[("mrg", c)], writes=[("ft", 2)])
                    S.dma("sp", lambda e, c=c: e.dma_start(out=dbg_out["d_mrg"][:, c * 512:(c + 1) * 512], in_=ft[2][:]), "dbg5", reads=[("ft", 2)], final=True)
            stage('brancha')
            mk_ = [("mrg", c) for c in range(16)]
            S.dma("sp", lambda e: e.dma_start(out=lng, in_=lng_d), "lng", writes=["lng"], region=RO)
            S.dma("sp", lambda e: e.dma_start(out=lnb, in_=lnb_d), "lnb", writes=["lnb"], region=RO)
            for tb in range(4):
                S.dma("sp", lambda e, tb=tb: e.dma_start(out=ysub[:, tb * 2048:(tb + 1) * 2048], in_=xtok_d[m, tb]), "xtk%d" % tb,
                      writes=[("ysub", tb)], region=RO)
            for jj in range(8):
                wsl = jj % 2
                S.dma("pool", lambda e, wsl=wsl, jj=jj: e.dma_start(out=wo[wsl], in_=wout_b[jj]), "wo%d" % wsl, reads=[("wob", jj)], writes=[("wo", wsl)], region=RO)
                for tb in range(4):
                    ob = gp()

                    def fo(e, ob=ob, wsl=wsl, tb=tb):
                        ins = None
                        for k in range(16):
                            ins = e.matmul(ps[:, ob, 0:256], lhsT=merged[:, k, tb * 128:(tb + 1) * 128], rhs=wo[wsl][:, k * 256:(k + 1) * 256],
                                           start=(k == 0), stop=(k == 15))
                        return ins
                    S.op("pe", fo, reads=mk_ + [("wo", wsl)], writes=[("ps", ob)], region=RO)
                    ys = ysub[:, tb * 2048 + jj * 256: tb * 2048 + (jj + 1) * 256]
                    S.op("dve", lambda e, ob=ob, ys=ys: e.scalar_tensor_tensor(out=ys, in0=ys, scalar=ALPHA, in1=ps[:, ob, 0:256], op0=ALU.mult, op1=ALU.add),
                         reads=[("ps", ob), ("ysub", tb)], writes=[("ysub", tb)], region=RO)
            for tb in range(4):
                yv = ysub[:, tb * 2048:(tb + 1) * 2048]
                mv = sm[:, 40:42]
                rs = sm[:, 42:43]
                for q in range(4):
                    S.op("dve", lambda e, yv=yv, q=q: e.bn_stats(out=bst[:, q, :], in_=yv[:, q * 512:(q + 1) * 512]),
                         reads=[("ysub", tb)], writes=[("bst", q)], region=RO)
                S.op("dve", lambda e, mv=mv: e.bn_aggr(out=mv, in_=bst[:].rearrange("p a b -> p (a b)")),
                     reads=[("bst", q) for q in range(4)], writes=["mv"])
                S.op("dve", lambda e, mv=mv, rs=rs: e.tensor_scalar(out=rs, in0=mv[:, 1:2], scalar1=EPS, scalar2=None, op0=ALU.add), reads=["mv"], writes=["rs"])
                S.op("pool", lambda e, rs=rs: e.tensor_tensor(out=rs, in0=rs, in1=mhalf[:], op=ALU.pow), reads=["rs", "mhalf"], writes=["rs"])
                S.op("dve", lambda e, yv=yv, mv=mv, rs=rs: e.tensor_scalar(out=yv, in0=yv, scalar1=mv[:, 0:1], scalar2=rs, op0=ALU.subtract, op1=ALU.mult),
                     reads=[("ysub", tb), "mv", "rs"], writes=[("ysub", tb)], region=RO)
                S.op("dve", lambda e, yv=yv: e.tensor_tensor(out=yv, in0=yv, in1=lng, op=ALU.mult), reads=[("ysub", tb), "lng"], writes=[("ysub", tb)], region=RO)
                S.op("dve", lambda e, yv=yv: e.tensor_tensor(out=yv, in0=yv, in1=lnb, op=ALU.add), reads=[("ysub", tb), "lnb"], writes=[("ysub", tb)], region=RO)
                S.dma("sp", lambda e, tb=tb, yv=yv: e.dma_start(out=y_d[m, tb], in_=yv), "yo%d" % tb, reads=[("ysub", tb)], region=RO, final=True)

        try:
            stage('init')
            for m in range(n_pairs):
                fullseq(2 * m, 0, m)
                fullseq(2 * m + 1, 1, m)
                own(m)
        except _Stop:
            pass
        S.emit()
    return nc


def _t5_bucket_np(n):
    n = np.maximum(n, 0)
    nf = np.maximum(n, 1).astype(np.float32)
    large = 16 + (np.log(nf / np.float32(16)) / np.float32(math.log(128 / 16)) * np.float32(16)).astype(np.int32)
    large = np.minimum(large, 31)
    return np.where(n < 16, n, large)


def _fm(w):
    n = w.shape[1] // 128
    return np.ascontiguousarray(w.reshape(16, 128, n, 128).transpose(2, 1, 0, 3)).reshape(n, 128, 16 * 128)


def _vec(v):
    return np.ascontiguousarray(v.reshape(16, 128).T)


def make_inputs(x, w_in, kv_norm_g, w_uv, w_branch_a, conv_w, conv_b, w_gate_a, b_gate_a,
                w_gate_x, b_gate_x, lru_lambda, w_branch_b, rel_bias, w_out, ln_g, ln_b):
    w_in = w_in[0]
    chunks = []
    for base, n in ((QL, 32), (AG, 16), (QI, 8), (XR, 16), (RG, 16), (GA, 16), (GB, 16)):
        c0 = COL[base]
        chunks.append(_fm(w_in[:, c0:c0 + n * 128]))
    chunks.append(_fm(w_branch_a[0]))
    chunks.append(_fm(w_branch_b[0]))
    wfm = np.concatenate(chunks, axis=0)
    tokcols = np.concatenate([np.arange(4096, 4352), np.arange(7424, 7488), np.arange(7424, 7488), np.arange(7488, 7504)])
    wt = w_in[:, tokcols]
    wtok = np.ascontiguousarray(wt.reshape(4, 4, 128, 400).transpose(0, 2, 1, 3)).reshape(4, 128, 1600)
    wout = np.ascontiguousarray(w_out[0].reshape(16, 128, 8, 256).transpose(2, 1, 0, 3)).reshape(8, 128, 16 * 256)
    wuv = np.ascontiguousarray(w_uv[0].reshape(16, 2, 128, 128).transpose(2, 0, 1, 3)).reshape(128, 32 * 128)
    wg = np.ascontiguousarray(np.stack([w_gate_a[0], w_gate_x[0]], 0).transpose(2, 0, 1, 3)).reshape(128, 32 * 128)
    cvec = np.stack([_vec(conv_w[0, 0]), _vec(conv_w[0, 1]), _vec(conv_w[0, 2]), _vec(conv_w[0, 3]), _vec(conv_b[0]),
                     _vec(b_gate_a[0]), _vec(b_gate_x[0]), _vec(lru_lambda[0])], axis=1).reshape(128, 128)
    kvg = np.ascontiguousarray(np.broadcast_to(kv_norm_g[0][None, :], (128, 256)))
    lng = np.ascontiguousarray(np.broadcast_to(ln_g[0][None, :], (128, D)))
    lnb = np.ascontiguousarray(np.broadcast_to(ln_b[0][None, :], (128, D)))
    s_ = np.arange(128)[:, None]
    t_ = np.arange(128)[None, :]
    diag = rel_bias[_t5_bucket_np(t_ - s_)]
    prev = rel_bias[_t5_bucket_np(128 + t_ - s_)]
    diag = np.ascontiguousarray(diag.transpose(0, 2, 1))
    prev = np.ascontiguousarray(prev.transpose(0, 2, 1))
    b31 = np.ascontiguousarray(np.broadcast_to(rel_bias[31][None, :, None], (128, 16, 128)))
    pow2 = np.ascontiguousarray(np.broadcast_to((2.0 ** -np.arange(NIT, dtype=np.float64)).astype(np.float32)[None, :], (128, NIT)))
    shared = dict(wfm=wfm, wtok=wtok, wout=wout, wuv=wuv, wg=wg, cvec=np.ascontiguousarray(cvec), kvg=kvg, lng=lng, lnb=lnb,
                  b31=b31.reshape(128, 2048), pow2=pow2)
    in_maps = []
    for core in range(8):
        b, par = core // 2, core % 2
        xb = x[b]
        xT = np.ascontiguousarray(xb.reshape(NT, 512, 16, 128).transpose(0, 3, 2, 1)).reshape(NT, 128, 16 * 512)
        xTo = np.ascontiguousarray(xT[par::2])
        xtok = np.ascontiguousarray(xb.reshape(NP, 2, 4, 128, D)[:, par])
        sel = np.zeros((128, 2), np.float32)
        sel[:, par] = 1.0
        cb = np.zeros((4, 128, 1024), np.float32)
        for qb in range(4):
            tg = par * 512 + qb * 128 + np.arange(128)[:, None]
            sg = np.arange(1024)[None, :]
            cb[qb] = np.where(sg <= tg, 0.0, NEG)
        if par == 0:
            slots = [prev, diag, b31, b31]
        else:
            slots = [b31, b31, prev, diag]
        biasS = np.ascontiguousarray(np.stack(slots, axis=1)).reshape(128, 4 * 16 * 128).astype(np.float32)
        d = dict(shared)
        d.update(xT=xT, xTo=xTo, xtok=xtok, sel=sel, cbias=cb, biasS=biasS)
        in_maps.append(d)
    return in_maps


def kernel(**inputs):
    inputs = {k: np.asarray(v) for k, v in inputs.items()}
    in_maps = make_inputs(**inputs)
    nc = build_nc()
    res = run_bass_kernel_spmd(nc, in_maps, core_ids=list(range(8)))
    out = np.empty((4, T, D), np.float32)
    for core in range(8):
        b, par = core // 2, core % 2
        yv = res.results[core]["y"].reshape(NP, 512, D)
        out.reshape(4, NP, 2, 512, D)[b, :, par] = yv
    return out
```

```python
import math
from contextlib import ExitStack

import numpy as np
import concourse.bass as bass
import concourse.mybir as mybir
from concourse.bass_utils import run_bass_kernel_spmd

F32 = mybir.dt.float32
BF16 = mybir.dt.bfloat16
U8 = mybir.dt.uint8
AF = mybir.ActivationFunctionType
ALU = mybir.AluOpType
AX = mybir.AxisListType

D = 2048
T = 8192
NT = 16
NP = 8
TOPK = 256
NIT = 24
ALPHA = 2.0 ** 0.25
EPS = 1e-5
NEG = -1.0e30
QL, AG, QI, XR, RG, GA, GB, WA, WB = 0, 32, 48, 56, 72, 88, 104, 120, 136
NCH = 152
COL = {QL: 0, AG: 4352, QI: 6400, XR: 7504, RG: 9552, GA: 11600, GB: 13648}


class _Op:
    __slots__ = ("eng", "fn", "waits", "idx", "inc", "semval", "dma_sem", "dma_val")

    def __init__(self, eng, fn):
        self.eng = eng
        self.fn = fn
        self.waits = []
        self.idx = -1
        self.inc = False
        self.semval = 0
        self.dma_sem = None
        self.dma_val = 0


class Sched:
    ENG = ("pe", "act", "dve", "pool", "sp")
    SAME = ("act", "dve", "pool")

    def __init__(self, nc, stack):
        self.nc = nc
        self.stack = stack
        self.ops = {e: [] for e in self.ENG}
        self.last_w = {}
        self.readers = {}
        self.dma_sems = {}
        self.final_tokens = []
        self.reg_mode = {}
        self.reg_cur = {}
        self.reg_fence = {}

    def _dma_sem(self, name):
        if name not in self.dma_sems:
            h = self.stack.enter_context(self.nc.semaphore("d_" + name))
            self.dma_sems[name] = [h, 0]
        return self.dma_sems[name]

    @staticmethod
    def _compress(toks):
        be = {}
        bd = {}
        for t in toks:
            if t[0] == "e":
                p = t[1]
                if p.eng not in be or be[p.eng].idx < p.idx:
                    be[p.eng] = p
            else:
                if bd.get(t[1], 0) < t[2]:
                    bd[t[1]] = t[2]
        return [("e", p) for p in be.values()] + [("d", k, v) for k, v in bd.items()]

    def _deps(self, reads, writes, region):
        toks = []
        for k in reads:
            t = self.last_w.get(k)
            if t is not None:
                toks.append(t)
        for k in writes:
            t = self.last_w.get(k)
            if t is not None:
                toks.append(t)
            toks.extend(self.readers.get(k, ()))
        if region is not None:
            r, mode = region
            if self.reg_mode.get(r) != mode:
                self.reg_fence[r] = self._compress(self.reg_cur.get(r, []) + self.reg_fence.get(r, []))
                self.reg_cur[r] = []
                self.reg_mode[r] = mode
            toks.extend(self.reg_fence.get(r, ()))
        return self._compress(toks)

    def _commit(self, tok, reads, writes, region):
        for k in reads:
            lst = self.readers.setdefault(k, [])
            lst.append(tok)
            if len(lst) > 12:
                self.readers[k] = self._compress(lst)
        for k in writes:
            self.last_w[k] = tok
            self.readers[k] = []
        if region is not None:
            lst = self.reg_cur.setdefault(region[0], [])
            lst.append(tok)
            if len(lst) > 12:
                self.reg_cur[region[0]] = self._compress(lst)

    @staticmethod
    def _excl(reads, writes):
        r = [k for k in reads if not (isinstance(k, tuple) and k[0] == "ps")]
        w = list(writes) + [k for k in reads if isinstance(k, tuple) and k[0] == "ps" and k not in writes]
        return r, w

    def op(self, eng, fn, reads=(), writes=(), region=None):
        reads, writes = self._excl(reads, writes)
        o = _Op(eng, fn)
        o.idx = len(self.ops[eng])
        o.waits = self._deps(reads, writes, region)
        self.ops[eng].append(o)
        self._commit(("e", o), reads, writes, region)
        return o

    def dma(self, eng, fn, sem, reads=(), writes=(), region=None, final=False):
        o = _Op(eng, fn)
        o.idx = len(self.ops[eng])
        o.waits = self._deps(reads, writes, region)
        s = self._dma_sem(sem)
        s[1] += 16
        o.dma_sem = sem
        o.dma_val = s[1]
        self.ops[eng].append(o)
        tok = ("d", sem, s[1])
        self._commit(tok, reads, writes, region)
        if final:
            self.final_tokens.append(tok)
        return o

    def dma_group(self, eng, fns, sem, keys):
        s = self._dma_sem(sem)
        for fn in fns:
            o = _Op(eng, fn)
            o.idx = len(self.ops[eng])
            o.waits = []
            s[1] += 16
            o.dma_sem = sem
            o.dma_val = s[1]
            self.ops[eng].append(o)
        tok = ("d", sem, s[1])
        for k in keys:
            self.last_w[k] = tok
            self.readers[k] = []

    def emit(self, final_eng="sp"):
        nc = self.nc
        fo = _Op(final_eng, None)
        fo.idx = len(self.ops[final_eng])
        fo.waits = list(self.final_tokens) + [("d", k, v[1]) for k, v in self.dma_sems.items()]
        for e in self.ENG:
            if e != final_eng and self.ops[e]:
                fo.waits.append(("e", self.ops[e][-1]))
        self.ops[final_eng].append(fo)
        for e in self.ENG:
            for o in self.ops[e]:
                for t in o.waits:
                    if t[0] == "e":
                        p = t[1]
                        if p.eng != o.eng or o.eng in self.SAME:
                            p.inc = True
        esem = {}
        for e in self.ENG:
            esem[e] = self.stack.enter_context(nc.semaphore("e_" + e))
            c = 0
            for o in self.ops[e]:
                if o.inc:
                    c += 1
                    o.semval = c
        block = self.stack.enter_context(nc.Block())

        def run(e, eng):
            seen_e = {x: -1 for x in self.ENG}
            seen_d = {}
            for o in self.ops[e]:
                for t in o.waits:
                    if t[0] == "e":
                        p = t[1]
                        if p.eng == e and e not in self.SAME:
                            continue
                        if p.idx > seen_e[p.eng]:
                            eng.wait_ge(esem[p.eng], p.semval)
                            seen_e[p.eng] = p.idx
                    else:
                        if seen_d.get(t[1], 0) < t[2]:
                            eng.wait_ge(self.dma_sems[t[1]][0], t[2])
                            seen_d[t[1]] = t[2]
                if o.fn is None:
                    continue
                ins = o.fn(eng)
                if o.dma_sem is not None:
                    ins.then_inc(self.dma_sems[o.dma_sem][0], 16)
                elif o.inc:
                    ins.then_inc(esem[e], 1)

        @block.tensor
        def _(eng):
            run("pe", eng)

        @block.scalar
        def _(eng):
            run("act", eng)

        @block.vector
        def _(eng):
            run("dve", eng)

        @block.gpsimd
        def _(eng):
            run("pool", eng)

        @block.sync
        def _(eng):
            run("sp", eng)


class _Stop(Exception):
    pass


def build_nc(n_pairs=NP, dbg=False, stop=None):
    def stage(name):
        if stop is not None and name == stop:
            raise _Stop()

    nc = bass.Bass("TRN2", target_bir_lowering=False)

    def din(name, shape, dt=F32):
        return nc.dram_tensor(name, list(shape), dt, kind="ExternalInput").ap()

    xT_d = din("xT", [NT, 128, 16 * 512])
    xTo_d = din("xTo", [NP, 128, 16 * 512])
    xtok_d = din("xtok", [NP, 4, 128, D])
    wfm_d = din("wfm", [NCH, 128, 16 * 128])
    wtok_d = din("wtok", [4, 128, 4 * 400])
    wout_d = din("wout", [8, 128, 16 * 256])
    wuv_d = din("wuv", [128, 32 * 128])
    wg_d = din("wg", [128, 32 * 128])
    cvec_d = din("cvec", [128, 8 * 16])
    kvg_d = din("kvg", [128, 256])
    lng_d = din("lng", [128, D])
    lnb_d = din("lnb", [128, D])
    biasS_d = din("biasS", [128, 4 * 16 * 128])
    b31_d = din("b31", [128, 16 * 128])
    sel_d = din("sel", [128, 2])
    cbias_d = din("cbias", [4, 128, 1024])
    pow2_d = din("pow2", [128, NIT])
    y_d = nc.dram_tensor("y", [NP, 4, 128, D], F32, kind="ExternalOutput").ap()
    ckv_d = nc.dram_tensor("ckv_s", [64, 128, 256], BF16, kind="Internal").ap()
    ckvT_d = nc.dram_tensor("ckvT_s", [128, 2, T], BF16, kind="Internal").ap()
    kixT_d = nc.dram_tensor("kixT_s", [128, T], BF16, kind="Internal").ap()
    wfm_b = nc.dram_tensor("wfm_bf", [NCH, 128, 16 * 128], BF16, kind="Internal").ap()
    wout_b = nc.dram_tensor("wout_bf", [8, 128, 16 * 256], BF16, kind="Internal").ap()
    dbg_out = {}
    if dbg:
        for nm, shp in (("d_ckv", [128, 256]), ("d_sc", [128, 1024]), ("d_thr", [128, 4]),
                        ("d_hg", [128, 16 * 512]), ("d_mrg", [128, 16 * 512]), ("d_ga", [128, 16 * 512])):
            dbg_out[nm] = nc.dram_tensor(nm, shp, F32, kind="ExternalOutput").ap()

    st = ExitStack()
    with st:
        def sb(name, shape, dt):
            return st.enter_context(nc.sbuf_tensor("s_" + name, list(shape), dt))

        S = Sched(nc, st)
        biasS = sb("biasS", [128, 4, 16, 128], BF16)
        cvec = sb("cvec", [128, 8, 16], F32)
        clam = sb("clam", [128, 16], F32)
        hclam = sb("hclam", [128, 16], F32)
        hba = sb("hba", [128, 16], F32)
        hbx = sb("hbx", [128, 16], F32)
        kvg = sb("kvg", [128, 256], F32)
        selt = sb("selt", [128, 2], F32)
        pow2 = sb("pow2", [128, NIT], F32)
        mhalf = sb("mhalf", [128, 1], F32)
        ident = sb("ident", [128, 128], BF16)
        ones = sb("ones", [128, 128], BF16)
        tail = sb("tail", [128, 16, 3], F32)
        hcar = sb("hcar", [128, 16], F32)
        xT = sb("xT", [128, 16, 512], BF16)
        NW = 4
        wst = [sb("wst%d" % i, [128, 16 * 128], BF16) for i in range(NW)]
        wgb = [sb("wgb%d" % i, [128, 2, 128], BF16) for i in range(2)]
        wuvb = [sb("wuvb%d" % i, [128, 8, 128], BF16) for i in range(2)]
        merged = sb("merged", [128, 16, 512], BF16)
        hg = sb("hg", [128, 16, 512], BF16)
        maskT = sb("maskT", [128, 64, 128], U8)
        qiT = sb("qiT", [128, 8, 256], BF16)
        REG = sb("REG", [128, 16384], F32)
        kix = sb("kix", [128, 512], BF16)
        ckvc = [sb("ckvc%d" % i, [128, 4, 256], BF16) for i in range(2)]
        ckvTc = [sb("ckvTc%d" % i, [128, 2, 512], BF16) for i in range(2)]
        Pb = [sb("Pb%d" % i, [128, 512], BF16) for i in range(3)]
        Pm = [sb("Pm%d" % i, [128, 4, 128], BF16) for i in range(3)]
        onT = sb("onT", [128, 2, 512], BF16)
        rden = sb("rden", [128, 512], F32)
        itmp = [sb("itmp%d" % i, [128, 512], F32) for i in range(2)]
        mk = sb("mk", [128, 512], BF16)
        ckv_tok = [sb("ckvtok%d" % i, [128, 256], BF16) for i in range(2)]
        kix_tok = [sb("kixtok%d" % i, [128, 128], BF16) for i in range(2)]
        ckvT_st = sb("ckvT_st", [128, 2, 512], BF16)
        kixT_st = sb("kixT_st", [128, 512], BF16)
        absw = sb("absw", [128, 4, 16], F32)
        sgnw = sb("sgnw", [128, 4, 16], F32)
        sm = sb("sm", [128, 64], F32)
        ft = [sb("ft%d" % i, [128, 512], F32) for i in range(3)]
        cbt = sb("cbt", [128, 1024], F32)
        bst = sb("bst", [128, 4, 6], F32)
        ps = st.enter_context(nc.psum_tensor("ps", [128, 8, 512], F32))

        scores = REG[:, 0:8192]
        qT = REG[:, 8192:12288].bitcast(BF16)
        ysub = REG[:, 0:8192]
        wo = [REG[:, 8192 + i * 2048: 8192 + (i + 1) * 2048].bitcast(BF16) for i in range(2)]
        lng = REG[:, 12288:14336]
        lnb = REG[:, 14336:16384]

        def rv(i):
            return REG[:, i * 1024:(i + 1) * 1024]

        gpc = [0]

        def gp():
            b = 6 + gpc[0] % 2
            gpc[0] += 1
            return b

        gqc = [0]

        def gq():
            b = 3 + gqc[0] % 3
            gqc[0] += 1
            return b

        wc = [0]

        def wslot():
            s = wc[0] % NW
            wc[0] += 1
            return s

        def psb(bank):
            return ps[:, bank, :]

        S.dma("sp", lambda e: e.dma_start(out=cvec[:].rearrange("p a b -> p (a b)"), in_=cvec_d), "c0", writes=["cvec"])
        S.dma("sp", lambda e: e.dma_start(out=kvg[:], in_=kvg_d), "c1", writes=["kvg"])
        S.dma("sp", lambda e: e.dma_start(out=selt[:], in_=sel_d), "c2", writes=["selt"])
        S.dma("sp", lambda e: e.dma_start(out=pow2[:], in_=pow2_d), "c3", writes=["pow2"])
        S.dma("sp", lambda e: e.dma_start(out=REG[:, 0:8192], in_=biasS_d), "c4", writes=["ibias"], region=("R", "init"))
        S.dma("sp", lambda e: e.dma_start(out=REG[:, 8192:10240], in_=b31_d), "c5", writes=["ib31"], region=("R", "init"))
        for s_ in range(4):
            S.op("dve", lambda e, s_=s_: e.tensor_tensor(out=REG[:, s_ * 2048:(s_ + 1) * 2048], in0=REG[:, s_ * 2048:(s_ + 1) * 2048],
                                                          in1=REG[:, 8192:10240], op=ALU.subtract),
                 reads=["ib31", "ibias"], writes=[("ibs", s_)], region=("R", "init"))
            S.op("dve", lambda e, s_=s_: e.tensor_scalar(out=biasS[:, s_, :, :].rearrange("p a b -> p (a b)"),
                                                          in0=REG[:, s_ * 2048:(s_ + 1) * 2048], scalar1=16.0, scalar2=None, op0=ALU.mult),
                 reads=[("ibs", s_)], writes=["biasS"], region=("R", "init"))
        S.op("pool", lambda e: e.memset(ones[:], 1.0), writes=["ones"])
        S.op("pool", lambda e: e.memset(mhalf[:], -0.5), writes=["mhalf"])
        S.op("pool", lambda e: e.memset(tail[:], 0.0), writes=["tail"])
        S.op("pool", lambda e: e.memset(hcar[:], 0.0), writes=["hcar"])
        S.op("pool", lambda e: e.affine_select(out=ident[:], in_=ones[:], pattern=[[-1, 128]], compare_op=ALU.is_equal,
                                                fill=0.0, base=0, channel_multiplier=1),
             reads=["ones"], writes=["ident"])
        S.op("act", lambda e: e.activation(out=clam[:], in_=cvec[:, 7, :], func=AF.Exp, scale=-1.0), reads=["cvec"], writes=["clam"])
        S.op("act", lambda e: e.activation(out=clam[:], in_=clam[:], func=AF.Ln, bias=1.0, scale=1.0), reads=["clam"], writes=["clam"])
        S.op("dve", lambda e: e.tensor_scalar(out=clam[:], in0=clam[:], scalar1=-8.0, scalar2=None, op0=ALU.mult), reads=["clam"], writes=["clam"])
        S.op("dve", lambda e: e.tensor_scalar(out=hclam[:], in0=clam[:], scalar1=0.5, scalar2=None, op0=ALU.mult), reads=["clam"], writes=["hclam"])
        S.op("dve", lambda e: e.tensor_scalar(out=hba[:], in0=cvec[:, 5, :], scalar1=0.5, scalar2=None, op0=ALU.mult), reads=["cvec"], writes=["hba"])
        S.op("dve", lambda e: e.tensor_scalar(out=hbx[:], in0=cvec[:, 6, :], scalar1=0.5, scalar2=None, op0=ALU.mult), reads=["cvec"], writes=["hbx"])

        for gname, base, n in (("XR", XR, 16), ("RG", RG, 16), ("WB", WB, 16), ("GB", GB, 16), ("QI", QI, 8), ("QL", QL, 32),
                               ("AG", AG, 16), ("WA", WA, 16), ("GA", GA, 16)):
            S.dma_group("pool", [(lambda e, c=c: e.dma_start(max_dma_last_dim=4096, out=wfm_b[c], in_=wfm_d[c])) for c in range(base, base + n)],
                        "cv" + gname, [("wb", c) for c in range(base, base + n)])
        S.dma_group("pool", [(lambda e, c=c: e.dma_start(max_dma_last_dim=4096, out=wout_b[c], in_=wout_d[c])) for c in range(8)],
                    "cvWO", [("wob", c) for c in range(8)])

        def load_w(chunk):
            s = wslot()
            S.dma("pool", lambda e: e.dma_start(out=wst[s][:], in_=wfm_b[chunk]), "w%d" % s, reads=[("wb", chunk)], writes=[("wst", s)])
            return s

        def proj_fm(chunk, rhs_fn, rhs_keys, ncol, bank=None):
            s = load_w(chunk)
            if bank is None:
                bank = gp()

            def f(e):
                ins = None
                for k in range(16):
                    ins = e.matmul(ps[:, bank, 0:ncol], lhsT=wst[s][:, k * 128:(k + 1) * 128], rhs=rhs_fn(k),
                                   start=(k == 0), stop=(k == 15))
                return ins
            S.op("pe", f, reads=[("wst", s)] + list(rhs_keys), writes=[("ps", bank)])
            return bank

        def xT_rhs(k):
            return xT[:, k, :]

        def sig_half(bank, ncol, out_t, key, bias=None, rkeys=()):
            if bias is None:
                S.op("act", lambda e: e.activation(out=out_t, in_=ps[:, bank, 0:ncol], func=AF.Tanh, scale=0.5),
                     reads=[("ps", bank)], writes=[key])
            else:
                S.op("act", lambda e: e.activation(out=out_t, in_=ps[:, bank, 0:ncol], func=AF.Tanh, bias=bias, scale=0.5),
                     reads=[("ps", bank)] + list(rkeys), writes=[key])

        def fullseq(i, half, m):
            RM = ("R", "rnn%d" % i)
            S.dma("pool", lambda e: e.dma_start(max_dma_last_dim=4096, out=xT[:].rearrange("p a b -> p (a b)"), in_=xT_d[i]), "xT", writes=["xT"])
            stage('s1')
            for piece in range(4):
                s = wslot()
                S.dma("pool", lambda e, s=s, piece=piece: e.dma_start(max_dma_last_dim=4096, out=wst[s][:, 0:1600], in_=wtok_d[piece]), "w%d" % s,
                      writes=[("wst", s)])
                for tb in range(4):
                    def f(e, s=s, piece=piece, tb=tb):
                        ins = None
                        for kk in range(4):
                            k = piece * 4 + kk
                            ins = e.matmul(ps[:, tb, 0:400], lhsT=xT[:, k, tb * 128:(tb + 1) * 128],
                                           rhs=wst[s][:, kk * 400:(kk + 1) * 400], start=(k == 0), stop=(k == 15))
                        return ins
                    S.op("pe", f, reads=[("wst", s), "xT"], writes=[("ps", tb)])
            stage('s2')
            for tb in range(4):
                pb = tb % 2
                ss = sm[:, tb:tb + 1]
                rs = sm[:, 4 + tb:5 + tb]
                S.op("act", lambda e, tb=tb, ss=ss: e.activation(out=ft[0][:, 0:256], in_=ps[:, tb, 0:256], func=AF.Square, accum_out=ss),
                     reads=[("ps", tb)], writes=[("ft", 0), ("sm", tb)])
                S.op("dve", lambda e, ss=ss: e.tensor_scalar(out=ss, in0=ss, scalar1=1.0 / 256.0, scalar2=EPS, op0=ALU.mult, op1=ALU.add),
                     reads=[("sm", tb)], writes=[("sm", tb)])
                stage('s3_%d' % tb)
                S.op("pool", lambda e, ss=ss, rs=rs: e.tensor_tensor(out=rs, in0=ss, in1=mhalf[:], op=ALU.pow),
                     reads=[("sm", tb), "mhalf"], writes=[("sm", 4 + tb)])
                S.op("dve", lambda e, tb=tb, rs=rs, pb=pb: e.scalar_tensor_tensor(out=ckv_tok[pb][:], in0=ps[:, tb, 0:256], scalar=rs,
                                                                                 in1=kvg[:], op0=ALU.mult, op1=ALU.mult),
                     reads=[("ps", tb), ("sm", 4 + tb), "kvg"], writes=[("ckvtok", pb)])
                S.op("act", lambda e, tb=tb, pb=pb: e.activation(out=kix_tok[pb][:], in_=ps[:, tb, 256:384], func=AF.Copy),
                     reads=[("ps", tb)], writes=[("kixtok", pb)])
                if half == 1 or True:
                    pass
                if dbg and i == 0 and tb == 0:
                    S.op("act", lambda e: e.activation(out=ft[1][:, 0:256], in_=ckv_tok[0][:], func=AF.Copy),
                         reads=[("ckvtok", 0)], writes=[("ft", 1)])
                    S.dma("sp", lambda e: e.dma_start(out=dbg_out["d_ckv"], in_=ft[1][:, 0:256]), "dbg0", reads=[("ft", 1)], final=True)
                stage('s4_%d' % tb)
                S.dma("sp", lambda e, tb=tb, pb=pb: e.dma_start(out=ckv_d[i * 4 + tb], in_=ckv_tok[pb][:]), "skv%d" % pb,
                      reads=[("ckvtok", pb)], writes=[("ckv_d", i)])
                stage('s5_%d' % tb)
                tbk = gp()
                pv = ps[:, tbk, :].bitcast(BF16)

                def ftr(e, pb=pb, pv=pv):
                    e.transpose(out=pv[:, 0:128], in_=ckv_tok[pb][:, 0:128], identity=ident[:])
                    e.transpose(out=pv[:, 128:256], in_=ckv_tok[pb][:, 128:256], identity=ident[:])
                    return e.transpose(out=pv[:, 256:384], in_=kix_tok[pb][:], identity=ident[:])
                S.op("pe", ftr, reads=[("ckvtok", pb), ("kixtok", pb), "ident"], writes=[("ps", tbk)])
                stage('s6_%d' % tb)
                S.op("act", lambda e, tb=tb, pv=pv: e.activation(out=ckvT_st[:, :, tb * 128:(tb + 1) * 128],
                                                               in_=pv[:, 0:256].rearrange("p (a b) -> p a b", a=2), func=AF.Copy),
                     reads=[("ps", tbk)], writes=["ckvT_st"])
                S.op("dve", lambda e, tb=tb, pv=pv: e.tensor_copy(out=kixT_st[:, tb * 128:(tb + 1) * 128], in_=pv[:, 256:384]),
                     reads=[("ps", tbk)], writes=["kixT_st"])
                stage('s7_%d' % tb)
            stage('s8')
            S.dma("sp", lambda e: e.dma_start(out=ckvT_d[:, :, i * 512:(i + 1) * 512], in_=ckvT_st[:]), "skvT",
                  reads=["ckvT_st"], writes=[("ckvT_d", i)])
            S.dma("sp", lambda e: e.dma_start(out=kixT_d[:, i * 512:(i + 1) * 512], in_=kixT_st[:]), "skix",
                  reads=["kixT_st"], writes=[("kixT_d", i)])

            stage('tokproj')
            for c in range(16):
                par = c % 2
                o = par * 8
                xr = rv(o + 0)[:, 0:515]
                xc = rv(o + 1)[:, 0:512]
                xcb = rv(o + 1)[:, 512:768].bitcast(BF16)
                thr_ = rv(o + 2)[:, 0:512]
                thi = rv(o + 2)[:, 512:1024]
                a_ = rv(o + 3)[:, 0:512]
                a2 = rv(o + 3)[:, 512:1024]
                b_ = rv(o + 4)[:, 0:512]
                hh = rv(o + 4)[:, 512:1024]
                kk_ = lambda n: ("rt", par, n)
                bank = proj_fm(XR + c, xT_rhs, ["xT"], 512)
                S.op("dve", lambda e, xr=xr, c=c: e.tensor_copy(out=xr[:, 0:3], in_=tail[:, c, :]), reads=["tail"], writes=[kk_("xr0")], region=RM)
                S.op("act", lambda e, xr=xr, bank=bank: e.activation(out=xr[:, 3:515], in_=ps[:, bank, :], func=AF.Copy),
                     reads=[("ps", bank)], writes=[kk_("xr")], region=RM)
                S.op("dve", lambda e, xr=xr, c=c: e.tensor_copy(out=tail[:, c, :], in_=xr[:, 512:515]), reads=[kk_("xr")], writes=["tail"], region=RM)
                S.op("dve", lambda e, xr=xr, xc=xc, c=c: e.tensor_scalar(out=xc, in0=xr[:, 0:512], scalar1=cvec[:, 0, c:c + 1],
                                                                         scalar2=cvec[:, 4, c:c + 1], op0=ALU.mult, op1=ALU.add),
                     reads=[kk_("xr"), kk_("xr0"), "cvec"], writes=[kk_("xc")], region=RM)
                for k in range(1, 4):
                    S.op("dve", lambda e, xr=xr, xc=xc, c=c, k=k: e.scalar_tensor_tensor(out=xc, in0=xr[:, k:k + 512], scalar=cvec[:, k, c:c + 1],
                                                                                      in1=xc, op0=ALU.mult, op1=ALU.add),
                         reads=[kk_("xr"), kk_("xr0"), kk_("xc")], writes=[kk_("xc")], region=RM)
                S.op("act", lambda e, xc=xc, xcb=xcb: e.activation(out=xcb, in_=xc, func=AF.Copy), reads=[kk_("xc")], writes=[kk_("xcb")], region=RM)
                gs = c % 2
                S.dma("pool", lambda e, gs=gs, c=c: e.dma_start(max_dma_last_dim=4096, out=wgb[gs][:], in_=wg_d.rearrange("p (g n e) -> p g n e", g=2, n=16)[:, :, c, :]),
                      "wg%d" % gs, writes=[("wgb", gs)])
                br = gq()
                S.op("pe", lambda e, gs=gs, br=br, xcb=xcb: e.matmul(ps[:, br, :], lhsT=wgb[gs][:, 0, :], rhs=xcb, start=True, stop=True),
                     reads=[("wgb", gs), kk_("xcb")], writes=[("ps", br)], region=RM)
                bi = gq()
                S.op("pe", lambda e, gs=gs, bi=bi, xcb=xcb: e.matmul(ps[:, bi, :], lhsT=wgb[gs][:, 1, :], rhs=xcb, start=True, stop=True),
                     reads=[("wgb", gs), kk_("xcb")], writes=[("ps", bi)], region=RM)
                S.op("act", lambda e, br=br, thr_=thr_, c=c: e.activation(out=thr_, in_=ps[:, br, :], func=AF.Tanh, bias=hba[:, c:c + 1], scale=0.5),
                     reads=[("ps", br), "hba"], writes=[kk_("thr")], region=RM)
                S.op("act", lambda e, bi=bi, thi=thi, c=c: e.activation(out=thi, in_=ps[:, bi, :], func=AF.Tanh, bias=hbx[:, c:c + 1], scale=0.5),
                     reads=[("ps", bi), "hbx"], writes=[kk_("thi")], region=RM)
                S.op("act", lambda e, thr_=thr_, a_=a_, c=c: e.activation(out=a_, in_=thr_, func=AF.Exp, bias=hclam[:, c:c + 1], scale=hclam[:, c:c + 1]),
                     reads=[kk_("thr"), "hclam"], writes=[kk_("a")], region=RM)
                S.op("act", lambda e, thr_=thr_, a2=a2, c=c: e.activation(out=a2, in_=thr_, func=AF.Exp, bias=clam[:, c:c + 1], scale=clam[:, c:c + 1]),
                     reads=[kk_("thr"), "clam"], writes=[kk_("a2")], region=RM)
                S.op("act", lambda e, a2=a2: e.activation(out=a2, in_=a2, func=AF.Sqrt, bias=1.0, scale=-1.0),
                     reads=[kk_("a2")], writes=[kk_("a2")], region=RM)
                S.op("dve", lambda e, thi=thi, xc=xc, b_=b_: e.scalar_tensor_tensor(out=b_, in0=thi, scalar=1.0, in1=xc, op0=ALU.add, op1=ALU.mult),
                     reads=[kk_("thi"), kk_("xc")], writes=[kk_("b")], region=RM)
                if i == 0:
                    S.op("pool", lambda e, a2=a2: e.memset(a2[:, 0:1], 1.0), reads=[kk_("a2")], writes=[kk_("a2")], region=RM)
                S.op("dve", lambda e, b_=b_, a2=a2: e.scalar_tensor_tensor(out=b_, in0=b_, scalar=0.5, in1=a2, op0=ALU.mult, op1=ALU.mult),
                     reads=[kk_("b"), kk_("a2")], writes=[kk_("b")], region=RM)
                S.op("dve", lambda e, hh=hh, a_=a_, b_=b_, c=c: e.tensor_tensor_scan(out=hh, data0=a_, data1=b_, initial=hcar[:, c:c + 1],
                                                                                  op0=ALU.mult, op1=ALU.add),
                     reads=[kk_("a"), kk_("b"), "hcar"], writes=[kk_("h")], region=RM)
                S.op("dve", lambda e, hh=hh, c=c: e.tensor_copy(out=hcar[:, c:c + 1], in_=hh[:, 511:512]), reads=[kk_("h")], writes=["hcar"], region=RM)
                if half == 0:
                    S.op("dve", lambda e, hh=hh, c=c: e.tensor_scalar(out=hg[:, c, :], in0=hh, scalar1=selt[:, 0:1], scalar2=None, op0=ALU.mult),
                         reads=[kk_("h"), "selt"], writes=[("hg", c)], region=RM)
                else:
                    S.op("dve", lambda e, hh=hh, c=c: e.scalar_tensor_tensor(out=hg[:, c, :], in0=hh, scalar=selt[:, 1:2], in1=hg[:, c, :],
                                                                          op0=ALU.mult, op1=ALU.add),
                         reads=[kk_("h"), "selt", ("hg", c)], writes=[("hg", c)], region=RM)

        def own(m):
            stage('rnn')
            RA = ("R", "att%d" % m)
            RO = ("R", "out%d" % m)
            S.dma("pool", lambda e: e.dma_start(max_dma_last_dim=4096, out=xT[:].rearrange("p a b -> p (a b)"), in_=xTo_d[m]), "xT", writes=["xT"])
            for c in range(16):
                bank = proj_fm(RG + c, xT_rhs, ["xT"], 512)
                t0 = ft[c % 2]
                S.op("act", lambda e, bank=bank, t0=t0: e.activation(out=t0[:], in_=ps[:, bank, :], func=AF.Tanh, scale=0.5),
                     reads=[("ps", bank)], writes=[("ft", c % 2)])
                S.op("dve", lambda e, bank=bank, t0=t0: e.scalar_tensor_tensor(out=t0[:], in0=t0[:], scalar=1.0, in1=ps[:, bank, :], op0=ALU.add, op1=ALU.mult),
                     reads=[("ps", bank), ("ft", c % 2)], writes=[("ft", c % 2)])
                S.op("dve", lambda e, t0=t0, c=c: e.scalar_tensor_tensor(out=hg[:, c, :], in0=t0[:], scalar=0.5, in1=hg[:, c, :], op0=ALU.mult, op1=ALU.mult),
                     reads=[("ft", c % 2), ("hg", c)], writes=[("hg", c)])
            if dbg and m == 0:
                for c in range(16):
                    S.op("act", lambda e, c=c: e.activation(out=ft[2][:], in_=hg[:, c, :], func=AF.Copy), reads=[("hg", c)], writes=[("ft", 2)])
                    S.dma("sp", lambda e, c=c: e.dma_start(out=dbg_out["d_hg"][:, c * 512:(c + 1) * 512], in_=ft[2][:]), "dbg1", reads=[("ft", 2)], final=True)
            hgk = [("hg", c) for c in range(16)]
            for jc in range(16):
                ba = proj_fm(WB + jc, lambda k: hg[:, k, :], hgk, 512)
                bb = proj_fm(GB + jc, xT_rhs, ["xT"], 512)
                t0 = ft[jc % 2]
                S.op("act", lambda e, bb=bb, t0=t0: e.activation(out=t0[:], in_=ps[:, bb, :], func=AF.Tanh, scale=0.5),
                     reads=[("ps", bb)], writes=[("ft", jc % 2)])
                S.op("dve", lambda e, ba=ba, t0=t0: e.scalar_tensor_tensor(out=t0[:], in0=t0[:], scalar=1.0, in1=ps[:, ba, :], op0=ALU.add, op1=ALU.mult),
                     reads=[("ps", ba), ("ft", jc % 2)], writes=[("ft", jc % 2)])
                S.op("act", lambda e, t0=t0, jc=jc: e.activation(out=merged[:, jc, :], in_=t0[:], func=AF.Copy, scale=0.5),
                     reads=[("ft", jc % 2)], writes=[("mrg", jc)])

            stage('ownb')
            nkb_pair = 8 * m
            for hf in range(2):
                tsl = slice(hf * 256, (hf + 1) * 256)
                for c in range(8):
                    bank = proj_fm(QI + c, lambda k, tsl=tsl: xT[:, k, tsl], ["xT"], 256)
                    S.op("act", lambda e, bank=bank, c=c: e.activation(out=qiT[:, c, :], in_=ps[:, bank, 0:256], func=AF.Copy),
                         reads=[("ps", bank)], writes=["qiT"])
                for c in range(32):
                    bank = proj_fm(QL + c, lambda k, tsl=tsl: xT[:, k, tsl], ["xT"], 256)
                    if c % 2 == 0:
                        S.op("act", lambda e, bank=bank, c=c: e.activation(out=qT[:, c * 256:(c + 1) * 256], in_=ps[:, bank, 0:256], func=AF.Copy),
                             reads=[("ps", bank)], writes=["qT"], region=RA)
                    else:
                        S.op("dve", lambda e, bank=bank, c=c: e.tensor_copy(out=qT[:, c * 256:(c + 1) * 256], in_=ps[:, bank, 0:256]),
                             reads=[("ps", bank)], writes=["qT"], region=RA)
                for q2 in range(2):
                    qb = hf * 2 + q2
                    s = wslot()
                    bw = gp()
                    for piece in range(4):
                        if piece > 0:
                            s = wslot()
                        S.dma("pool", lambda e, s=s, piece=piece: e.dma_start(max_dma_last_dim=4096, out=wst[s][:, 0:1600], in_=wtok_d[piece]), "w%d" % s, writes=[("wst", s)])

                        def f(e, s=s, piece=piece, qb=qb, bw=bw):
                            ins = None
                            for kk in range(4):
                                k = piece * 4 + kk
                                ins = e.matmul(ps[:, bw, 0:16], lhsT=xT[:, k, qb * 128:(qb + 1) * 128],
                                               rhs=wst[s][:, kk * 400 + 384:kk * 400 + 400], start=(k == 0), stop=(k == 15))
                            return ins
                        S.op("pe", f, reads=[("wst", s), "xT"], writes=[("ps", bw)])
                    S.op("act", lambda e, bw=bw, qb=qb: e.activation(out=absw[:, qb, :], in_=ps[:, bw, 0:16], func=AF.Abs),
                         reads=[("ps", bw)], writes=["absw"])
                    S.op("act", lambda e, bw=bw, qb=qb: e.activation(out=sgnw[:, qb, :], in_=ps[:, bw, 0:16], func=AF.Sign),
                         reads=[("ps", bw)], writes=["sgnw"])

                for q2 in range(2):
                    qb = hf * 2 + q2
                    nkb = nkb_pair + 5 + qb
                    nk = nkb * 128
                    nkc = (nkb + 3) // 4
                    qsl = slice(q2 * 128, (q2 + 1) * 128)
                    for kc in range(nkc):
                        w = min(512, nk - kc * 512)
                        S.dma("sp", lambda e, kc=kc, w=w: e.dma_start(out=kix[:, 0:w], in_=kixT_d[:, kc * 512:kc * 512 + w]), "kix",
                              reads=[("kixT_d", t_) for t_ in range(kc, min(kc + 1, 2 * m + 2))], writes=["kix"])
                        accb = gp()
                        for h in range(16):
                            c = h // 2
                            po = (h % 2) * 64
                            zb = gq()
                            S.op("pe", lambda e, zb=zb, c=c, po=po, w=w, qsl=qsl: e.matmul(ps[:, zb, 0:w], lhsT=qiT[po:po + 64, c, qsl], rhs=kix[po:po + 64, 0:w],
                                                                                start=True, stop=True),
                                 reads=["qiT", "kix"], writes=[("ps", zb)])
                            it = itmp[h % 2]
                            S.op("act", lambda e, zb=zb, it=it, h=h, w=w, qb=qb: e.activation(out=it[:, 0:w], in_=ps[:, zb, 0:w], func=AF.Relu,
                                                                                          scale=absw[:, qb, h:h + 1]),
                                 reads=[("ps", zb), "absw"], writes=[("itmp", h % 2)])
                            if h == 0:
                                S.op("dve", lambda e, it=it, accb=accb, w=w, qb=qb: e.tensor_scalar(out=ps[:, accb, 0:w], in0=it[:, 0:w],
                                                                                                scalar1=sgnw[:, qb, 0:1], scalar2=None, op0=ALU.mult),
                                     reads=[("itmp", 0), "sgnw"], writes=[("ps", accb)])
                            else:
                                S.op("dve", lambda e, it=it, accb=accb, w=w, h=h, qb=qb: e.scalar_tensor_tensor(out=ps[:, accb, 0:w], in0=it[:, 0:w],
                                                                                                          scalar=sgnw[:, qb, h:h + 1], in1=ps[:, accb, 0:w],
                                                                                                          op0=ALU.mult, op1=ALU.add),
                                     reads=[("itmp", h % 2), "sgnw", ("ps", accb)], writes=[("ps", accb)])
                        S.op("act", lambda e, accb=accb, kc=kc, w=w: e.activation(out=scores[:, kc * 512:kc * 512 + w], in_=ps[:, accb, 0:w], func=AF.Copy),
                             reads=[("ps", accb)], writes=["scores"], region=RA)
                    stage('indexer')
                    am = sm[:, 8:9]
                    lo = sm[:, 9:10]
                    mid = sm[:, 10:11]
                    cnt = sm[:, 11:12]
                    dl = sm[:, 12:13]
                    wt = sm[:, 16:16 + NIT]
                    S.op("dve", lambda e, nk=nk, am=am: e.tensor_reduce(out=am, in_=scores[:, 0:nk], axis=AX.X, op=ALU.max, apply_absolute_value=True),
                         reads=["scores"], writes=["am"], region=RA)
                    S.op("dve", lambda e, am=am: e.tensor_scalar(out=am, in0=am, scalar1=1.0, scalar2=None, op0=ALU.add), reads=["am"], writes=["am"])
                    S.op("dve", lambda e, am=am, lo=lo: e.tensor_scalar(out=lo, in0=am, scalar1=-1.0, scalar2=None, op0=ALU.mult), reads=["am"], writes=["lo"])
                    S.op("dve", lambda e, am=am, wt=wt: e.tensor_scalar(out=wt, in0=pow2[:], scalar1=am, scalar2=None, op0=ALU.mult),
                         reads=["am", "pow2"], writes=["wt"])
                    S.dma("sp", lambda e, qb=qb: e.dma_start(out=cbt[:], in_=cbias_d[qb]), "cbt", writes=["cbt"])
                    cw = (5 + qb) * 128
                    S.op("dve", lambda e, cw=cw: e.tensor_tensor(out=scores[:, nkb_pair * 128:nkb_pair * 128 + cw],
                                                                in0=scores[:, nkb_pair * 128:nkb_pair * 128 + cw], in1=cbt[:, 0:cw], op=ALU.add),
                         reads=["scores", "cbt", "am"], writes=["scores"], region=RA)
                    junk = maskT[:].rearrange("p a b -> p (a b)")
                    for it_ in range(NIT):
                        S.op("dve", lambda e, it_=it_, lo=lo, mid=mid, wt=wt: e.tensor_tensor(out=mid, in0=lo, in1=wt[:, it_:it_ + 1], op=ALU.add),
                             reads=["lo", "wt"], writes=["mid"])
                        S.op("dve", lambda e, nk=nk, mid=mid, cnt=cnt: e.tensor_scalar(out=junk[:, 0:nk], in0=scores[:, 0:nk], scalar1=mid, scalar2=None,
                                                                                    op0=ALU.is_ge, op1=ALU.add, accum_out=cnt),
                             reads=["scores", "mid"], writes=["maskT", "cnt"], region=RA)
                        S.op("dve", lambda e, it_=it_, cnt=cnt, dl=dl, wt=wt: e.scalar_tensor_tensor(out=dl, in0=cnt, scalar=TOPK - 0.5, in1=wt[:, it_:it_ + 1],
                                                                                             op0=ALU.is_ge, op1=ALU.mult),
                             reads=["cnt", "wt"], writes=["dl"])
                        S.op("dve", lambda e, lo=lo, dl=dl: e.tensor_tensor(out=lo, in0=lo, in1=dl, op=ALU.add), reads=["lo", "dl"], writes=["lo"])
                    if dbg and m == 0:
                        S.dma("sp", lambda e, qb=qb, lo=lo: e.dma_start(out=dbg_out["d_thr"][:, qb:qb + 1], in_=lo, allow_slow_non_contiguous=True), "dbg2", reads=["lo"], final=True)
                        if qb == 3:
                            S.dma("sp", lambda e: e.dma_start(out=dbg_out["d_sc"], in_=scores[:, 0:1024]), "dbg3", reads=["scores"], region=RA, final=True)
                    stage('bisect')
                    for kc in range(nkc):
                        w = min(512, nk - kc * 512)
                        nb = w // 128
                        S.op("dve", lambda e, kc=kc, w=w, lo=lo: e.tensor_scalar(out=mk[:, 0:w], in0=scores[:, kc * 512:kc * 512 + w], scalar1=lo, scalar2=None,
                                                                              op0=ALU.is_ge),
                             reads=["scores", "lo"], writes=["mk"], region=RA)
                        tbk = gp()
                        pv = ps[:, tbk, :].bitcast(BF16)

                        def ftr(e, pv=pv, nb=nb):
                            ins = None
                            for j in range(nb):
                                ins = e.transpose(out=pv[:, j * 128:(j + 1) * 128], in_=mk[:, j * 128:(j + 1) * 128], identity=ident[:])
                            return ins
                        S.op("pe", ftr, reads=["mk", "ident"], writes=[("ps", tbk)])
                        S.op("act", lambda e, kc=kc, w=w, nb=nb, pv=pv: e.activation(out=maskT[:, kc * 4:kc * 4 + nb, :].rearrange("p a b -> p (a b)"),
                                                                                  in_=pv[:, 0:w], func=AF.Copy),
                             reads=[("ps", tbk)], writes=["maskT"])
                    stage('mask')
                    for hq in range(4):
                        ws_ = hq % 2
                        S.dma("pool", lambda e, ws_=ws_, hq=hq: e.dma_start(max_dma_last_dim=4096, out=wuvb[ws_][:].rearrange("p a b -> p (a b)"),
                                                                             in_=wuv_d[:, hq * 1024:(hq + 1) * 1024]), "wuv%d" % ws_, writes=[("wuvb", ws_)])
                        qv = qT.rearrange("p (c t) -> p c t", c=32)
                        DEP = 2

                        def chunk_dma(kc):
                            w = min(512, nk - kc * 512)
                            nb = w // 128
                            cb_ = kc % 2
                            S.dma("sp", lambda e, kc=kc, nb=nb, cb_=cb_: e.dma_start(out=ckvc[cb_][:, 0:nb, :],
                                                                                      in_=ckv_d[kc * 4:kc * 4 + nb].rearrange("b s d -> s b d")),
                                  "ckvc%d" % cb_, reads=[("ckv_d", kc)], writes=[("ckvc", cb_)])
                            S.dma("sp", lambda e, kc=kc, w=w, cb_=cb_: e.dma_start(out=ckvTc[cb_][:, :, 0:w], in_=ckvT_d[:, :, kc * 512:kc * 512 + w]),
                                  "ckvTc%d" % cb_, reads=[("ckvT_d", kc)], writes=[("ckvTc", cb_)])

                        chunk_dma(0)
                        if nkc > 1:
                            chunk_dma(1)
                        for idx_ in range(nkb + DEP):
                            if idx_ < nkb:
                                kb = idx_
                                kc, j = kb // 4, kb % 4
                                cb_ = kc % 2
                                rel = kb - nkb_pair
                                slot = {qb - 1: 0, qb: 1, qb + 3: 2, qb + 4: 3}.get(rel)
                                qkb = 3 + (kb % 3)
                                pp = kb % 3

                                def fqk(e, j=j, qkb=qkb, slot=slot, hq=hq, qsl=qsl, qv=qv, cb_=cb_):
                                    ins = None
                                    for k in range(2):
                                        ins = e.matmul(ps[:, qkb, :], lhsT=ckvTc[cb_][:, k, j * 128:(j + 1) * 128],
                                                       rhs=qv[:, hq * 8 + k:hq * 8 + 8:2, qsl], start=(k == 0), stop=(k == 1 and slot is None))
                                    if slot is not None:
                                        ins = e.matmul(ps[:, qkb, :], lhsT=ident[:], rhs=biasS[:, slot, hq * 4:hq * 4 + 4, :], start=False, stop=True)
                                    return ins
                                S.op("pe", fqk, reads=[("ckvTc", cb_), "qT", "ident", "biasS"], writes=[("ps", qkb)], region=RA)
                                S.op("act", lambda e, qkb=qkb, pp=pp: e.activation(out=Pb[pp][:], in_=ps[:, qkb, :], func=AF.Exp, scale=1.0 / 16.0),
                                     reads=[("ps", qkb)], writes=[("Pb", pp)])
                                S.op("dve", lambda e, pp=pp, kb=kb: e.tensor_tensor(out=Pm[pp][:], in0=Pb[pp][:].rearrange("p (a b) -> p a b", a=4),
                                                                                 in1=maskT[:, kb:kb + 1, :].to_broadcast([128, 4, 128]), op=ALU.mult),
                                     reads=[("Pb", pp), "maskT"], writes=[("Pm", pp)])
                            if idx_ >= DEP:
                                kb = idx_ - DEP
                                kc, j = kb // 4, kb % 4
                                cb_ = kc % 2
                                pp = kb % 3

                                def fpv(e, j=j, pp=pp, kb=kb, nkb=nkb, cb_=cb_):
                                    rhs = Pm[pp][:].rearrange("p a b -> p (a b)")
                                    e.matmul(ps[:, 0, :], lhsT=ckvc[cb_][:, j, 0:128], rhs=rhs, start=(kb == 0), stop=(kb == nkb - 1))
                                    e.matmul(ps[:, 1, :], lhsT=ckvc[cb_][:, j, 128:256], rhs=rhs, start=(kb == 0), stop=(kb == nkb - 1))
                                    return e.matmul(ps[:, 2, :], lhsT=ones[:], rhs=rhs, start=(kb == 0), stop=(kb == nkb - 1))
                                S.op("pe", fpv, reads=[("ckvc", cb_), ("Pm", pp), "ones"], writes=[("ps", 0), ("ps", 1), ("ps", 2)])
                                if j == 3 and kc + 2 < nkc:
                                    chunk_dma(kc + 2)
                        S.op("dve", lambda e: e.reciprocal(out=rden[:], in_=ps[:, 2, :]), reads=[("ps", 2)], writes=["rden"])
                        for k in range(2):
                            S.op("dve", lambda e, k=k: e.tensor_tensor(out=onT[:, k, :], in0=ps[:, k, :], in1=rden[:], op=ALU.mult),
                                 reads=[("ps", k), "rden"], writes=["onT"])
                        ub = gp()

                        def fuv(e, ub=ub, ws_=ws_):
                            ins = None
                            for hh_ in range(4):
                                for k in range(2):
                                    ins = e.matmul(ps[:, ub, hh_ * 128:(hh_ + 1) * 128], lhsT=wuvb[ws_][:, hh_ * 2 + k, :],
                                                   rhs=onT[:, k, hh_ * 128:(hh_ + 1) * 128], start=(k == 0), stop=(k == 1))
                            return ins
                        S.op("pe", fuv, reads=["onT", ("wuvb", ws_)], writes=[("ps", ub)])
                        S.op("act", lambda e, ub=ub, hq=hq, qb=qb: e.activation(out=hg[:, hq * 4:hq * 4 + 4, qb * 128:(qb + 1) * 128],
                                                                              in_=ps[:, ub, :].rearrange("p (a b) -> p a b", a=4), func=AF.Copy),
                             reads=[("ps", ub)], writes=[("hg", hq * 4 + x) for x in range(4)])
            stage('attn')
            for c in range(16):
                bank = proj_fm(AG + c, xT_rhs, ["xT"], 512)
                t0 = ft[c % 2]
                S.op("act", lambda e, bank=bank, t0=t0: e.activation(out=t0[:], in_=ps[:, bank, :], func=AF.Tanh, scale=0.5),
                     reads=[("ps", bank)], writes=[("ft", c % 2)])
                S.op("dve", lambda e, bank=bank, t0=t0: e.scalar_tensor_tensor(out=t0[:], in0=t0[:], scalar=1.0, in1=ps[:, bank, :], op0=ALU.add, op1=ALU.mult),
                     reads=[("ps", bank), ("ft", c % 2)], writes=[("ft", c % 2)])
                S.op("dve", lambda e, t0=t0, c=c: e.scalar_tensor_tensor(out=hg[:, c, :], in0=t0[:], scalar=0.5, in1=hg[:, c, :], op0=ALU.mult, op1=ALU.mult),
                     reads=[("ft", c % 2), ("hg", c)], writes=[("hg", c)])
            if dbg and m == 0:
                for c in range(16):
                    S.op("act", lambda e, c=c: e.activation(out=ft[2][:], in_=hg[:, c, :], func=AF.Copy), reads=[("hg", c)], writes=[("ft", 2)])
                    S.dma("sp", lambda e, c=c: e.dma_start(out=dbg_out["d_ga"][:, c * 512:(c + 1) * 512], in_=ft[2][:]), "dbg4", reads=[("ft", 2)], final=True)
            stage('gate')
            for jc in range(16):
                ba = proj_fm(WA + jc, lambda k: hg[:, k, :], hgk, 512)
                bb = proj_fm(GA + jc, xT_rhs, ["xT"], 512)
                t0 = ft[jc % 2]
                S.op("act", lambda e, bb=bb, t0=t0: e.activation(out=t0[:], in_=ps[:, bb, :], func=AF.Tanh, scale=0.5),
                     reads=[("ps", bb)], writes=[("ft", jc % 2)])
                S.op("dve", lambda e, ba=ba, t0=t0: e.scalar_tensor_tensor(out=t0[:], in0=t0[:], scalar=1.0, in1=ps[:, ba, :], op0=ALU.add, op1=ALU.mult),
                     reads=[("ps", ba), ("ft", jc % 2)], writes=[("ft", jc % 2)])
                S.op("dve", lambda e, t0=t0, jc=jc: e.scalar_tensor_tensor(out=merged[:, jc, :], in0=t0[:], scalar=0.5, in1=merged[:, jc, :], op0=ALU.mult, op1=ALU.add),
                     reads=[("ft", jc % 2), ("mrg", jc)], writes=[("mrg", jc)])
            if dbg and m == 0:
                for c in range(16):
                    S.op("act", lambda e, c=c: e.activation(out=ft[2][:], in_=merged[:, c, :], func=AF.Copy), reads=[("mrg", c)], writes=[("ft", 2)])
                    S.dma("sp", lambda e, c=c: e.dma_start(out=dbg_out["d_mrg"][:, c * 512:(c + 1) * 512], in_=ft[2][:]), "dbg5", reads=[("ft", 2)], final=True)
            stage('brancha')
            mk_ = [("mrg", c) for c in range(16)]
            S.dma("sp", lambda e: e.dma_start(out=lng, in_=lng_d), "lng", writes=["lng"], region=RO)
            S.dma("sp", lambda e: e.dma_start(out=lnb, in_=lnb_d), "lnb", writes=["lnb"], region=RO)
            for tb in range(4):
                S.dma("sp", lambda e, tb=tb: e.dma_start(out=ysub[:, tb * 2048:(tb + 1) * 2048], in_=xtok_d[m, tb]), "xtk%d" % tb,
                      writes=[("ysub", tb)], region=RO)
            for jj in range(8):
                wsl = jj % 2
                S.dma("pool", lambda e, wsl=wsl, jj=jj: e.dma_start(out=wo[wsl], in_=wout_b[jj]), "wo%d" % wsl, reads=[("wob", jj)], writes=[("wo", wsl)], region=RO)
                for tb in range(4):
                    ob = gp()

                    def fo(e, ob=ob, wsl=wsl, tb=tb):
                        ins = None
                        for k in range(16):
                            ins = e.matmul(ps[:, ob, 0:256], lhsT=merged[:, k, tb * 128:(tb + 1) * 128], rhs=wo[wsl][:, k * 256:(k + 1) * 256],
                                           start=(k == 0), stop=(k == 15))
                        return ins
                    S.op("pe", fo, reads=mk_ + [("wo", wsl)], writes=[("ps", ob)], region=RO)
                    ys = ysub[:, tb * 2048 + jj * 256: tb * 2048 + (jj + 1) * 256]
                    S.op("dve", lambda e, ob=ob, ys=ys: e.scalar_tensor_tensor(out=ys, in0=ys, scalar=ALPHA, in1=ps[:, ob, 0:256], op0=ALU.mult, op1=ALU.add),
                         reads=[("ps", ob), ("ysub", tb)], writes=[("ysub", tb)], region=RO)
            for tb in range(4):
                yv = ysub[:, tb * 2048:(tb + 1) * 2048]
                mv = sm[:, 40:42]
                rs = sm[:, 42:43]
                for q in range(4):
                    S.op("dve", lambda e, yv=yv, q=q: e.bn_stats(out=bst[:, q, :], in_=yv[:, q * 512:(q + 1) * 512]),
                         reads=[("ysub", tb)], writes=[("bst", q)], region=RO)
                S.op("dve", lambda e, mv=mv: e.bn_aggr(out=mv, in_=bst[:].rearrange("p a b -> p (a b)")),
                     reads=[("bst", q) for q in range(4)], writes=["mv"])
                S.op("dve", lambda e, mv=mv, rs=rs: e.tensor_scalar(out=rs, in0=mv[:, 1:2], scalar1=EPS, scalar2=None, op0=ALU.add), reads=["mv"], writes=["rs"])
                S.op("pool", lambda e, rs=rs: e.tensor_tensor(out=rs, in0=rs, in1=mhalf[:], op=ALU.pow), reads=["rs", "mhalf"], writes=["rs"])
                S.op("dve", lambda e, yv=yv, mv=mv, rs=rs: e.tensor_scalar(out=yv, in0=yv, scalar1=mv[:, 0:1], scalar2=rs, op0=ALU.subtract, op1=ALU.mult),
                     reads=[("ysub", tb), "mv", "rs"], writes=[("ysub", tb)], region=RO)
                S.op("dve", lambda e, yv=yv: e.tensor_tensor(out=yv, in0=yv, in1=lng, op=ALU.mult), reads=[("ysub", tb), "lng"], writes=[("ysub", tb)], region=RO)
                S.op("dve", lambda e, yv=yv: e.tensor_tensor(out=yv, in0=yv, in1=lnb, op=ALU.add), reads=[("ysub", tb), "lnb"], writes=[("ysub", tb)], region=RO)
                S.dma("sp", lambda e, tb=tb, yv=yv: e.dma_start(out=y_d[m, tb], in_=yv), "yo%d" % tb, reads=[("ysub", tb)], region=RO, final=True)

        try:
            stage('init')
            for m in range(n_pairs):
                fullseq(2 * m, 0, m)
                fullseq(2 * m + 1, 1, m)
                own(m)
        except _Stop:
            pass
        S.emit()
    return nc


def _t5_bucket_np(n):
    n = np.maximum(n, 0)
    nf = np.maximum(n, 1).astype(np.float32)
    large = 16 + (np.log(nf / np.float32(16)) / np.float32(math.log(128 / 16)) * np.float32(16)).astype(np.int32)
    large = np.minimum(large, 31)
    return np.where(n < 16, n, large)


def _fm(w):
    n = w.shape[1] // 128
    return np.ascontiguousarray(w.reshape(16, 128, n, 128).transpose(2, 1, 0, 3)).reshape(n, 128, 16 * 128)


def _vec(v):
    return np.ascontiguousarray(v.reshape(16, 128).T)


def make_inputs(x, w_in, kv_norm_g, w_uv, w_branch_a, conv_w, conv_b, w_gate_a, b_gate_a,
                w_gate_x, b_gate_x, lru_lambda, w_branch_b, rel_bias, w_out, ln_g, ln_b):
    w_in = w_in[0]
    chunks = []
    for base, n in ((QL, 32), (AG, 16), (QI, 8), (XR, 16), (RG, 16), (GA, 16), (GB, 16)):
        c0 = COL[base]
        chunks.append(_fm(w_in[:, c0:c0 + n * 128]))
    chunks.append(_fm(w_branch_a[0]))
    chunks.append(_fm(w_branch_b[0]))
    wfm = np.concatenate(chunks, axis=0)
    tokcols = np.concatenate([np.arange(4096, 4352), np.arange(7424, 7488), np.arange(7424, 7488), np.arange(7488, 7504)])
    wt = w_in[:, tokcols]
    wtok = np.ascontiguousarray(wt.reshape(4, 4, 128, 400).transpose(0, 2, 1, 3)).reshape(4, 128, 1600)
    wout = np.ascontiguousarray(w_out[0].reshape(16, 128, 8, 256).transpose(2, 1, 0, 3)).reshape(8, 128, 16 * 256)
    wuv = np.ascontiguousarray(w_uv[0].reshape(16, 2, 128, 128).transpose(2, 0, 1, 3)).reshape(128, 32 * 128)
    wg = np.ascontiguousarray(np.stack([w_gate_a[0], w_gate_x[0]], 0).transpose(2, 0, 1, 3)).reshape(128, 32 * 128)
    cvec = np.stack([_vec(conv_w[0, 0]), _vec(conv_w[0, 1]), _vec(conv_w[0, 2]), _vec(conv_w[0, 3]), _vec(conv_b[0]),
                     _vec(b_gate_a[0]), _vec(b_gate_x[0]), _vec(lru_lambda[0])], axis=1).reshape(128, 128)
    kvg = np.ascontiguousarray(np.broadcast_to(kv_norm_g[0][None, :], (128, 256)))
    lng = np.ascontiguousarray(np.broadcast_to(ln_g[0][None, :], (128, D)))
    lnb = np.ascontiguousarray(np.broadcast_to(ln_b[0][None, :], (128, D)))
    s_ = np.arange(128)[:, None]
    t_ = np.arange(128)[None, :]
    diag = rel_bias[_t5_bucket_np(t_ - s_)]
    prev = rel_bias[_t5_bucket_np(128 + t_ - s_)]
    diag = np.ascontiguousarray(diag.transpose(0, 2, 1))
    prev = np.ascontiguousarray(prev.transpose(0, 2, 1))
    b31 = np.ascontiguousarray(np.broadcast_to(rel_bias[31][None, :, None], (128, 16, 128)))
    pow2 = np.ascontiguousarray(np.broadcast_to((2.0 ** -np.arange(NIT, dtype=np.float64)).astype(np.float32)[None, :], (128, NIT)))
    shared = dict(wfm=wfm, wtok=wtok, wout=wout, wuv=wuv, wg=wg, cvec=np.ascontiguousarray(cvec), kvg=kvg, lng=lng, lnb=lnb,
                  b31=b31.reshape(128, 2048), pow2=pow2)
    in_maps = []
    for core in range(8):
        b, par = core // 2, core % 2
        xb = x[b]
        xT = np.ascontiguousarray(xb.reshape(NT, 512, 16, 128).transpose(0, 3, 2, 1)).reshape(NT, 128, 16 * 512)
        xTo = np.ascontiguousarray(xT[par::2])
        xtok = np.ascontiguousarray(xb.reshape(NP, 2, 4, 128, D)[:, par])
        sel = np.zeros((128, 2), np.float32)
        sel[:, par] = 1.0
        cb = np.zeros((4, 128, 1024), np.float32)
        for qb in range(4):
            tg = par * 512 + qb * 128 + np.arange(128)[:, None]
            sg = np.arange(1024)[None, :]
            cb[qb] = np.where(sg <= tg, 0.0, NEG)
        if par == 0:
            slots = [prev, diag, b31, b31]
        else:
            slots = [b31, b31, prev, diag]
        biasS = np.ascontiguousarray(np.stack(slots, axis=1)).reshape(128, 4 * 16 * 128).astype(np.float32)
        d = dict(shared)
        d.update(xT=xT, xTo=xTo, xtok=xtok, sel=sel, cbias=cb, biasS=biasS)
        in_maps.append(d)
    return in_maps


def kernel(**inputs):
    inputs = {k: np.asarray(v) for k, v in inputs.items()}
    in_maps = make_inputs(**inputs)
    nc = build_nc()
    res = run_bass_kernel_spmd(nc, in_maps, core_ids=list(range(8)))
    out = np.empty((4, T, D), np.float32)
    for core in range(8):
        b, par = core // 2, core % 2
        yv = res.results[core]["y"].reshape(NP, 512, D)
        out.reshape(4, NP, 2, 512, D)[b, :, par] = yv
    return out
```

```python
import math
from contextlib import ExitStack

import numpy as np
import concourse.bass as bass
import concourse.mybir as mybir
from concourse.bass_utils import run_bass_kernel_spmd

F32 = mybir.dt.float32
BF16 = mybir.dt.bfloat16
U8 = mybir.dt.uint8
AF = mybir.ActivationFunctionType
ALU = mybir.AluOpType
AX = mybir.AxisListType

D = 2048
T = 8192
NT = 16
NP = 8
TOPK = 256
NIT = 20
ALPHA = 2.0 ** 0.25
EPS = 1e-5
NEG = -1.0e30
QL, AG, QI, XR, RG, GA, GB, WA, WB = 0, 32, 48, 56, 72, 88, 104, 120, 136
NCH = 152
COL = {QL: 0, AG: 4352, QI: 6400, XR: 7504, RG: 9552, GA: 11600, GB: 13648}


class _Op:
    __slots__ = ("eng", "fn", "waits", "idx", "inc", "semval", "dma_sem", "dma_val")

    def __init__(self, eng, fn):
        self.eng = eng
        self.fn = fn
        self.waits = []
        self.idx = -1
        self.inc = False
        self.semval = 0
        self.dma_sem = None
        self.dma_val = 0


class Sched:
    ENG = ("pe", "act", "dve", "pool", "sp")
    SAME = ("act", "dve", "pool")

    def __init__(self, nc, stack):
        self.nc = nc
        self.stack = stack
        self.ops = {e: [] for e in self.ENG}
        self.last_w = {}
        self.readers = {}
        self.dma_sems = {}
        self.final_tokens = []
        self.reg_mode = {}
        self.reg_cur = {}
        self.reg_fence = {}

    def _dma_sem(self, name):
        if name not in self.dma_sems:
            h = self.stack.enter_context(self.nc.semaphore("d_" + name))
            self.dma_sems[name] = [h, 0]
        return self.dma_sems[name]

    @staticmethod
    def _compress(toks):
        be = {}
        bd = {}
        for t in toks:
            if t[0] == "e":
                p = t[1]
                if p.eng not in be or be[p.eng].idx < p.idx:
                    be[p.eng] = p
            else:
                if bd.get(t[1], 0) < t[2]:
                    bd[t[1]] = t[2]
        return [("e", p) for p in be.values()] + [("d", k, v) for k, v in bd.items()]

    def _deps(self, reads, writes, region):
        toks = []
        for k in reads:
            t = self.last_w.get(k)
            if t is not None:
                toks.append(t)
        for k in writes:
            t = self.last_w.get(k)
            if t is not None:
                toks.append(t)
            toks.extend(self.readers.get(k, ()))
        if region is not None:
            r, mode = region
            if self.reg_mode.get(r) != mode:
                self.reg_fence[r] = self._compress(self.reg_cur.get(r, []) + self.reg_fence.get(r, []))
                self.reg_cur[r] = []
                self.reg_mode[r] = mode
            toks.extend(self.reg_fence.get(r, ()))
        return self._compress(toks)

    def _commit(self, tok, reads, writes, region):
        for k in reads:
            lst = self.readers.setdefault(k, [])
            lst.append(tok)
            if len(lst) > 12:
                self.readers[k] = self._compress(lst)
        for k in writes:
            self.last_w[k] = tok
            self.readers[k] = []
        if region is not None:
            lst = self.reg_cur.setdefault(region[0], [])
            lst.append(tok)
            if len(lst) > 12:
                self.reg_cur[region[0]] = self._compress(lst)

    @staticmethod
    def _excl(reads, writes):
        r = [k for k in reads if not (isinstance(k, tuple) and k[0] == "ps")]
        w = list(writes) + [k for k in reads if isinstance(k, tuple) and k[0] == "ps" and k not in writes]
        return r, w

    def op(self, eng, fn, reads=(), writes=(), region=None):
        reads, writes = self._excl(reads, writes)
        o = _Op(eng, fn)
        o.idx = len(self.ops[eng])
        o.waits = self._deps(reads, writes, region)
        self.ops[eng].append(o)
        self._commit(("e", o), reads, writes, region)
        return o

    def dma(self, eng, fn, sem, reads=(), writes=(), region=None, final=False):
        o = _Op(eng, fn)
        o.idx = len(self.ops[eng])
        o.waits = self._deps(reads, writes, region)
        s = self._dma_sem(sem)
        s[1] += 16
        o.dma_sem = sem
        o.dma_val = s[1]
        self.ops[eng].append(o)
        tok = ("d", sem, s[1])
        self._commit(tok, reads, writes, region)
        if final:
            self.final_tokens.append(tok)
        return o

    def dma_group(self, eng, fns, sem, keys):
        s = self._dma_sem(sem)
        for fn in fns:
            o = _Op(eng, fn)
            o.idx = len(self.ops[eng])
            o.waits = []
            s[1] += 16
            o.dma_sem = sem
            o.dma_val = s[1]
            self.ops[eng].append(o)
        tok = ("d", sem, s[1])
        for k in keys:
            self.last_w[k] = tok
            self.readers[k] = []

    def emit(self, final_eng="sp"):
        nc = self.nc
        fo = _Op(final_eng, None)
        fo.idx = len(self.ops[final_eng])
        fo.waits = list(self.final_tokens) + [("d", k, v[1]) for k, v in self.dma_sems.items()]
        for e in self.ENG:
            if e != final_eng and self.ops[e]:
                fo.waits.append(("e", self.ops[e][-1]))
        self.ops[final_eng].append(fo)
        for e in self.ENG:
            for o in self.ops[e]:
                for t in o.waits:
                    if t[0] == "e":
                        p = t[1]
                        if p.eng != o.eng or o.eng in self.SAME:
                            p.inc = True
        esem = {}
        for e in self.ENG:
            esem[e] = self.stack.enter_context(nc.semaphore("e_" + e))
            c = 0
            for o in self.ops[e]:
                if o.inc:
                    c += 1
                    o.semval = c
        block = self.stack.enter_context(nc.Block())

        def run(e, eng):
            seen_e = {x: -1 for x in self.ENG}
            seen_d = {}
            for o in self.ops[e]:
                for t in o.waits:
                    if t[0] == "e":
                        p = t[1]
                        if p.eng == e and e not in self.SAME:
                            continue
                        if p.idx > seen_e[p.eng]:
                            eng.wait_ge(esem[p.eng], p.semval)
                            seen_e[p.eng] = p.idx
                    else:
                        if seen_d.get(t[1], 0) < t[2]:
                            eng.wait_ge(self.dma_sems[t[1]][0], t[2])
                            seen_d[t[1]] = t[2]
                if o.fn is None:
                    continue
                ins = o.fn(eng)
                if o.dma_sem is not None:
                    ins.then_inc(self.dma_sems[o.dma_sem][0], 16)
                elif o.inc:
                    ins.then_inc(esem[e], 1)

        @block.tensor
        def _(eng):
            run("pe", eng)

        @block.scalar
        def _(eng):
            run("act", eng)

        @block.vector
        def _(eng):
            run("dve", eng)

        @block.gpsimd
        def _(eng):
            run("pool", eng)

        @block.sync
        def _(eng):
            run("sp", eng)


class _Stop(Exception):
    pass


def build_nc(n_pairs=NP, dbg=False, stop=None):
    def stage(name):
        if stop is not None and name == stop:
            raise _Stop()

    nc = bass.Bass("TRN2", target_bir_lowering=False)

    def din(name, shape, dt=F32):
        return nc.dram_tensor(name, list(shape), dt, kind="ExternalInput").ap()

    xT_d = din("xT", [NT, 128, 16 * 512])
    xTo_d = din("xTo", [NP, 128, 16 * 512])
    xtok_d = din("xtok", [NP, 4, 128, D])
    wfm_d = din("wfm", [NCH, 128, 16 * 128])
    wtok_d = din("wtok", [4, 128, 4 * 400])
    wout_d = din("wout", [8, 128, 16 * 256])
    wuv_d = din("wuv", [128, 32 * 128])
    wg_d = din("wg", [128, 32 * 128])
    cvec_d = din("cvec", [128, 8 * 16])
    kvg_d = din("kvg", [128, 256])
    lng_d = din("lng", [128, D])
    lnb_d = din("lnb", [128, D])
    biasS_d = din("biasS", [128, 4 * 16 * 128])
    b31_d = din("b31", [128, 16 * 128])
    sel_d = din("sel", [128, 2])
    cbias_d = din("cbias", [4, 128, 1024])
    pow2_d = din("pow2", [128, NIT])
    y_d = nc.dram_tensor("y", [NP, 4, 128, D], F32, kind="ExternalOutput").ap()
    ckv_d = nc.dram_tensor("ckv_s", [64, 128, 256], BF16, kind="Internal").ap()
    ckvT_d = nc.dram_tensor("ckvT_s", [128, 2, T], BF16, kind="Internal").ap()
    kixT_d = nc.dram_tensor("kixT_s", [128, T], BF16, kind="Internal").ap()
    wfm_b = nc.dram_tensor("wfm_bf", [NCH, 128, 16 * 128], BF16, kind="Internal").ap()
    wout_b = nc.dram_tensor("wout_bf", [8, 128, 16 * 256], BF16, kind="Internal").ap()
    dbg_out = {}
    if dbg:
        for nm, shp in (("d_ckv", [128, 256]), ("d_sc", [128, 1024]), ("d_thr", [128, 4]),
                        ("d_hg", [128, 16 * 512]), ("d_mrg", [128, 16 * 512]), ("d_ga", [128, 16 * 512])):
            dbg_out[nm] = nc.dram_tensor(nm, shp, F32, kind="ExternalOutput").ap()

    st = ExitStack()
    with st:
        def sb(name, shape, dt):
            return st.enter_context(nc.sbuf_tensor("s_" + name, list(shape), dt))

        S = Sched(nc, st)
        biasS = sb("biasS", [128, 4, 16, 128], BF16)
        cvec = sb("cvec", [128, 8, 16], F32)
        clam = sb("clam", [128, 16], F32)
        hclam = sb("hclam", [128, 16], F32)
        hba = sb("hba", [128, 16], F32)
        hbx = sb("hbx", [128, 16], F32)
        kvg = sb("kvg", [128, 256], F32)
        selt = sb("selt", [128, 2], F32)
        pow2 = sb("pow2", [128, NIT], F32)
        mhalf = sb("mhalf", [128, 1], F32)
        ident = sb("ident", [128, 128], BF16)
        ones = sb("ones", [128, 128], BF16)
        tail = sb("tail", [128, 16, 3], F32)
        hcar = sb("hcar", [128, 16], F32)
        xT = sb("xT", [128, 16, 512], BF16)
        NW = 2 if dbg else 3
        wst = [sb("wst%d" % i, [128, 16 * 128], BF16) for i in range(NW)]
        wgb = [sb("wgb%d" % i, [128, 2, 128], BF16) for i in range(2)]
        wuvb = [sb("wuvb%d" % i, [128, 8, 128], BF16) for i in range(2)]
        merged = sb("merged", [128, 16, 512], BF16)
        hg = sb("hg", [128, 16, 512], BF16)
        maskT = [sb("maskT%d" % i, [128, 64, 128], U8) for i in range(2)]
        _qi0 = sb("qiT0", [128, 8, 256], BF16)
        qiT = [_qi0, _qi0]
        REG = sb("REG", [128, 16384], F32)
        kix = [sb("kix%d" % i, [128, 512], BF16) for i in range(2)]
        ckvc = [sb("ckvc%d" % i, [128, 4, 256], BF16) for i in range(2)]
        ckvTc = [sb("ckvTc%d" % i, [128, 2, 512], BF16) for i in range(2)]
        Pb = [sb("Pb%d" % i, [128, 512], BF16) for i in range(3)]
        Pm = [sb("Pm%d" % i, [128, 4, 128], BF16) for i in range(3)]
        onT = sb("onT", [128, 2, 512], BF16)
        rden = sb("rden", [128, 512], F32)
        itmp = [sb("itmp%d" % i, [128, 512], F32) for i in range(2)]
        mk = sb("mk", [128, 512], BF16)
        ckv_tok = [sb("ckvtok%d" % i, [128, 256], BF16) for i in range(2)]
        kix_tok = [sb("kixtok%d" % i, [128, 128], BF16) for i in range(2)]
        ckvT_st = sb("ckvT_st", [128, 2, 512], BF16)
        kixT_st = sb("kixT_st", [128, 512], BF16)
        absw = sb("absw", [128, 4, 16], F32)
        sgnw = sb("sgnw", [128, 4, 16], F32)
        sm = sb("sm", [128, 64], F32)
        ft = [sb("ft%d" % i, [128, 512], F32) for i in range(3 if dbg else 2)]
        cbt = sb("cbt", [128, 1024], F32)
        bst = sb("bst", [128, 4, 6], F32)
        ps = st.enter_context(nc.psum_tensor("ps", [128, 8, 512], F32))

        scores = REG[:, 0:8192]
        qT = REG[:, 8192:12288].bitcast(BF16)
        ysub = REG[:, 0:8192]
        wo = [REG[:, 8192 + i * 2048: 8192 + (i + 1) * 2048].bitcast(BF16) for i in range(2)]
        lng = REG[:, 12288:14336]
        lnb = REG[:, 14336:16384]

        def rv(i):
            return REG[:, i * 1024:(i + 1) * 1024]

        gpc = [0]

        def gp():
            b = 6 + gpc[0] % 2
            gpc[0] += 1
            return b

        gqc = [0]

        def gq():
            b = 3 + gqc[0] % 3
            gqc[0] += 1
            return b

        wc = [0]

        def wslot():
            s = wc[0] % NW
            wc[0] += 1
            return s

        def psb(bank):
            return ps[:, bank, :]

        S.dma("sp", lambda e: e.dma_start(out=cvec[:].rearrange("p a b -> p (a b)"), in_=cvec_d), "c0", writes=["cvec"])
        S.dma("sp", lambda e: e.dma_start(out=kvg[:], in_=kvg_d), "c1", writes=["kvg"])
        S.dma("sp", lambda e: e.dma_start(out=selt[:], in_=sel_d), "c2", writes=["selt"])
        S.dma("sp", lambda e: e.dma_start(out=pow2[:], in_=pow2_d), "c3", writes=["pow2"])
        S.dma("sp", lambda e: e.dma_start(out=REG[:, 0:8192], in_=biasS_d), "c4", writes=["ibias"], region=("R", "init"))
        S.dma("sp", lambda e: e.dma_start(out=REG[:, 8192:10240], in_=b31_d), "c5", writes=["ib31"], region=("R", "init"))
        for s_ in range(4):
            S.op("dve", lambda e, s_=s_: e.tensor_tensor(out=REG[:, s_ * 2048:(s_ + 1) * 2048], in0=REG[:, s_ * 2048:(s_ + 1) * 2048],
                                                          in1=REG[:, 8192:10240], op=ALU.subtract),
                 reads=["ib31", "ibias"], writes=[("ibs", s_)], region=("R", "init"))
            S.op("dve", lambda e, s_=s_: e.tensor_scalar(out=biasS[:, s_, :, :].rearrange("p a b -> p (a b)"),
                                                          in0=REG[:, s_ * 2048:(s_ + 1) * 2048], scalar1=16.0, scalar2=None, op0=ALU.mult),
                 reads=[("ibs", s_)], writes=["biasS"], region=("R", "init"))
        S.op("pool", lambda e: e.memset(ones[:], 1.0), writes=["ones"])
        S.op("pool", lambda e: e.memset(mhalf[:], -0.5), writes=["mhalf"])
        S.op("pool", lambda e: e.memset(tail[:], 0.0), writes=["tail"])
        S.op("pool", lambda e: e.memset(hcar[:], 0.0), writes=["hcar"])
        S.op("pool", lambda e: e.affine_select(out=ident[:], in_=ones[:], pattern=[[-1, 128]], compare_op=ALU.is_equal,
                                                fill=0.0, base=0, channel_multiplier=1),
             reads=["ones"], writes=["ident"])
        S.op("act", lambda e: e.activation(out=clam[:], in_=cvec[:, 7, :], func=AF.Exp, scale=-1.0), reads=["cvec"], writes=["clam"])
        S.op("act", lambda e: e.activation(out=clam[:], in_=clam[:], func=AF.Ln, bias=1.0, scale=1.0), reads=["clam"], writes=["clam"])
        S.op("dve", lambda e: e.tensor_scalar(out=clam[:], in0=clam[:], scalar1=-8.0, scalar2=None, op0=ALU.mult), reads=["clam"], writes=["clam"])
        S.op("dve", lambda e: e.tensor_scalar(out=hclam[:], in0=clam[:], scalar1=0.5, scalar2=None, op0=ALU.mult), reads=["clam"], writes=["hclam"])
        S.op("dve", lambda e: e.tensor_scalar(out=hba[:], in0=cvec[:, 5, :], scalar1=0.5, scalar2=None, op0=ALU.mult), reads=["cvec"], writes=["hba"])
        S.op("dve", lambda e: e.tensor_scalar(out=hbx[:], in0=cvec[:, 6, :], scalar1=0.5, scalar2=None, op0=ALU.mult), reads=["cvec"], writes=["hbx"])

        for gname, base, n in (("XR", XR, 16), ("RG", RG, 16), ("WB", WB, 16), ("GB", GB, 16), ("QI", QI, 8), ("QL", QL, 32),
                               ("AG", AG, 16), ("WA", WA, 16), ("GA", GA, 16)):
            S.dma_group("pool", [(lambda e, c=c: e.dma_start(max_dma_last_dim=4096, out=wfm_b[c], in_=wfm_d[c])) for c in range(base, base + n)],
                        "cv" + gname, [("wb", c) for c in range(base, base + n)])
        S.dma_group("pool", [(lambda e, c=c: e.dma_start(max_dma_last_dim=4096, out=wout_b[c], in_=wout_d[c])) for c in range(8)],
                    "cvWO", [("wob", c) for c in range(8)])

        def load_w(chunk):
            s = wslot()
            S.dma("pool", lambda e: e.dma_start(out=wst[s][:], in_=wfm_b[chunk]), "w%d" % s, reads=[("wb", chunk)], writes=[("wst", s)])
            return s

        def proj_fm(chunk, rhs_fn, rhs_keys, ncol, bank=None):
            s = load_w(chunk)
            if bank is None:
                bank = gp()

            def f(e):
                ins = None
                for k in range(16):
                    ins = e.matmul(ps[:, bank, 0:ncol], lhsT=wst[s][:, k * 128:(k + 1) * 128], rhs=rhs_fn(k),
                                   start=(k == 0), stop=(k == 15))
                return ins
            S.op("pe", f, reads=[("wst", s)] + list(rhs_keys), writes=[("ps", bank)])
            return bank

        def xT_rhs(k):
            return xT[:, k, :]

        def sig_half(bank, ncol, out_t, key, bias=None, rkeys=()):
            if bias is None:
                S.op("act", lambda e: e.activation(out=out_t, in_=ps[:, bank, 0:ncol], func=AF.Tanh, scale=0.5),
                     reads=[("ps", bank)], writes=[key])
            else:
                S.op("act", lambda e: e.activation(out=out_t, in_=ps[:, bank, 0:ncol], func=AF.Tanh, bias=bias, scale=0.5),
                     reads=[("ps", bank)] + list(rkeys), writes=[key])

        def fullseq(i, half, m):
            RM = ("R", "rnn%d" % i)
            S.dma("pool", lambda e: e.dma_start(max_dma_last_dim=4096, out=xT[:].rearrange("p a b -> p (a b)"), in_=xT_d[i]), "xT", writes=["xT"])
            stage('s1')
            for piece in range(4):
                s = wslot()
                S.dma("pool", lambda e, s=s, piece=piece: e.dma_start(max_dma_last_dim=4096, out=wst[s][:, 0:1600], in_=wtok_d[piece]), "w%d" % s,
                      writes=[("wst", s)])
                for tb in range(4):
                    def f(e, s=s, piece=piece, tb=tb):
                        ins = None
                        for kk in range(4):
                            k = piece * 4 + kk
                            ins = e.matmul(ps[:, tb, 0:400], lhsT=xT[:, k, tb * 128:(tb + 1) * 128],
                                           rhs=wst[s][:, kk * 400:(kk + 1) * 400], start=(k == 0), stop=(k == 15))
                        return ins
                    S.op("pe", f, reads=[("wst", s), "xT"], writes=[("ps", tb)])
            stage('s2')
            for tb in range(4):
                pb = tb % 2
                ss = sm[:, tb:tb + 1]
                rs = sm[:, 4 + tb:5 + tb]
                S.op("act", lambda e, tb=tb, ss=ss: e.activation(out=ft[0][:, 0:256], in_=ps[:, tb, 0:256], func=AF.Square, accum_out=ss),
                     reads=[("ps", tb)], writes=[("ft", 0), ("sm", tb)])
                S.op("dve", lambda e, ss=ss: e.tensor_scalar(out=ss, in0=ss, scalar1=1.0 / 256.0, scalar2=EPS, op0=ALU.mult, op1=ALU.add),
                     reads=[("sm", tb)], writes=[("sm", tb)])
                stage('s3_%d' % tb)
                S.op("pool", lambda e, ss=ss, rs=rs: e.tensor_tensor(out=rs, in0=ss, in1=mhalf[:], op=ALU.pow),
                     reads=[("sm", tb), "mhalf"], writes=[("sm", 4 + tb)])
                S.op("dve", lambda e, tb=tb, rs=rs, pb=pb: e.scalar_tensor_tensor(out=ckv_tok[pb][:], in0=ps[:, tb, 0:256], scalar=rs,
                                                                                 in1=kvg[:], op0=ALU.mult, op1=ALU.mult),
                     reads=[("ps", tb), ("sm", 4 + tb), "kvg"], writes=[("ckvtok", pb)])
                S.op("act", lambda e, tb=tb, pb=pb: e.activation(out=kix_tok[pb][:], in_=ps[:, tb, 256:384], func=AF.Copy),
                     reads=[("ps", tb)], writes=[("kixtok", pb)])
                if half == 1 or True:
                    pass
                if dbg and i == 0 and tb == 0:
                    S.op("act", lambda e: e.activation(out=ft[1][:, 0:256], in_=ckv_tok[0][:], func=AF.Copy),
                         reads=[("ckvtok", 0)], writes=[("ft", 1)])
                    S.dma("sp", lambda e: e.dma_start(out=dbg_out["d_ckv"], in_=ft[1][:, 0:256]), "dbg0", reads=[("ft", 1)], final=True)
                stage('s4_%d' % tb)
                S.dma("sp", lambda e, tb=tb, pb=pb: e.dma_start(out=ckv_d[i * 4 + tb], in_=ckv_tok[pb][:]), "skv%d" % pb,
                      reads=[("ckvtok", pb)], writes=[("ckv_d", i)])
                stage('s5_%d' % tb)
                tbk = gp()
                pv = ps[:, tbk, :].bitcast(BF16)

                def ftr(e, pb=pb, pv=pv):
                    e.transpose(out=pv[:, 0:128], in_=ckv_tok[pb][:, 0:128], identity=ident[:])
                    e.transpose(out=pv[:, 128:256], in_=ckv_tok[pb][:, 128:256], identity=ident[:])
                    return e.transpose(out=pv[:, 256:384], in_=kix_tok[pb][:], identity=ident[:])
                S.op("pe", ftr, reads=[("ckvtok", pb), ("kixtok", pb), "ident"], writes=[("ps", tbk)])
                stage('s6_%d' % tb)
                S.op("act", lambda e, tb=tb, pv=pv: e.activation(out=ckvT_st[:, :, tb * 128:(tb + 1) * 128],
                                                               in_=pv[:, 0:256].rearrange("p (a b) -> p a b", a=2), func=AF.Copy),
                     reads=[("ps", tbk)], writes=["ckvT_st"])
                S.op("dve", lambda e, tb=tb, pv=pv: e.tensor_copy(out=kixT_st[:, tb * 128:(tb + 1) * 128], in_=pv[:, 256:384]),
                     reads=[("ps", tbk)], writes=["kixT_st"])
                stage('s7_%d' % tb)
            stage('s8')
            S.dma("sp", lambda e: e.dma_start(out=ckvT_d[:, :, i * 512:(i + 1) * 512], in_=ckvT_st[:]), "skvT",
                  reads=["ckvT_st"], writes=[("ckvT_d", i)])
            S.dma("sp", lambda e: e.dma_start(out=kixT_d[:, i * 512:(i + 1) * 512], in_=kixT_st[:]), "skix",
                  reads=["kixT_st"], writes=[("kixT_d", i)])

            stage('tokproj')
            for c in range(16):
                par = c % 2
                o = par * 8
                xr = rv(o + 0)[:, 0:515]
                xc = rv(o + 1)[:, 0:512]
                xcb = rv(o + 1)[:, 512:768].bitcast(BF16)
                thr_ = rv(o + 2)[:, 0:512]
                thi = rv(o + 2)[:, 512:1024]
                a_ = rv(o + 3)[:, 0:512]
                a2 = rv(o + 3)[:, 512:1024]
                b_ = rv(o + 4)[:, 0:512]
                hh = rv(o + 4)[:, 512:1024]
                kk_ = lambda n: ("rt", par, n)
                bank = proj_fm(XR + c, xT_rhs, ["xT"], 512)
                S.op("dve", lambda e, xr=xr, c=c: e.tensor_copy(out=xr[:, 0:3], in_=tail[:, c, :]), reads=["tail"], writes=[kk_("xr0")], region=RM)
                S.op("act", lambda e, xr=xr, bank=bank: e.activation(out=xr[:, 3:515], in_=ps[:, bank, :], func=AF.Copy),
                     reads=[("ps", bank)], writes=[kk_("xr")], region=RM)
                S.op("dve", lambda e, xr=xr, c=c: e.tensor_copy(out=tail[:, c, :], in_=xr[:, 512:515]), reads=[kk_("xr")], writes=["tail"], region=RM)
                S.op("dve", lambda e, xr=xr, xc=xc, c=c: e.tensor_scalar(out=xc, in0=xr[:, 0:512], scalar1=cvec[:, 0, c:c + 1],
                                                                         scalar2=cvec[:, 4, c:c + 1], op0=ALU.mult, op1=ALU.add),
                     reads=[kk_("xr"), kk_("xr0"), "cvec"], writes=[kk_("xc")], region=RM)
                for k in range(1, 4):
                    S.op("dve", lambda e, xr=xr, xc=xc, c=c, k=k: e.scalar_tensor_tensor(out=xc, in0=xr[:, k:k + 512], scalar=cvec[:, k, c:c + 1],
                                                                                      in1=xc, op0=ALU.mult, op1=ALU.add),
                         reads=[kk_("xr"), kk_("xr0"), kk_("xc")], writes=[kk_("xc")], region=RM)
                S.op("act", lambda e, xc=xc, xcb=xcb: e.activation(out=xcb, in_=xc, func=AF.Copy), reads=[kk_("xc")], writes=[kk_("xcb")], region=RM)
                gs = c % 2
                S.dma("pool", lambda e, gs=gs, c=c: e.dma_start(max_dma_last_dim=4096, out=wgb[gs][:], in_=wg_d.rearrange("p (g n e) -> p g n e", g=2, n=16)[:, :, c, :]),
                      "wg%d" % gs, writes=[("wgb", gs)])
                br = gq()
                S.op("pe", lambda e, gs=gs, br=br, xcb=xcb: e.matmul(ps[:, br, :], lhsT=wgb[gs][:, 0, :], rhs=xcb, start=True, stop=True),
                     reads=[("wgb", gs), kk_("xcb")], writes=[("ps", br)], region=RM)
                bi = gq()
                S.op("pe", lambda e, gs=gs, bi=bi, xcb=xcb: e.matmul(ps[:, bi, :], lhsT=wgb[gs][:, 1, :], rhs=xcb, start=True, stop=True),
                     reads=[("wgb", gs), kk_("xcb")], writes=[("ps", bi)], region=RM)
                S.op("act", lambda e, br=br, thr_=thr_, c=c: e.activation(out=thr_, in_=ps[:, br, :], func=AF.Tanh, bias=hba[:, c:c + 1], scale=0.5),
                     reads=[("ps", br), "hba"], writes=[kk_("thr")], region=RM)
                S.op("act", lambda e, bi=bi, thi=thi, c=c: e.activation(out=thi, in_=ps[:, bi, :], func=AF.Tanh, bias=hbx[:, c:c + 1], scale=0.5),
                     reads=[("ps", bi), "hbx"], writes=[kk_("thi")], region=RM)
                S.op("act", lambda e, thr_=thr_, a_=a_, c=c: e.activation(out=a_, in_=thr_, func=AF.Exp, bias=hclam[:, c:c + 1], scale=hclam[:, c:c + 1]),
                     reads=[kk_("thr"), "hclam"], writes=[kk_("a")], region=RM)
                S.op("act", lambda e, thr_=thr_, a2=a2, c=c: e.activation(out=a2, in_=thr_, func=AF.Exp, bias=clam[:, c:c + 1], scale=clam[:, c:c + 1]),
                     reads=[kk_("thr"), "clam"], writes=[kk_("a2")], region=RM)
                S.op("act", lambda e, a2=a2: e.activation(out=a2, in_=a2, func=AF.Sqrt, bias=1.0, scale=-1.0),
                     reads=[kk_("a2")], writes=[kk_("a2")], region=RM)
                S.op("dve", lambda e, thi=thi, xc=xc, b_=b_: e.scalar_tensor_tensor(out=b_, in0=thi, scalar=1.0, in1=xc, op0=ALU.add, op1=ALU.mult),
                     reads=[kk_("thi"), kk_("xc")], writes=[kk_("b")], region=RM)
                if i == 0:
                    S.op("pool", lambda e, a2=a2: e.memset(a2[:, 0:1], 1.0), reads=[kk_("a2")], writes=[kk_("a2")], region=RM)
                S.op("dve", lambda e, b_=b_, a2=a2: e.scalar_tensor_tensor(out=b_, in0=b_, scalar=0.5, in1=a2, op0=ALU.mult, op1=ALU.mult),
                     reads=[kk_("b"), kk_("a2")], writes=[kk_("b")], region=RM)
                S.op("dve", lambda e, hh=hh, a_=a_, b_=b_, c=c: e.tensor_tensor_scan(out=hh, data0=a_, data1=b_, initial=hcar[:, c:c + 1],
                                                                                  op0=ALU.mult, op1=ALU.add),
                     reads=[kk_("a"), kk_("b"), "hcar"], writes=[kk_("h")], region=RM)
                S.op("dve", lambda e, hh=hh, c=c: e.tensor_copy(out=hcar[:, c:c + 1], in_=hh[:, 511:512]), reads=[kk_("h")], writes=["hcar"], region=RM)
                if half == 0:
                    S.op("dve", lambda e, hh=hh, c=c: e.tensor_scalar(out=hg[:, c, :], in0=hh, scalar1=selt[:, 0:1], scalar2=None, op0=ALU.mult),
                         reads=[kk_("h"), "selt"], writes=[("hg", c)], region=RM)
                else:
                    S.op("dve", lambda e, hh=hh, c=c: e.scalar_tensor_tensor(out=hg[:, c, :], in0=hh, scalar=selt[:, 1:2], in1=hg[:, c, :],
                                                                          op0=ALU.mult, op1=ALU.add),
                         reads=[kk_("h"), "selt", ("hg", c)], writes=[("hg", c)], region=RM)

        def own(m):
            stage('rnn')
            RA = ("R", "att%d" % m)
            RO = ("R", "out%d" % m)
            S.dma("pool", lambda e: e.dma_start(max_dma_last_dim=4096, out=xT[:].rearrange("p a b -> p (a b)"), in_=xTo_d[m]), "xT", writes=["xT"])
            for c in range(16):
                bank = proj_fm(RG + c, xT_rhs, ["xT"], 512)
                t0 = ft[c % 2]
                S.op("act", lambda e, bank=bank, t0=t0: e.activation(out=t0[:], in_=ps[:, bank, :], func=AF.Tanh, scale=0.5),
                     reads=[("ps", bank)], writes=[("ft", c % 2)])
                S.op("dve", lambda e, bank=bank, t0=t0: e.scalar_tensor_tensor(out=t0[:], in0=t0[:], scalar=1.0, in1=ps[:, bank, :], op0=ALU.add, op1=ALU.mult),
                     reads=[("ps", bank), ("ft", c % 2)], writes=[("ft", c % 2)])
                S.op("dve", lambda e, t0=t0, c=c: e.scalar_tensor_tensor(out=hg[:, c, :], in0=t0[:], scalar=0.5, in1=hg[:, c, :], op0=ALU.mult, op1=ALU.mult),
                     reads=[("ft", c % 2), ("hg", c)], writes=[("hg", c)])
            if dbg and m == 0:
                for c in range(16):
                    S.op("act", lambda e, c=c: e.activation(out=ft[2][:], in_=hg[:, c, :], func=AF.Copy), reads=[("hg", c)], writes=[("ft", 2)])
                    S.dma("sp", lambda e, c=c: e.dma_start(out=dbg_out["d_hg"][:, c * 512:(c + 1) * 512], in_=ft[2][:]), "dbg1", reads=[("ft", 2)], final=True)
            hgk = [("hg", c) for c in range(16)]
            for jc in range(16):
                ba = proj_fm(WB + jc, lambda k: hg[:, k, :], hgk, 512)
                bb = proj_fm(GB + jc, xT_rhs, ["xT"], 512)
                t0 = ft[jc % 2]
                S.op("act", lambda e, bb=bb, t0=t0: e.activation(out=t0[:], in_=ps[:, bb, :], func=AF.Tanh, scale=0.5),
                     reads=[("ps", bb)], writes=[("ft", jc % 2)])
                S.op("dve", lambda e, ba=ba, t0=t0: e.scalar_tensor_tensor(out=t0[:], in0=t0[:], scalar=1.0, in1=ps[:, ba, :], op0=ALU.add, op1=ALU.mult),
                     reads=[("ps", ba), ("ft", jc % 2)], writes=[("ft", jc % 2)])
                S.op("act", lambda e, t0=t0, jc=jc: e.activation(out=merged[:, jc, :], in_=t0[:], func=AF.Copy, scale=0.5),
                     reads=[("ft", jc % 2)], writes=[("mrg", jc)])

            stage('ownb')
            nkb_pair = 8 * m
            qv = qT.rearrange("p (c t) -> p c t", c=32)
            am = sm[:, 8:9]
            lo = sm[:, 9:10]
            mid = sm[:, 10:11]
            cnt = sm[:, 11:12]
            dl = sm[:, 12:13]
            wt = sm[:, 16:16 + NIT]

            def geom(qb):
                nkb = nkb_pair + 5 + qb
                return nkb, nkb * 128, (nkb + 3) // 4

            def proj_qi(hf):
                tsl = slice(hf * 256, (hf + 1) * 256)
                for c in range(8):
                    bank = proj_fm(QI + c, lambda k, tsl=tsl: xT[:, k, tsl], ["xT"], 256)
                    S.op("act", lambda e, bank=bank, c=c, hf=hf: e.activation(out=qiT[hf][:, c, :], in_=ps[:, bank, 0:256], func=AF.Copy),
                         reads=[("ps", bank)], writes=["qiT"])

            def proj_q(hf):
                tsl = slice(hf * 256, (hf + 1) * 256)
                for c in range(32):
                    bank = proj_fm(QL + c, lambda k, tsl=tsl: xT[:, k, tsl], ["xT"], 256)
                    if c % 2 == 0:
                        S.op("act", lambda e, bank=bank, c=c: e.activation(out=qT[:, c * 256:(c + 1) * 256], in_=ps[:, bank, 0:256], func=AF.Copy),
                             reads=[("ps", bank)], writes=["qT"], region=RA)
                    else:
                        S.op("dve", lambda e, bank=bank, c=c: e.tensor_copy(out=qT[:, c * 256:(c + 1) * 256], in_=ps[:, bank, 0:256]),
                             reads=[("ps", bank)], writes=["qT"], region=RA)

            def proj_widx(qb):
                s = wslot()
                bw = gp()
                for piece in range(4):
                    if piece > 0:
                        s = wslot()
                    S.dma("pool", lambda e, s=s, piece=piece: e.dma_start(max_dma_last_dim=4096, out=wst[s][:, 0:1600], in_=wtok_d[piece]), "w%d" % s, writes=[("wst", s)])

                    def f(e, s=s, piece=piece, qb=qb, bw=bw):
                        ins = None
                        for kk in range(4):
                            k = piece * 4 + kk
                            ins = e.matmul(ps[:, bw, 0:16], lhsT=xT[:, k, qb * 128:(qb + 1) * 128],
                                           rhs=wst[s][:, kk * 400 + 384:kk * 400 + 400], start=(k == 0), stop=(k == 15))
                        return ins
                    S.op("pe", f, reads=[("wst", s), "xT"], writes=[("ps", bw)])
                S.op("act", lambda e, bw=bw, qb=qb: e.activation(out=absw[:, qb, :], in_=ps[:, bw, 0:16], func=AF.Abs),
                     reads=[("ps", bw)], writes=[("absw", qb)])
                S.op("act", lambda e, bw=bw, qb=qb: e.activation(out=sgnw[:, qb, :], in_=ps[:, bw, 0:16], func=AF.Sign),
                     reads=[("ps", bw)], writes=[("sgnw", qb)])

            def do_idx(qb):
                nkb, nk, nkc = geom(qb)
                hf, q2 = qb // 2, qb % 2
                qsl = slice(q2 * 128, (q2 + 1) * 128)
                for kc in range(nkc):
                    w = min(512, nk - kc * 512)
                    kxs = kc % 2
                    S.dma("sp", lambda e, kc=kc, w=w, kxs=kxs: e.dma_start(out=kix[kxs][:, 0:w], in_=kixT_d[:, kc * 512:kc * 512 + w]), "kix%d" % kxs,
                          reads=[("kixT_d", kc)], writes=[("kix", kxs)])
                    accb = gp()
                    for h in range(16):
                        c = h // 2
                        po = (h % 2) * 64
                        zb = gq()
                        S.op("pe", lambda e, zb=zb, c=c, po=po, w=w, qsl=qsl, hf=hf, kxs=kxs: e.matmul(ps[:, zb, 0:w], lhsT=qiT[hf][po:po + 64, c, qsl],
                                                                                                     rhs=kix[kxs][po:po + 64, 0:w], start=True, stop=True),
                             reads=["qiT", ("kix", kxs)], writes=[("ps", zb)])
                        it = itmp[h % 2]
                        S.op("act", lambda e, zb=zb, it=it, h=h, w=w, qb=qb: e.activation(out=it[:, 0:w], in_=ps[:, zb, 0:w], func=AF.Relu,
                                                                                      scale=absw[:, qb, h:h + 1]),
                             reads=[("ps", zb), ("absw", qb)], writes=[("itmp", h % 2)])
                        if h == 0:
                            S.op("dve", lambda e, it=it, accb=accb, w=w, qb=qb: e.tensor_scalar(out=ps[:, accb, 0:w], in0=it[:, 0:w],
                                                                                            scalar1=sgnw[:, qb, 0:1], scalar2=None, op0=ALU.mult),
                                 reads=[("itmp", 0), ("sgnw", qb)], writes=[("ps", accb)])
                        else:
                            S.op("dve", lambda e, it=it, accb=accb, w=w, h=h, qb=qb: e.scalar_tensor_tensor(out=ps[:, accb, 0:w], in0=it[:, 0:w],
                                                                                                      scalar=sgnw[:, qb, h:h + 1], in1=ps[:, accb, 0:w],
                                                                                                      op0=ALU.mult, op1=ALU.add),
                                 reads=[("itmp", h % 2), ("sgnw", qb), ("ps", accb)], writes=[("ps", accb)])
                    S.op("act", lambda e, accb=accb, kc=kc, w=w: e.activation(out=scores[:, kc * 512:kc * 512 + w], in_=ps[:, accb, 0:w], func=AF.Copy),
                         reads=[("ps", accb)], writes=["scores"], region=RA)
                S.op("dve", lambda e, nk=nk: e.tensor_reduce(out=am, in_=scores[:, 0:nk], axis=AX.X, op=ALU.max, apply_absolute_value=True),
                     reads=["scores"], writes=["am"], region=RA)
                S.op("dve", lambda e: e.tensor_scalar(out=am, in0=am, scalar1=1.0, scalar2=None, op0=ALU.add), reads=["am"], writes=["am"])
                S.op("dve", lambda e: e.tensor_scalar(out=lo, in0=am, scalar1=-1.0, scalar2=None, op0=ALU.mult), reads=["am"], writes=["lo"])
                S.op("dve", lambda e: e.tensor_scalar(out=wt, in0=pow2[:], scalar1=am, scalar2=None, op0=ALU.mult),
                     reads=["am", "pow2"], writes=["wt"])
                S.dma("sp", lambda e, qb=qb: e.dma_start(out=cbt[:], in_=cbias_d[qb]), "cbt", writes=["cbt"])
                cw = (5 + qb) * 128
                S.op("dve", lambda e, cw=cw: e.tensor_tensor(out=scores[:, nkb_pair * 128:nkb_pair * 128 + cw],
                                                            in0=scores[:, nkb_pair * 128:nkb_pair * 128 + cw], in1=cbt[:, 0:cw], op=ALU.add),
                     reads=["scores", "cbt", "am"], writes=["scores"], region=RA)

            def do_bisect(qb, its):
                nkb, nk, nkc = geom(qb)
                mt = maskT[qb % 2]
                junk = mt[:].rearrange("p a b -> p (a b)")
                for it_ in its:
                    S.op("dve", lambda e, it_=it_: e.tensor_tensor(out=mid, in0=lo, in1=wt[:, it_:it_ + 1], op=ALU.add),
                         reads=["lo", "wt"], writes=["mid"])
                    S.op("dve", lambda e, nk=nk, junk=junk: e.tensor_scalar(out=junk[:, 0:nk], in0=scores[:, 0:nk], scalar1=mid, scalar2=None,
                                                                             op0=ALU.is_ge, op1=ALU.add, accum_out=cnt),
                         reads=["scores", "mid"], writes=[("maskT", qb % 2), "cnt"], region=RA)
                    S.op("dve", lambda e, it_=it_: e.scalar_tensor_tensor(out=dl, in0=cnt, scalar=TOPK - 0.5, in1=wt[:, it_:it_ + 1],
                                                                         op0=ALU.is_ge, op1=ALU.mult),
                         reads=["cnt", "wt"], writes=["dl"])
                    S.op("dve", lambda e: e.tensor_tensor(out=lo, in0=lo, in1=dl, op=ALU.add), reads=["lo", "dl"], writes=["lo"])

            def do_mask(qb):
                nkb, nk, nkc = geom(qb)
                mt = maskT[qb % 2]
                if dbg and m == 0:
                    S.dma("sp", lambda e, qb=qb: e.dma_start(out=dbg_out["d_thr"][:, qb:qb + 1], in_=lo, allow_slow_non_contiguous=True), "dbg2", reads=["lo"], final=True)
                    if qb == 3:
                        S.dma("sp", lambda e: e.dma_start(out=dbg_out["d_sc"], in_=scores[:, 0:1024]), "dbg3", reads=["scores"], region=RA, final=True)
                for kc in range(nkc):
                    w = min(512, nk - kc * 512)
                    nb = w // 128
                    S.op("dve", lambda e, kc=kc, w=w: e.tensor_scalar(out=mk[:, 0:w], in0=scores[:, kc * 512:kc * 512 + w], scalar1=lo, scalar2=None,
                                                                       op0=ALU.is_ge),
                         reads=["scores", "lo"], writes=["mk"], region=RA)
                    tbk = gp()
                    pv = ps[:, tbk, :].bitcast(BF16)

                    def ftr(e, pv=pv, nb=nb):
                        ins = None
                        for j in range(nb):
                            ins = e.transpose(out=pv[:, j * 128:(j + 1) * 128], in_=mk[:, j * 128:(j + 1) * 128], identity=ident[:])
                        return ins
                    S.op("pe", ftr, reads=["mk", "ident"], writes=[("ps", tbk)])
                    S.op("act", lambda e, kc=kc, w=w, nb=nb, pv=pv, mt=mt: e.activation(out=mt[:, kc * 4:kc * 4 + nb, :].rearrange("p a b -> p (a b)"),
                                                                                     in_=pv[:, 0:w], func=AF.Copy),
                         reads=[("ps", tbk)], writes=[("maskT", qb % 2)])

            def do_att(qb, hook=None):
                nkb, nk, nkc = geom(qb)
                q2 = qb % 2
                qsl = slice(q2 * 128, (q2 + 1) * 128)
                mt = maskT[qb % 2]
                meng = "pool" if hook is not None else "dve"
                for hq in range(4):
                    ws_ = hq % 2
                    S.dma("pool", lambda e, ws_=ws_, hq=hq: e.dma_start(max_dma_last_dim=4096, out=wuvb[ws_][:].rearrange("p a b -> p (a b)"),
                                                                         in_=wuv_d[:, hq * 1024:(hq + 1) * 1024]), "wuv%d" % ws_, writes=[("wuvb", ws_)])
                    if hook is not None:
                        hook(hq)
                    DEP = 2

                    def chunk_dma(kc):
                        w = min(512, nk - kc * 512)
                        nb = w // 128
                        cb_ = kc % 2
                        S.dma("sp", lambda e, kc=kc, nb=nb, cb_=cb_: e.dma_start(out=ckvc[cb_][:, 0:nb, :],
                                                                                  in_=ckv_d[kc * 4:kc * 4 + nb].rearrange("b s d -> s b d")),
                              "ckvc%d" % cb_, reads=[("ckv_d", kc)], writes=[("ckvc", cb_)])
                        S.dma("sp", lambda e, kc=kc, w=w, cb_=cb_: e.dma_start(out=ckvTc[cb_][:, :, 0:w], in_=ckvT_d[:, :, kc * 512:kc * 512 + w]),
                              "ckvTc%d" % cb_, reads=[("ckvT_d", kc)], writes=[("ckvTc", cb_)])

                    chunk_dma(0)
                    if nkc > 1:
                        chunk_dma(1)
                    for idx_ in range(nkb + DEP):
                        if idx_ < nkb:
                            kb = idx_
                            kc, j = kb // 4, kb % 4
                            cb_ = kc % 2
                            rel = kb - nkb_pair
                            slot = {qb - 1: 0, qb: 1, qb + 3: 2, qb + 4: 3}.get(rel)
                            qkb = 3 + (kb % 3)
                            pp = kb % 3

                            def fqk(e, j=j, qkb=qkb, slot=slot, hq=hq, qsl=qsl, cb_=cb_):
                                ins = None
                                for k in range(2):
                                    ins = e.matmul(ps[:, qkb, :], lhsT=ckvTc[cb_][:, k, j * 128:(j + 1) * 128],
                                                   rhs=qv[:, hq * 8 + k:hq * 8 + 8:2, qsl], start=(k == 0), stop=(k == 1 and slot is None))
                                if slot is not None:
                                    ins = e.matmul(ps[:, qkb, :], lhsT=ident[:], rhs=biasS[:, slot, hq * 4:hq * 4 + 4, :], start=False, stop=True)
                                return ins
                            S.op("pe", fqk, reads=[("ckvTc", cb_), "qT", "ident", "biasS"], writes=[("ps", qkb)], region=RA)
                            S.op("act", lambda e, qkb=qkb, pp=pp: e.activation(out=Pb[pp][:], in_=ps[:, qkb, :], func=AF.Exp, scale=1.0 / 16.0),
                                 reads=[("ps", qkb)], writes=[("Pb", pp)])
                            S.op(meng, lambda e, pp=pp, kb=kb, mt=mt: e.tensor_tensor(out=Pm[pp][:], in0=Pb[pp][:].rearrange("p (a b) -> p a b", a=4),
                                                                                   in1=mt[:, kb:kb + 1, :].to_broadcast([128, 4, 128]), op=ALU.mult),
                                 reads=[("Pb", pp), ("maskT", qb % 2)], writes=[("Pm", pp)])
                        if idx_ >= DEP:
                            kb = idx_ - DEP
                            kc, j = kb // 4, kb % 4
                            cb_ = kc % 2
                            pp = kb % 3

                            def fpv(e, j=j, pp=pp, kb=kb, nkb=nkb, cb_=cb_):
                                rhs = Pm[pp][:].rearrange("p a b -> p (a b)")
                                e.matmul(ps[:, 0, :], lhsT=ckvc[cb_][:, j, 0:128], rhs=rhs, start=(kb == 0), stop=(kb == nkb - 1))
                                e.matmul(ps[:, 1, :], lhsT=ckvc[cb_][:, j, 128:256], rhs=rhs, start=(kb == 0), stop=(kb == nkb - 1))
                                return e.matmul(ps[:, 2, :], lhsT=ones[:], rhs=rhs, start=(kb == 0), stop=(kb == nkb - 1))
                            S.op("pe", fpv, reads=[("ckvc", cb_), ("Pm", pp), "ones"], writes=[("ps", 0), ("ps", 1), ("ps", 2)])
                            if j == 3 and kc + 2 < nkc:
                                chunk_dma(kc + 2)
                    S.op("dve", lambda e: e.reciprocal(out=rden[:], in_=ps[:, 2, :]), reads=[("ps", 2)], writes=["rden"])
                    for k in range(2):
                        S.op("dve", lambda e, k=k: e.tensor_tensor(out=onT[:, k, :], in0=ps[:, k, :], in1=rden[:], op=ALU.mult),
                             reads=[("ps", k), "rden"], writes=["onT"])
                    ub = gp()

                    def fuv(e, ub=ub, ws_=ws_):
                        ins = None
                        for hh_ in range(4):
                            for k in range(2):
                                ins = e.matmul(ps[:, ub, hh_ * 128:(hh_ + 1) * 128], lhsT=wuvb[ws_][:, hh_ * 2 + k, :],
                                               rhs=onT[:, k, hh_ * 128:(hh_ + 1) * 128], start=(k == 0), stop=(k == 1))
                        return ins
                    S.op("pe", fuv, reads=["onT", ("wuvb", ws_)], writes=[("ps", ub)])
                    S.op("act", lambda e, ub=ub, hq=hq, qb=qb: e.activation(out=hg[:, hq * 4:hq * 4 + 4, qb * 128:(qb + 1) * 128],
                                                                          in_=ps[:, ub, :].rearrange("p (a b) -> p a b", a=4), func=AF.Copy),
                         reads=[("ps", ub)], writes=[("hg", hq * 4 + x) for x in range(4)])

            def bis_hook(qb):
                per = (NIT + 3) // 4

                def hk(hq):
                    do_bisect(qb, range(hq * per, min(NIT, (hq + 1) * per)))
                return hk

            proj_qi(0)
            proj_q(0)
            proj_widx(0)
            proj_widx(1)
            do_idx(0)
            do_bisect(0, range(NIT))
            do_mask(0)
            do_idx(1)
            do_att(0, bis_hook(1))
            do_mask(1)
            proj_qi(1)
            proj_widx(2)
            proj_widx(3)
            do_idx(2)
            do_att(1, bis_hook(2))
            do_mask(2)
            proj_q(1)
            do_idx(3)
            do_att(2, bis_hook(3))
            do_mask(3)
            do_att(3)
            stage('attn')
            for c in range(16):
                bank = proj_fm(AG + c, xT_rhs, ["xT"], 512)
                t0 = ft[c % 2]
                S.op("act", lambda e, bank=bank, t0=t0: e.activation(out=t0[:], in_=ps[:, bank, :], func=AF.Tanh, scale=0.5),
                     reads=[("ps", bank)], writes=[("ft", c % 2)])
                S.op("dve", lambda e, bank=bank, t0=t0: e.scalar_tensor_tensor(out=t0[:], in0=t0[:], scalar=1.0, in1=ps[:, bank, :], op0=ALU.add, op1=ALU.mult),
                     reads=[("ps", bank), ("ft", c % 2)], writes=[("ft", c % 2)])
                S.op("dve", lambda e, t0=t0, c=c: e.scalar_tensor_tensor(out=hg[:, c, :], in0=t0[:], scalar=0.5, in1=hg[:, c, :], op0=ALU.mult, op1=ALU.mult),
                     reads=[("ft", c % 2), ("hg", c)], writes=[("hg", c)])
            if dbg and m == 0:
                for c in range(16):
                    S.op("act", lambda e, c=c: e.activation(out=ft[2][:], in_=hg[:, c, :], func=AF.Copy), reads=[("hg", c)], writes=[("ft", 2)])
                    S.dma("sp", lambda e, c=c: e.dma_start(out=dbg_out["d_ga"][:, c * 512:(c + 1) * 512], in_=ft[2][:]), "dbg4", reads=[("ft", 2)], final=True)
            stage('gate')
            for jc in range(16):
                ba = proj_fm(WA + jc, lambda k: hg[:, k, :], hgk, 512)
                bb = proj_fm(GA + jc, xT_rhs, ["xT"], 512)
                t0 = ft[jc % 2]
                S.op("act", lambda e, bb=bb, t0=t0: e.activation(out=t0[:], in_=ps[:, bb, :], func=AF.Tanh, scale=0.5),
                     reads=[("ps", bb)], writes=[("ft", jc % 2)])
                S.op("dve", lambda e, ba=ba, t0=t0: e.scalar_tensor_tensor(out=t0[:], in0=t0[:], scalar=1.0, in1=ps[:, ba, :], op0=ALU.add, op1=ALU.mult),
                     reads=[("ps", ba), ("ft", jc % 2)], writes=[("ft", jc % 2)])
                S.op("dve", lambda e, t0=t0, jc=jc: e.scalar_tensor_tensor(out=merged[:, jc, :], in0=t0[:], scalar=0.5, in1=merged[:, jc, :], op0=ALU.mult, op1=ALU.add),
                     reads=[("ft", jc % 2), ("mrg", jc)], writes=[("mrg", jc)])
            if dbg and m == 0:
                for c in range(16):
                    S.op("act", lambda e, c=c: e.activation(out=ft[2][:], in_=merged[:, c, :], func=AF.Copy), reads=[("mrg", c)], writes=[("ft", 2)])
                    S.dma("sp", lambda e, c=c: e.dma_start(out=dbg_out["d_mrg"][:, c * 512:(c + 1) * 512], in_=ft[2][:]), "dbg5", reads=[("ft", 2)], final=True)
            stage('brancha')
            mk_ = [("mrg", c) for c in range(16)]
            S.dma("sp", lambda e: e.dma_start(out=lng, in_=lng_d), "lng", writes=["lng"], region=RO)
            S.dma("sp", lambda e: e.dma_start(out=lnb, in_=lnb_d), "lnb", writes=["lnb"], region=RO)
            for tb in range(4):
                S.dma("sp", lambda e, tb=tb: e.dma_start(out=ysub[:, tb * 2048:(tb + 1) * 2048], in_=xtok_d[m, tb]), "xtk%d" % tb,
                      writes=[("ysub", tb)], region=RO)
            for jj in range(8):
                wsl = jj % 2
                S.dma("pool", lambda e, wsl=wsl, jj=jj: e.dma_start(out=wo[wsl], in_=wout_b[jj]), "wo%d" % wsl, reads=[("wob", jj)], writes=[("wo", wsl)], region=RO)
                for tb in range(4):
                    ob = gp()

                    def fo(e, ob=ob, wsl=wsl, tb=tb):
                        ins = None
                        for k in range(16):
                            ins = e.matmul(ps[:, ob, 0:256], lhsT=merged[:, k, tb * 128:(tb + 1) * 128], rhs=wo[wsl][:, k * 256:(k + 1) * 256],
                                           start=(k == 0), stop=(k == 15))
                        return ins
                    S.op("pe", fo, reads=mk_ + [("wo", wsl)], writes=[("ps", ob)], region=RO)
                    ys = ysub[:, tb * 2048 + jj * 256: tb * 2048 + (jj + 1) * 256]
                    S.op("dve", lambda e, ob=ob, ys=ys: e.scalar_tensor_tensor(out=ys, in0=ys, scalar=ALPHA, in1=ps[:, ob, 0:256], op0=ALU.mult, op1=ALU.add),
                         reads=[("ps", ob), ("ysub", tb)], writes=[("ysub", tb)], region=RO)
            for tb in range(4):
                yv = ysub[:, tb * 2048:(tb + 1) * 2048]
                mv = sm[:, 40:42]
                rs = sm[:, 42:43]
                for q in range(4):
                    S.op("dve", lambda e, yv=yv, q=q: e.bn_stats(out=bst[:, q, :], in_=yv[:, q * 512:(q + 1) * 512]),
                         reads=[("ysub", tb)], writes=[("bst", q)], region=RO)
                S.op("dve", lambda e, mv=mv: e.bn_aggr(out=mv, in_=bst[:].rearrange("p a b -> p (a b)")),
                     reads=[("bst", q) for q in range(4)], writes=["mv"])
                S.op("dve", lambda e, mv=mv, rs=rs: e.tensor_scalar(out=rs, in0=mv[:, 1:2], scalar1=EPS, scalar2=None, op0=ALU.add), reads=["mv"], writes=["rs"])
                S.op("pool", lambda e, rs=rs: e.tensor_tensor(out=rs, in0=rs, in1=mhalf[:], op=ALU.pow), reads=["rs", "mhalf"], writes=["rs"])
                S.op("dve", lambda e, yv=yv, mv=mv, rs=rs: e.tensor_scalar(out=yv, in0=yv, scalar1=mv[:, 0:1], scalar2=rs, op0=ALU.subtract, op1=ALU.mult),
                     reads=[("ysub", tb), "mv", "rs"], writes=[("ysub", tb)], region=RO)
                S.op("dve", lambda e, yv=yv: e.tensor_tensor(out=yv, in0=yv, in1=lng, op=ALU.mult), reads=[("ysub", tb), "lng"], writes=[("ysub", tb)], region=RO)
                S.op("dve", lambda e, yv=yv: e.tensor_tensor(out=yv, in0=yv, in1=lnb, op=ALU.add), reads=[("ysub", tb), "lnb"], writes=[("ysub", tb)], region=RO)
                S.dma("sp", lambda e, tb=tb, yv=yv: e.dma_start(out=y_d[m, tb], in_=yv), "yo%d" % tb, reads=[("ysub", tb)], region=RO, final=True)

        try:
            stage('init')
            for m in range(n_pairs):
                fullseq(2 * m, 0, m)
                fullseq(2 * m + 1, 1, m)
                own(m)
        except _Stop:
            pass
        S.emit()
    return nc


def _t5_bucket_np(n):
    n = np.maximum(n, 0)
    nf = np.maximum(n, 1).astype(np.float32)
    large = 16 + (np.log(nf / np.float32(16)) / np.float32(math.log(128 / 16)) * np.float32(16)).astype(np.int32)
    large = np.minimum(large, 31)
    return np.where(n < 16, n, large)


def _fm(w):
    n = w.shape[1] // 128
    return np.ascontiguousarray(w.reshape(16, 128, n, 128).transpose(2, 1, 0, 3)).reshape(n, 128, 16 * 128)


def _vec(v):
    return np.ascontiguousarray(v.reshape(16, 128).T)


def make_inputs(x, w_in, kv_norm_g, w_uv, w_branch_a, conv_w, conv_b, w_gate_a, b_gate_a,
                w_gate_x, b_gate_x, lru_lambda, w_branch_b, rel_bias, w_out, ln_g, ln_b):
    w_in = w_in[0]
    chunks = []
    for base, n in ((QL, 32), (AG, 16), (QI, 8), (XR, 16), (RG, 16), (GA, 16), (GB, 16)):
        c0 = COL[base]
        chunks.append(_fm(w_in[:, c0:c0 + n * 128]))
    chunks.append(_fm(w_branch_a[0]))
    chunks.append(_fm(w_branch_b[0]))
    wfm = np.concatenate(chunks, axis=0)
    tokcols = np.concatenate([np.arange(4096, 4352), np.arange(7424, 7488), np.arange(7424, 7488), np.arange(7488, 7504)])
    wt = w_in[:, tokcols]
    wtok = np.ascontiguousarray(wt.reshape(4, 4, 128, 400).transpose(0, 2, 1, 3)).reshape(4, 128, 1600)
    wout = np.ascontiguousarray(w_out[0].reshape(16, 128, 8, 256).transpose(2, 1, 0, 3)).reshape(8, 128, 16 * 256)
    wuv = np.ascontiguousarray(w_uv[0].reshape(16, 2, 128, 128).transpose(2, 0, 1, 3)).reshape(128, 32 * 128)
    wg = np.ascontiguousarray(np.stack([w_gate_a[0], w_gate_x[0]], 0).transpose(2, 0, 1, 3)).reshape(128, 32 * 128)
    cvec = np.stack([_vec(conv_w[0, 0]), _vec(conv_w[0, 1]), _vec(conv_w[0, 2]), _vec(conv_w[0, 3]), _vec(conv_b[0]),
                     _vec(b_gate_a[0]), _vec(b_gate_x[0]), _vec(lru_lambda[0])], axis=1).reshape(128, 128)
    kvg = np.ascontiguousarray(np.broadcast_to(kv_norm_g[0][None, :], (128, 256)))
    lng = np.ascontiguousarray(np.broadcast_to(ln_g[0][None, :], (128, D)))
    lnb = np.ascontiguousarray(np.broadcast_to(ln_b[0][None, :], (128, D)))
    s_ = np.arange(128)[:, None]
    t_ = np.arange(128)[None, :]
    diag = rel_bias[_t5_bucket_np(t_ - s_)]
    prev = rel_bias[_t5_bucket_np(128 + t_ - s_)]
    diag = np.ascontiguousarray(diag.transpose(0, 2, 1))
    prev = np.ascontiguousarray(prev.transpose(0, 2, 1))
    b31 = np.ascontiguousarray(np.broadcast_to(rel_bias[31][None, :, None], (128, 16, 128)))
    pow2 = np.ascontiguousarray(np.broadcast_to((2.0 ** -np.arange(NIT, dtype=np.float64)).astype(np.float32)[None, :], (128, NIT)))
    shared = dict(wfm=wfm, wtok=wtok, wout=wout, wuv=wuv, wg=wg, cvec=np.ascontiguousarray(cvec), kvg=kvg, lng=lng, lnb=lnb,
                  b31=b31.reshape(128, 2048), pow2=pow2)
    in_maps = []
    for core in range(8):
        b, par = core // 2, core % 2
        xb = x[b]
        xT = np.ascontiguousarray(xb.reshape(NT, 512, 16, 128).transpose(0, 3, 2, 1)).reshape(NT, 128, 16 * 512)
        xTo = np.ascontiguousarray(xT[par::2])
        xtok = np.ascontiguousarray(xb.reshape(NP, 2, 4, 128, D)[:, par])
        sel = np.zeros((128, 2), np.float32)
        sel[:, par] = 1.0
        cb = np.zeros((4, 128, 1024), np.float32)
        for qb in range(4):
            tg = par * 512 + qb * 128 + np.arange(128)[:, None]
            sg = np.arange(1024)[None, :]
            cb[qb] = np.where(sg <= tg, 0.0, NEG)
        if par == 0:
            slots = [prev, diag, b31, b31]
        else:
            slots = [b31, b31, prev, diag]
        biasS = np.ascontiguousarray(np.stack(slots, axis=1)).reshape(128, 4 * 16 * 128).astype(np.float32)
        d = dict(shared)
        d.update(xT=xT, xTo=xTo, xtok=xtok, sel=sel, cbias=cb, biasS=biasS)
        in_maps.append(d)
    return in_maps


def kernel(**inputs):
    inputs = {k: np.asarray(v) for k, v in inputs.items()}
    in_maps = make_inputs(**inputs)
    nc = build_nc()
    res = run_bass_kernel_spmd(nc, in_maps, core_ids=list(range(8)))
    out = np.empty((4, T, D), np.float32)
    for core in range(8):
        b, par = core // 2, core % 2
        yv = res.results[core]["y"].reshape(NP, 512, D)
        out.reshape(4, NP, 2, 512, D)[b, :, par] = yv
    return out
```

```python
import math
from contextlib import ExitStack

import numpy as np
import concourse.bass as bass
import concourse.mybir as mybir
from concourse.bass_utils import run_bass_kernel_spmd

F32 = mybir.dt.float32
BF16 = mybir.dt.bfloat16
U8 = mybir.dt.uint8
AF = mybir.ActivationFunctionType
ALU = mybir.AluOpType
AX = mybir.AxisListType

D = 2048
T = 8192
NT = 16
NP = 8
TOPK = 256
NIT = 20
ALPHA = 2.0 ** 0.25
EPS = 1e-5
NEG = -1.0e30
QL, AG, QI, XR, RG, GA, GB, WA, WB = 0, 32, 48, 56, 72, 88, 104, 120, 136
NCH = 152
COL = {QL: 0, AG: 4352, QI: 6400, XR: 7504, RG: 9552, GA: 11600, GB: 13648}


class _Op:
    __slots__ = ("eng", "fn", "waits", "idx", "inc", "semval", "dma_sem", "dma_val")

    def __init__(self, eng, fn):
        self.eng = eng
        self.fn = fn
        self.waits = []
        self.idx = -1
        self.inc = False
        self.semval = 0
        self.dma_sem = None
        self.dma_val = 0


class Sched:
    ENG = ("pe", "act", "dve", "pool", "sp")
    SAME = ("act", "dve", "pool")

    def __init__(self, nc, stack):
        self.nc = nc
        self.stack = stack
        self.ops = {e: [] for e in self.ENG}
        self.last_w = {}
        self.readers = {}
        self.dma_sems = {}
        self.final_tokens = []
        self.reg_mode = {}
        self.reg_cur = {}
        self.reg_fence = {}

    def _dma_sem(self, name):
        if name not in self.dma_sems:
            h = self.stack.enter_context(self.nc.semaphore("d_" + name))
            self.dma_sems[name] = [h, 0]
        return self.dma_sems[name]

    @staticmethod
    def _compress(toks):
        be = {}
        bd = {}
        for t in toks:
            if t[0] == "e":
                p = t[1]
                if p.eng not in be or be[p.eng].idx < p.idx:
                    be[p.eng] = p
            else:
                if bd.get(t[1], 0) < t[2]:
                    bd[t[1]] = t[2]
        return [("e", p) for p in be.values()] + [("d", k, v) for k, v in bd.items()]

    def _deps(self, reads, writes, region):
        toks = []
        for k in reads:
            t = self.last_w.get(k)
            if t is not None:
                toks.append(t)
        for k in writes:
            t = self.last_w.get(k)
            if t is not None:
                toks.append(t)
            toks.extend(self.readers.get(k, ()))
        if region is not None:
            r, mode = region
            if self.reg_mode.get(r) != mode:
                self.reg_fence[r] = self._compress(self.reg_cur.get(r, []) + self.reg_fence.get(r, []))
                self.reg_cur[r] = []
                self.reg_mode[r] = mode
            toks.extend(self.reg_fence.get(r, ()))
        return self._compress(toks)

    def _commit(self, tok, reads, writes, region):
        for k in reads:
            lst = self.readers.setdefault(k, [])
            lst.append(tok)
            if len(lst) > 12:
                self.readers[k] = self._compress(lst)
        for k in writes:
            self.last_w[k] = tok
            self.readers[k] = []
        if region is not None:
            lst = self.reg_cur.setdefault(region[0], [])
            lst.append(tok)
            if len(lst) > 12:
                self.reg_cur[region[0]] = self._compress(lst)

    @staticmethod
    def _excl(reads, writes):
        r = [k for k in reads if not (isinstance(k, tuple) and k[0] == "ps")]
        w = list(writes) + [k for k in reads if isinstance(k, tuple) and k[0] == "ps" and k not in writes]
        return r, w

    def op(self, eng, fn, reads=(), writes=(), region=None):
        reads, writes = self._excl(reads, writes)
        o = _Op(eng, fn)
        o.idx = len(self.ops[eng])
        o.waits = self._deps(reads, writes, region)
        self.ops[eng].append(o)
        self._commit(("e", o), reads, writes, region)
        return o

    def dma(self, eng, fn, sem, reads=(), writes=(), region=None, final=False):
        o = _Op(eng, fn)
        o.idx = len(self.ops[eng])
        o.waits = self._deps(reads, writes, region)
        s = self._dma_sem(sem)
        s[1] += 16
        o.dma_sem = sem
        o.dma_val = s[1]
        self.ops[eng].append(o)
        tok = ("d", sem, s[1])
        self._commit(tok, reads, writes, region)
        if final:
            self.final_tokens.append(tok)
        return o

    def dma_group(self, eng, fns, sem, keys):
        s = self._dma_sem(sem)
        for fn in fns:
            o = _Op(eng, fn)
            o.idx = len(self.ops[eng])
            o.waits = []
            s[1] += 16
            o.dma_sem = sem
            o.dma_val = s[1]
            self.ops[eng].append(o)
        tok = ("d", sem, s[1])
        for k in keys:
            self.last_w[k] = tok
            self.readers[k] = []

    def emit(self, final_eng="sp"):
        nc = self.nc
        fo = _Op(final_eng, None)
        fo.idx = len(self.ops[final_eng])
        fo.waits = list(self.final_tokens) + [("d", k, v[1]) for k, v in self.dma_sems.items()]
        for e in self.ENG:
            if e != final_eng and self.ops[e]:
                fo.waits.append(("e", self.ops[e][-1]))
        self.ops[final_eng].append(fo)
        for e in self.ENG:
            for o in self.ops[e]:
                for t in o.waits:
                    if t[0] == "e":
                        p = t[1]
                        if p.eng != o.eng or o.eng in self.SAME:
                            p.inc = True
        esem = {}
        for e in self.ENG:
            esem[e] = self.stack.enter_context(nc.semaphore("e_" + e))
            c = 0
            for o in self.ops[e]:
                if o.inc:
                    c += 1
                    o.semval = c
        block = self.stack.enter_context(nc.Block())

        def run(e, eng):
            seen_e = {x: -1 for x in self.ENG}
            seen_d = {}
            for o in self.ops[e]:
                for t in o.waits:
                    if t[0] == "e":
                        p = t[1]
                        if p.eng == e and e not in self.SAME:
                            continue
                        if p.idx > seen_e[p.eng]:
                            eng.wait_ge(esem[p.eng], p.semval)
                            seen_e[p.eng] = p.idx
                    else:
                        if seen_d.get(t[1], 0) < t[2]:
                            eng.wait_ge(self.dma_sems[t[1]][0], t[2])
                            seen_d[t[1]] = t[2]
                if o.fn is None:
                    continue
                ins = o.fn(eng)
                if o.dma_sem is not None:
                    ins.then_inc(self.dma_sems[o.dma_sem][0], 16)
                elif o.inc:
                    ins.then_inc(esem[e], 1)

        @block.tensor
        def _(eng):
            run("pe", eng)

        @block.scalar
        def _(eng):
            run("act", eng)

        @block.vector
        def _(eng):
            run("dve", eng)

        @block.gpsimd
        def _(eng):
            run("pool", eng)

        @block.sync
        def _(eng):
            run("sp", eng)


class _Stop(Exception):
    pass


def build_nc(n_pairs=NP, dbg=False, stop=None):
    def stage(name):
        if stop is not None and name == stop:
            raise _Stop()

    nc = bass.Bass("TRN2", target_bir_lowering=False)

    def din(name, shape, dt=F32):
        return nc.dram_tensor(name, list(shape), dt, kind="ExternalInput").ap()

    xT_d = din("xT", [NT, 128, 16 * 512])
    xTo_d = din("xTo", [NP, 128, 16 * 512])
    xtok_d = din("xtok", [NP, 4, 128, D])
    wfm_d = din("wfm", [NCH, 128, 16 * 128])
    wtok_d = din("wtok", [4, 128, 4 * 400])
    wout_d = din("wout", [8, 128, 16 * 256])
    wuv_d = din("wuv", [128, 32 * 128])
    wg_d = din("wg", [128, 32 * 128])
    cvec_d = din("cvec", [128, 8 * 16])
    kvg_d = din("kvg", [128, 256])
    lng_d = din("lng", [128, D])
    lnb_d = din("lnb", [128, D])
    biasS_d = din("biasS", [128, 4 * 16 * 128])
    b31_d = din("b31", [128, 16 * 128])
    sel_d = din("sel", [128, 2])
    cbias_d = din("cbias", [4, 128, 1024])
    pow2_d = din("pow2", [128, NIT])
    y_d = nc.dram_tensor("y", [NP, 4, 128, D], F32, kind="ExternalOutput").ap()
    ckv_d = nc.dram_tensor("ckv_s", [64, 128, 256], BF16, kind="Internal").ap()
    ckvT_d = nc.dram_tensor("ckvT_s", [128, 2, T], BF16, kind="Internal").ap()
    kixT_d = nc.dram_tensor("kixT_s", [128, T], BF16, kind="Internal").ap()
    wfm_b = nc.dram_tensor("wfm_bf", [NCH, 128, 16 * 128], BF16, kind="Internal").ap()
    wout_b = nc.dram_tensor("wout_bf", [8, 128, 16 * 256], BF16, kind="Internal").ap()
    dbg_out = {}
    if dbg:
        for nm, shp in (("d_ckv", [128, 256]), ("d_sc", [128, 1024]), ("d_thr", [128, 4]),
                        ("d_hg", [128, 16 * 512]), ("d_mrg", [128, 16 * 512]), ("d_ga", [128, 16 * 512])):
            dbg_out[nm] = nc.dram_tensor(nm, shp, F32, kind="ExternalOutput").ap()

    st = ExitStack()
    with st:
        def sb(name, shape, dt):
            return st.enter_context(nc.sbuf_tensor("s_" + name, list(shape), dt))

        S = Sched(nc, st)
        biasS = sb("biasS", [128, 4, 16, 128], BF16)
        cvec = sb("cvec", [128, 8, 16], F32)
        clam = sb("clam", [128, 16], F32)
        hclam = sb("hclam", [128, 16], F32)
        hba = sb("hba", [128, 16], F32)
        hbx = sb("hbx", [128, 16], F32)
        kvg = sb("kvg", [128, 256], F32)
        selt = sb("selt", [128, 2], F32)
        pow2 = sb("pow2", [128, NIT], F32)
        mhalf = sb("mhalf", [128, 1], F32)
        ident = sb("ident", [128, 128], BF16)
        ones = sb("ones", [128, 128], BF16)
        tail = sb("tail", [128, 16, 3], F32)
        hcar = sb("hcar", [128, 16], F32)
        xT = sb("xT", [128, 16, 512], BF16)
        NW = 2 if dbg else 3
        wst = [sb("wst%d" % i, [128, 16 * 128], BF16) for i in range(NW)]
        wgb = [sb("wgb%d" % i, [128, 2, 128], BF16) for i in range(2)]
        wuvb = [sb("wuvb%d" % i, [128, 8, 128], BF16) for i in range(2)]
        merged = sb("merged", [128, 16, 512], BF16)
        hg = sb("hg", [128, 16, 512], BF16)
        maskT = [sb("maskT%d" % i, [128, 64, 128], U8) for i in range(2)]
        _qi0 = sb("qiT0", [128, 8, 256], BF16)
        qiT = [_qi0, _qi0]
        REG = sb("REG", [128, 16384], F32)
        kix = [sb("kix%d" % i, [128, 512], BF16) for i in range(2)]
        ckvc = [sb("ckvc%d" % i, [128, 4, 256], BF16) for i in range(2)]
        ckvTc = [sb("ckvTc%d" % i, [128, 2, 512], BF16) for i in range(2)]
        Pb = [sb("Pb%d" % i, [128, 512], BF16) for i in range(3)]
        Pm = [sb("Pm%d" % i, [128, 4, 128], BF16) for i in range(3)]
        onT = sb("onT", [128, 2, 512], BF16)
        rden = sb("rden", [128, 512], F32)
        itmp = [sb("itmp%d" % i, [128, 512], F32) for i in range(2)]
        mk = sb("mk", [128, 512], BF16)
        ckv_tok = [sb("ckvtok%d" % i, [128, 256], BF16) for i in range(2)]
        kix_tok = [sb("kixtok%d" % i, [128, 128], BF16) for i in range(2)]
        ckvT_st = sb("ckvT_st", [128, 2, 512], BF16)
        kixT_st = sb("kixT_st", [128, 512], BF16)
        absw = sb("absw", [128, 4, 16], F32)
        sgnw = sb("sgnw", [128, 4, 16], F32)
        sm = sb("sm", [128, 64], F32)
        ft = [sb("ft%d" % i, [128, 512], F32) for i in range(3 if dbg else 2)]
        cbt = sb("cbt", [128, 1024], F32)
        bst = sb("bst", [128, 4, 6], F32)
        ps = st.enter_context(nc.psum_tensor("ps", [128, 8, 512], F32))

        scores = REG[:, 0:8192]
        qT = REG[:, 8192:12288].bitcast(BF16)
        ysub = REG[:, 0:8192]
        wo = [REG[:, 8192 + i * 2048: 8192 + (i + 1) * 2048].bitcast(BF16) for i in range(2)]
        lng = REG[:, 12288:14336]
        lnb = REG[:, 14336:16384]

        def rv(i):
            return REG[:, i * 1024:(i + 1) * 1024]

        gpc = [0]

        def gp():
            b = 6 + gpc[0] % 2
            gpc[0] += 1
            return b

        gqc = [0]

        def gq():
            b = 3 + gqc[0] % 3
            gqc[0] += 1
            return b

        wc = [0]

        def wslot():
            s = wc[0] % NW
            wc[0] += 1
            return s

        def psb(bank):
            return ps[:, bank, :]

        S.dma("sp", lambda e: e.dma_start(out=cvec[:].rearrange("p a b -> p (a b)"), in_=cvec_d), "c0", writes=["cvec"])
        S.dma("sp", lambda e: e.dma_start(out=kvg[:], in_=kvg_d), "c1", writes=["kvg"])
        S.dma("sp", lambda e: e.dma_start(out=selt[:], in_=sel_d), "c2", writes=["selt"])
        S.dma("sp", lambda e: e.dma_start(out=pow2[:], in_=pow2_d), "c3", writes=["pow2"])
        S.dma("sp", lambda e: e.dma_start(out=REG[:, 0:8192], in_=biasS_d), "c4", writes=["ibias"], region=("R", "init"))
        S.dma("sp", lambda e: e.dma_start(out=REG[:, 8192:10240], in_=b31_d), "c5", writes=["ib31"], region=("R", "init"))
        for s_ in range(4):
            S.op("dve", lambda e, s_=s_: e.tensor_tensor(out=REG[:, s_ * 2048:(s_ + 1) * 2048], in0=REG[:, s_ * 2048:(s_ + 1) * 2048],
                                                          in1=REG[:, 8192:10240], op=ALU.subtract),
                 reads=["ib31", "ibias"], writes=[("ibs", s_)], region=("R", "init"))
            S.op("dve", lambda e, s_=s_: e.tensor_scalar(out=biasS[:, s_, :, :].rearrange("p a b -> p (a b)"),
                                                          in0=REG[:, s_ * 2048:(s_ + 1) * 2048], scalar1=16.0, scalar2=None, op0=ALU.mult),
                 reads=[("ibs", s_)], writes=["biasS"], region=("R", "init"))
        S.op("pool", lambda e: e.memset(ones[:], 1.0), writes=["ones"])
        S.op("pool", lambda e: e.memset(mhalf[:], -0.5), writes=["mhalf"])
        S.op("pool", lambda e: e.memset(tail[:], 0.0), writes=["tail"])
        S.op("pool", lambda e: e.memset(hcar[:], 0.0), writes=["hcar"])
        S.op("pool", lambda e: e.affine_select(out=ident[:], in_=ones[:], pattern=[[-1, 128]], compare_op=ALU.is_equal,
                                                fill=0.0, base=0, channel_multiplier=1),
             reads=["ones"], writes=["ident"])
        S.op("act", lambda e: e.activation(out=clam[:], in_=cvec[:, 7, :], func=AF.Exp, scale=-1.0), reads=["cvec"], writes=["clam"])
        S.op("act", lambda e: e.activation(out=clam[:], in_=clam[:], func=AF.Ln, bias=1.0, scale=1.0), reads=["clam"], writes=["clam"])
        S.op("dve", lambda e: e.tensor_scalar(out=clam[:], in0=clam[:], scalar1=-8.0, scalar2=None, op0=ALU.mult), reads=["clam"], writes=["clam"])
        S.op("dve", lambda e: e.tensor_scalar(out=hclam[:], in0=clam[:], scalar1=0.5, scalar2=None, op0=ALU.mult), reads=["clam"], writes=["hclam"])
        S.op("dve", lambda e: e.tensor_scalar(out=hba[:], in0=cvec[:, 5, :], scalar1=0.5, scalar2=None, op0=ALU.mult), reads=["cvec"], writes=["hba"])
        S.op("dve", lambda e: e.tensor_scalar(out=hbx[:], in0=cvec[:, 6, :], scalar1=0.5, scalar2=None, op0=ALU.mult), reads=["cvec"], writes=["hbx"])

        def convert(groups):
            for gname, base, n in groups:
                S.dma_group("pool", [(lambda e, c=c: e.dma_start(max_dma_last_dim=4096, out=wfm_b[c], in_=wfm_d[c])) for c in range(base, base + n)],
                            "cv" + gname, [("wb", c) for c in range(base, base + n)])

        def convert_rest():
            convert((("RG", RG, 16), ("WB", WB, 16), ("GB", GB, 16), ("QI", QI, 8), ("QL", QL, 32), ("AG", AG, 16), ("WA", WA, 16), ("GA", GA, 16)))
            S.dma_group("pool", [(lambda e, c=c: e.dma_start(max_dma_last_dim=4096, out=wout_b[c], in_=wout_d[c])) for c in range(8)],
                        "cvWO", [("wob", c) for c in range(8)])

        convert((("XR", XR, 16),))

        def load_w(chunk):
            s = wslot()
            S.dma("pool", lambda e: e.dma_start(out=wst[s][:], in_=wfm_b[chunk]), "w%d" % s, reads=[("wb", chunk)], writes=[("wst", s)])
            return s

        def proj_fm(chunk, rhs_fn, rhs_keys, ncol, bank=None):
            s = load_w(chunk)
            if bank is None:
                bank = gp()

            def f(e):
                ins = None
                for k in range(16):
                    ins = e.matmul(ps[:, bank, 0:ncol], lhsT=wst[s][:, k * 128:(k + 1) * 128], rhs=rhs_fn(k),
                                   start=(k == 0), stop=(k == 15))
                return ins
            S.op("pe", f, reads=[("wst", s)] + list(rhs_keys), writes=[("ps", bank)])
            return bank

        def xT_rhs(k):
            return xT[:, k, :]

        def sig_half(bank, ncol, out_t, key, bias=None, rkeys=()):
            if bias is None:
                S.op("act", lambda e: e.activation(out=out_t, in_=ps[:, bank, 0:ncol], func=AF.Tanh, scale=0.5),
                     reads=[("ps", bank)], writes=[key])
            else:
                S.op("act", lambda e: e.activation(out=out_t, in_=ps[:, bank, 0:ncol], func=AF.Tanh, bias=bias, scale=0.5),
                     reads=[("ps", bank)] + list(rkeys), writes=[key])

        def fullseq(i, half, m):
            RM = ("R", "rnn%d" % i)
            S.dma("pool", lambda e: e.dma_start(max_dma_last_dim=4096, out=xT[:].rearrange("p a b -> p (a b)"), in_=xT_d[i]), "xT", writes=["xT"])
            stage('s1')
            for piece in range(4):
                s = wslot()
                S.dma("pool", lambda e, s=s, piece=piece: e.dma_start(max_dma_last_dim=4096, out=wst[s][:, 0:1600], in_=wtok_d[piece]), "w%d" % s,
                      writes=[("wst", s)])
                for tb in range(4):
                    def f(e, s=s, piece=piece, tb=tb):
                        ins = None
                        for kk in range(4):
                            k = piece * 4 + kk
                            ins = e.matmul(ps[:, tb, 0:400], lhsT=xT[:, k, tb * 128:(tb + 1) * 128],
                                           rhs=wst[s][:, kk * 400:(kk + 1) * 400], start=(k == 0), stop=(k == 15))
                        return ins
                    S.op("pe", f, reads=[("wst", s), "xT"], writes=[("ps", tb)])
            stage('s2')
            for tb in range(4):
                pb = tb % 2
                ss = sm[:, tb:tb + 1]
                rs = sm[:, 4 + tb:5 + tb]
                S.op("act", lambda e, tb=tb, ss=ss: e.activation(out=ft[0][:, 0:256], in_=ps[:, tb, 0:256], func=AF.Square, accum_out=ss),
                     reads=[("ps", tb)], writes=[("ft", 0), ("sm", tb)])
                S.op("dve", lambda e, ss=ss: e.tensor_scalar(out=ss, in0=ss, scalar1=1.0 / 256.0, scalar2=EPS, op0=ALU.mult, op1=ALU.add),
                     reads=[("sm", tb)], writes=[("sm", tb)])
                stage('s3_%d' % tb)
                S.op("pool", lambda e, ss=ss, rs=rs: e.tensor_tensor(out=rs, in0=ss, in1=mhalf[:], op=ALU.pow),
                     reads=[("sm", tb), "mhalf"], writes=[("sm", 4 + tb)])
                S.op("dve", lambda e, tb=tb, rs=rs, pb=pb: e.scalar_tensor_tensor(out=ckv_tok[pb][:], in0=ps[:, tb, 0:256], scalar=rs,
                                                                                 in1=kvg[:], op0=ALU.mult, op1=ALU.mult),
                     reads=[("ps", tb), ("sm", 4 + tb), "kvg"], writes=[("ckvtok", pb)])
                S.op("act", lambda e, tb=tb, pb=pb: e.activation(out=kix_tok[pb][:], in_=ps[:, tb, 256:384], func=AF.Copy),
                     reads=[("ps", tb)], writes=[("kixtok", pb)])
                if half == 1 or True:
                    pass
                if dbg and i == 0 and tb == 0:
                    S.op("act", lambda e: e.activation(out=ft[1][:, 0:256], in_=ckv_tok[0][:], func=AF.Copy),
                         reads=[("ckvtok", 0)], writes=[("ft", 1)])
                    S.dma("sp", lambda e: e.dma_start(out=dbg_out["d_ckv"], in_=ft[1][:, 0:256]), "dbg0", reads=[("ft", 1)], final=True)
                stage('s4_%d' % tb)
                S.dma("sp", lambda e, tb=tb, pb=pb: e.dma_start(out=ckv_d[i * 4 + tb], in_=ckv_tok[pb][:]), "skv%d" % pb,
                      reads=[("ckvtok", pb)], writes=[("ckv_d", i)])
                stage('s5_%d' % tb)
                tbk = gp()
                pv = ps[:, tbk, :].bitcast(BF16)

                def ftr(e, pb=pb, pv=pv):
                    e.transpose(out=pv[:, 0:128], in_=ckv_tok[pb][:, 0:128], identity=ident[:])
                    e.transpose(out=pv[:, 128:256], in_=ckv_tok[pb][:, 128:256], identity=ident[:])
                    return e.transpose(out=pv[:, 256:384], in_=kix_tok[pb][:], identity=ident[:])
                S.op("pe", ftr, reads=[("ckvtok", pb), ("kixtok", pb), "ident"], writes=[("ps", tbk)])
                stage('s6_%d' % tb)
                S.op("act", lambda e, tb=tb, pv=pv: e.activation(out=ckvT_st[:, :, tb * 128:(tb + 1) * 128],
                                                               in_=pv[:, 0:256].rearrange("p (a b) -> p a b", a=2), func=AF.Copy),
                     reads=[("ps", tbk)], writes=["ckvT_st"])
                S.op("dve", lambda e, tb=tb, pv=pv: e.tensor_copy(out=kixT_st[:, tb * 128:(tb + 1) * 128], in_=pv[:, 256:384]),
                     reads=[("ps", tbk)], writes=["kixT_st"])
                stage('s7_%d' % tb)
            stage('s8')
            S.dma("sp", lambda e: e.dma_start(out=ckvT_d[:, :, i * 512:(i + 1) * 512], in_=ckvT_st[:]), "skvT",
                  reads=["ckvT_st"], writes=[("ckvT_d", i)])
            S.dma("sp", lambda e: e.dma_start(out=kixT_d[:, i * 512:(i + 1) * 512], in_=kixT_st[:]), "skix",
                  reads=["kixT_st"], writes=[("kixT_d", i)])

            stage('tokproj')
            for c in range(16):
                par = c % 2
                o = par * 8
                xr = rv(o + 0)[:, 0:515]
                xc = rv(o + 1)[:, 0:512]
                xcb = rv(o + 1)[:, 512:768].bitcast(BF16)
                thr_ = rv(o + 2)[:, 0:512]
                thi = rv(o + 2)[:, 512:1024]
                a_ = rv(o + 3)[:, 0:512]
                a2 = rv(o + 3)[:, 512:1024]
                b_ = rv(o + 4)[:, 0:512]
                hh = rv(o + 4)[:, 512:1024]
                kk_ = lambda n: ("rt", par, n)
                bank = proj_fm(XR + c, xT_rhs, ["xT"], 512)
                S.op("dve", lambda e, xr=xr, c=c: e.tensor_copy(out=xr[:, 0:3], in_=tail[:, c, :]), reads=["tail"], writes=[kk_("xr0")], region=RM)
                S.op("act", lambda e, xr=xr, bank=bank: e.activation(out=xr[:, 3:515], in_=ps[:, bank, :], func=AF.Copy),
                     reads=[("ps", bank)], writes=[kk_("xr")], region=RM)
                S.op("dve", lambda e, xr=xr, c=c: e.tensor_copy(out=tail[:, c, :], in_=xr[:, 512:515]), reads=[kk_("xr")], writes=["tail"], region=RM)
                S.op("dve", lambda e, xr=xr, xc=xc, c=c: e.tensor_scalar(out=xc, in0=xr[:, 0:512], scalar1=cvec[:, 0, c:c + 1],
                                                                         scalar2=cvec[:, 4, c:c + 1], op0=ALU.mult, op1=ALU.add),
                     reads=[kk_("xr"), kk_("xr0"), "cvec"], writes=[kk_("xc")], region=RM)
                for k in range(1, 4):
                    S.op("dve", lambda e, xr=xr, xc=xc, c=c, k=k: e.scalar_tensor_tensor(out=xc, in0=xr[:, k:k + 512], scalar=cvec[:, k, c:c + 1],
                                                                                      in1=xc, op0=ALU.mult, op1=ALU.add),
                         reads=[kk_("xr"), kk_("xr0"), kk_("xc")], writes=[kk_("xc")], region=RM)
                S.op("act", lambda e, xc=xc, xcb=xcb: e.activation(out=xcb, in_=xc, func=AF.Copy), reads=[kk_("xc")], writes=[kk_("xcb")], region=RM)
                gs = c % 2
                S.dma("pool", lambda e, gs=gs, c=c: e.dma_start(max_dma_last_dim=4096, out=wgb[gs][:], in_=wg_d.rearrange("p (g n e) -> p g n e", g=2, n=16)[:, :, c, :]),
                      "wg%d" % gs, writes=[("wgb", gs)])
                br = gq()
                S.op("pe", lambda e, gs=gs, br=br, xcb=xcb: e.matmul(ps[:, br, :], lhsT=wgb[gs][:, 0, :], rhs=xcb, start=True, stop=True),
                     reads=[("wgb", gs), kk_("xcb")], writes=[("ps", br)], region=RM)
                bi = gq()
                S.op("pe", lambda e, gs=gs, bi=bi, xcb=xcb: e.matmul(ps[:, bi, :], lhsT=wgb[gs][:, 1, :], rhs=xcb, start=True, stop=True),
                     reads=[("wgb", gs), kk_("xcb")], writes=[("ps", bi)], region=RM)
                S.op("act", lambda e, br=br, thr_=thr_, c=c: e.activation(out=thr_, in_=ps[:, br, :], func=AF.Tanh, bias=hba[:, c:c + 1], scale=0.5),
                     reads=[("ps", br), "hba"], writes=[kk_("thr")], region=RM)
                S.op("act", lambda e, bi=bi, thi=thi, c=c: e.activation(out=thi, in_=ps[:, bi, :], func=AF.Tanh, bias=hbx[:, c:c + 1], scale=0.5),
                     reads=[("ps", bi), "hbx"], writes=[kk_("thi")], region=RM)
                S.op("act", lambda e, thr_=thr_, a_=a_, c=c: e.activation(out=a_, in_=thr_, func=AF.Exp, bias=hclam[:, c:c + 1], scale=hclam[:, c:c + 1]),
                     reads=[kk_("thr"), "hclam"], writes=[kk_("a")], region=RM)
                S.op("act", lambda e, thr_=thr_, a2=a2, c=c: e.activation(out=a2, in_=thr_, func=AF.Exp, bias=clam[:, c:c + 1], scale=clam[:, c:c + 1]),
                     reads=[kk_("thr"), "clam"], writes=[kk_("a2")], region=RM)
                S.op("act", lambda e, a2=a2: e.activation(out=a2, in_=a2, func=AF.Sqrt, bias=1.0, scale=-1.0),
                     reads=[kk_("a2")], writes=[kk_("a2")], region=RM)
                S.op("dve", lambda e, thi=thi, xc=xc, b_=b_: e.scalar_tensor_tensor(out=b_, in0=thi, scalar=1.0, in1=xc, op0=ALU.add, op1=ALU.mult),
                     reads=[kk_("thi"), kk_("xc")], writes=[kk_("b")], region=RM)
                if i == 0:
                    S.op("pool", lambda e, a2=a2: e.memset(a2[:, 0:1], 1.0), reads=[kk_("a2")], writes=[kk_("a2")], region=RM)
                S.op("dve", lambda e, b_=b_, a2=a2: e.scalar_tensor_tensor(out=b_, in0=b_, scalar=0.5, in1=a2, op0=ALU.mult, op1=ALU.mult),
                     reads=[kk_("b"), kk_("a2")], writes=[kk_("b")], region=RM)
                S.op("dve", lambda e, hh=hh, a_=a_, b_=b_, c=c: e.tensor_tensor_scan(out=hh, data0=a_, data1=b_, initial=hcar[:, c:c + 1],
                                                                                  op0=ALU.mult, op1=ALU.add),
                     reads=[kk_("a"), kk_("b"), "hcar"], writes=[kk_("h")], region=RM)
                S.op("dve", lambda e, hh=hh, c=c: e.tensor_copy(out=hcar[:, c:c + 1], in_=hh[:, 511:512]), reads=[kk_("h")], writes=["hcar"], region=RM)
                if half == 0:
                    S.op("dve", lambda e, hh=hh, c=c: e.tensor_scalar(out=hg[:, c, :], in0=hh, scalar1=selt[:, 0:1], scalar2=None, op0=ALU.mult),
                         reads=[kk_("h"), "selt"], writes=[("hg", c)], region=RM)
                else:
                    S.op("dve", lambda e, hh=hh, c=c: e.scalar_tensor_tensor(out=hg[:, c, :], in0=hh, scalar=selt[:, 1:2], in1=hg[:, c, :],
                                                                          op0=ALU.mult, op1=ALU.add),
                         reads=[kk_("h"), "selt", ("hg", c)], writes=[("hg", c)], region=RM)

        def own(m):
            stage('rnn')
            RA = ("R", "att%d" % m)
            RO = ("R", "out%d" % m)
            S.dma("pool", lambda e: e.dma_start(max_dma_last_dim=4096, out=xT[:].rearrange("p a b -> p (a b)"), in_=xTo_d[m]), "xT", writes=["xT"])
            hgk = [("hg", c) for c in range(16)]

            def branch_b(hook):
              if True:
                for c in range(16):
                    hook(c)
                    bank = proj_fm(RG + c, xT_rhs, ["xT"], 512)
                    t0 = ft[c % 2]
                    S.op("act", lambda e, bank=bank, t0=t0: e.activation(out=t0[:], in_=ps[:, bank, :], func=AF.Tanh, scale=0.5),
                         reads=[("ps", bank)], writes=[("ft", c % 2)])
                    S.op("dve", lambda e, bank=bank, t0=t0: e.scalar_tensor_tensor(out=t0[:], in0=t0[:], scalar=1.0, in1=ps[:, bank, :], op0=ALU.add, op1=ALU.mult),
                         reads=[("ps", bank), ("ft", c % 2)], writes=[("ft", c % 2)])
                    S.op("dve", lambda e, t0=t0, c=c: e.scalar_tensor_tensor(out=hg[:, c, :], in0=t0[:], scalar=0.5, in1=hg[:, c, :], op0=ALU.mult, op1=ALU.mult),
                         reads=[("ft", c % 2), ("hg", c)], writes=[("hg", c)])
                if dbg and m == 0:
                    for c in range(16):
                        S.op("act", lambda e, c=c: e.activation(out=ft[2][:], in_=hg[:, c, :], func=AF.Copy), reads=[("hg", c)], writes=[("ft", 2)])
                        S.dma("sp", lambda e, c=c: e.dma_start(out=dbg_out["d_hg"][:, c * 512:(c + 1) * 512], in_=ft[2][:]), "dbg1", reads=[("ft", 2)], final=True)
                for jc in range(16):
                    hook(16 + jc)
                    ba = proj_fm(WB + jc, lambda k: hg[:, k, :], hgk, 512)
                    bb = proj_fm(GB + jc, xT_rhs, ["xT"], 512)
                    t0 = ft[jc % 2]
                    S.op("act", lambda e, bb=bb, t0=t0: e.activation(out=t0[:], in_=ps[:, bb, :], func=AF.Tanh, scale=0.5),
                         reads=[("ps", bb)], writes=[("ft", jc % 2)])
                    S.op("dve", lambda e, ba=ba, t0=t0: e.scalar_tensor_tensor(out=t0[:], in0=t0[:], scalar=1.0, in1=ps[:, ba, :], op0=ALU.add, op1=ALU.mult),
                         reads=[("ps", ba), ("ft", jc % 2)], writes=[("ft", jc % 2)])
                    S.op("act", lambda e, t0=t0, jc=jc: e.activation(out=merged[:, jc, :], in_=t0[:], func=AF.Copy, scale=0.5),
                         reads=[("ft", jc % 2)], writes=[("mrg", jc)])

            stage('ownb')
            nkb_pair = 8 * m
            qv = qT.rearrange("p (c t) -> p c t", c=32)
            am = sm[:, 8:9]
            lo = sm[:, 9:10]
            mid = sm[:, 10:11]
            cnt = sm[:, 11:12]
            dl = sm[:, 12:13]
            wt = sm[:, 16:16 + NIT]

            def geom(qb):
                nkb = nkb_pair + 5 + qb
                return nkb, nkb * 128, (nkb + 3) // 4

            def proj_qi(hf):
                tsl = slice(hf * 256, (hf + 1) * 256)
                for c in range(8):
                    bank = proj_fm(QI + c, lambda k, tsl=tsl: xT[:, k, tsl], ["xT"], 256)
                    S.op("act", lambda e, bank=bank, c=c, hf=hf: e.activation(out=qiT[hf][:, c, :], in_=ps[:, bank, 0:256], func=AF.Copy),
                         reads=[("ps", bank)], writes=["qiT"])

            def proj_q(hf):
                tsl = slice(hf * 256, (hf + 1) * 256)
                for c in range(32):
                    bank = proj_fm(QL + c, lambda k, tsl=tsl: xT[:, k, tsl], ["xT"], 256)
                    if c % 2 == 0:
                        S.op("act", lambda e, bank=bank, c=c: e.activation(out=qT[:, c * 256:(c + 1) * 256], in_=ps[:, bank, 0:256], func=AF.Copy),
                             reads=[("ps", bank)], writes=["qT"], region=RA)
                    else:
                        S.op("dve", lambda e, bank=bank, c=c: e.tensor_copy(out=qT[:, c * 256:(c + 1) * 256], in_=ps[:, bank, 0:256]),
                             reads=[("ps", bank)], writes=["qT"], region=RA)

            def proj_widx(qb):
                s = wslot()
                bw = gp()
                for piece in range(4):
                    if piece > 0:
                        s = wslot()
                    S.dma("pool", lambda e, s=s, piece=piece: e.dma_start(max_dma_last_dim=4096, out=wst[s][:, 0:1600], in_=wtok_d[piece]), "w%d" % s, writes=[("wst", s)])

                    def f(e, s=s, piece=piece, qb=qb, bw=bw):
                        ins = None
                        for kk in range(4):
                            k = piece * 4 + kk
                            ins = e.matmul(ps[:, bw, 0:16], lhsT=xT[:, k, qb * 128:(qb + 1) * 128],
                                           rhs=wst[s][:, kk * 400 + 384:kk * 400 + 400], start=(k == 0), stop=(k == 15))
                        return ins
                    S.op("pe", f, reads=[("wst", s), "xT"], writes=[("ps", bw)])
                S.op("act", lambda e, bw=bw, qb=qb: e.activation(out=absw[:, qb, :], in_=ps[:, bw, 0:16], func=AF.Abs),
                     reads=[("ps", bw)], writes=[("absw", qb)])
                S.op("act", lambda e, bw=bw, qb=qb: e.activation(out=sgnw[:, qb, :], in_=ps[:, bw, 0:16], func=AF.Sign),
                     reads=[("ps", bw)], writes=[("sgnw", qb)])

            def do_idx(qb):
                nkb, nk, nkc = geom(qb)
                hf, q2 = qb // 2, qb % 2
                qsl = slice(q2 * 128, (q2 + 1) * 128)
                for kc in range(nkc):
                    w = min(512, nk - kc * 512)
                    kxs = kc % 2
                    S.dma("sp", lambda e, kc=kc, w=w, kxs=kxs: e.dma_start(out=kix[kxs][:, 0:w], in_=kixT_d[:, kc * 512:kc * 512 + w]), "kix%d" % kxs,
                          reads=[("kixT_d", kc)], writes=[("kix", kxs)])
                    accs = [gp(), gp()]
                    for h in range(16):
                        c = h // 2
                        po = (h % 2) * 64
                        accb = accs[h % 2]
                        zb = gq()
                        S.op("pe", lambda e, zb=zb, c=c, po=po, w=w, qsl=qsl, hf=hf, kxs=kxs: e.matmul(ps[:, zb, 0:w], lhsT=qiT[hf][po:po + 64, c, qsl],
                                                                                                     rhs=kix[kxs][po:po + 64, 0:w], start=True, stop=True),
                             reads=["qiT", ("kix", kxs)], writes=[("ps", zb)])
                        it = itmp[h % 2]
                        S.op("act", lambda e, zb=zb, it=it, h=h, w=w, qb=qb: e.activation(out=it[:, 0:w], in_=ps[:, zb, 0:w], func=AF.Relu,
                                                                                      scale=absw[:, qb, h:h + 1]),
                             reads=[("ps", zb), ("absw", qb)], writes=[("itmp", h % 2)])
                        if h < 2:
                            S.op("dve", lambda e, it=it, accb=accb, w=w, qb=qb, h=h: e.tensor_scalar(out=ps[:, accb, 0:w], in0=it[:, 0:w],
                                                                                                 scalar1=sgnw[:, qb, h:h + 1], scalar2=None, op0=ALU.mult),
                                 reads=[("itmp", h % 2), ("sgnw", qb)], writes=[("ps", accb)])
                        else:
                            S.op("dve", lambda e, it=it, accb=accb, w=w, h=h, qb=qb: e.scalar_tensor_tensor(out=ps[:, accb, 0:w], in0=it[:, 0:w],
                                                                                                      scalar=sgnw[:, qb, h:h + 1], in1=ps[:, accb, 0:w],
                                                                                                      op0=ALU.mult, op1=ALU.add),
                                 reads=[("itmp", h % 2), ("sgnw", qb), ("ps", accb)], writes=[("ps", accb)])
                    S.op("act", lambda e, accs=accs, kc=kc, w=w: e.activation(out=scores[:, kc * 512:kc * 512 + w], in_=ps[:, accs[0], 0:w], func=AF.Copy),
                         reads=[("ps", accs[0])], writes=["scores"], region=RA)
                    S.op("dve", lambda e, accs=accs, kc=kc, w=w: e.tensor_tensor(out=scores[:, kc * 512:kc * 512 + w], in0=scores[:, kc * 512:kc * 512 + w],
                                                                                in1=ps[:, accs[1], 0:w], op=ALU.add),
                         reads=[("ps", accs[1]), "scores"], writes=["scores"], region=RA)
                S.op("dve", lambda e, nk=nk: e.tensor_reduce(out=am, in_=scores[:, 0:nk], axis=AX.X, op=ALU.max, apply_absolute_value=True),
                     reads=["scores"], writes=["am"], region=RA)
                S.op("dve", lambda e: e.tensor_scalar(out=am, in0=am, scalar1=1.0, scalar2=None, op0=ALU.add), reads=["am"], writes=["am"])
                S.op("dve", lambda e: e.tensor_scalar(out=lo, in0=am, scalar1=-1.0, scalar2=None, op0=ALU.mult), reads=["am"], writes=["lo"])
                S.op("dve", lambda e: e.tensor_scalar(out=wt, in0=pow2[:], scalar1=am, scalar2=None, op0=ALU.mult),
                     reads=["am", "pow2"], writes=["wt"])
                S.op("dve", lambda e: e.tensor_tensor(out=mid, in0=lo, in1=wt[:, 0:1], op=ALU.add), reads=["lo", "wt"], writes=["mid"])
                S.dma("sp", lambda e, qb=qb: e.dma_start(out=cbt[:], in_=cbias_d[qb]), "cbt", writes=["cbt"])
                cw = (5 + qb) * 128
                S.op("dve", lambda e, cw=cw: e.tensor_tensor(out=scores[:, nkb_pair * 128:nkb_pair * 128 + cw],
                                                            in0=scores[:, nkb_pair * 128:nkb_pair * 128 + cw], in1=cbt[:, 0:cw], op=ALU.add),
                     reads=["scores", "cbt", "am"], writes=["scores"], region=RA)

            def do_bisect(qb, its):
                nkb, nk, nkc = geom(qb)
                mt = maskT[qb % 2]
                junk = mt[:].rearrange("p a b -> p (a b)")
                for it_ in its:
                    S.op("dve", lambda e, nk=nk, junk=junk: e.tensor_scalar(out=junk[:, 0:nk], in0=scores[:, 0:nk], scalar1=mid, scalar2=None,
                                                                             op0=ALU.is_ge, op1=ALU.add, accum_out=cnt),
                         reads=["scores", "mid"], writes=[("maskT", qb % 2), "cnt"], region=RA)
                    S.op("dve", lambda e: e.tensor_scalar(out=dl, in0=cnt, scalar1=TOPK - 0.5, scalar2=0.5, op0=ALU.is_ge, op1=ALU.subtract),
                         reads=["cnt"], writes=["dl"])
                    S.op("dve", lambda e, it_=it_: e.scalar_tensor_tensor(out=mid, in0=dl, scalar=wt[:, it_:it_ + 1], in1=mid, op0=ALU.mult, op1=ALU.add),
                         reads=["dl", "wt", "mid"], writes=["mid"])

            def do_mask(qb):
                nkb, nk, nkc = geom(qb)
                mt = maskT[qb % 2]
                S.op("dve", lambda e: e.scalar_tensor_tensor(out=lo, in0=wt[:, NIT - 1:NIT], scalar=-0.5, in1=mid, op0=ALU.mult, op1=ALU.add),
                     reads=["wt", "mid"], writes=["lo"])
                if dbg and m == 0:
                    S.dma("sp", lambda e, qb=qb: e.dma_start(out=dbg_out["d_thr"][:, qb:qb + 1], in_=lo, allow_slow_non_contiguous=True), "dbg2", reads=["lo"], final=True)
                    if qb == 3:
                        S.dma("sp", lambda e: e.dma_start(out=dbg_out["d_sc"], in_=scores[:, 0:1024]), "dbg3", reads=["scores"], region=RA, final=True)
                for kc in range(nkc):
                    w = min(512, nk - kc * 512)
                    nb = w // 128
                    S.op("dve", lambda e, kc=kc, w=w: e.tensor_scalar(out=mk[:, 0:w], in0=scores[:, kc * 512:kc * 512 + w], scalar1=lo, scalar2=None,
                                                                       op0=ALU.is_ge),
                         reads=["scores", "lo"], writes=["mk"], region=RA)
                    tbk = gp()
                    pv = ps[:, tbk, :].bitcast(BF16)

                    def ftr(e, pv=pv, nb=nb):
                        ins = None
                        for j in range(nb):
                            ins = e.transpose(out=pv[:, j * 128:(j + 1) * 128], in_=mk[:, j * 128:(j + 1) * 128], identity=ident[:])
                        return ins
                    S.op("pe", ftr, reads=["mk", "ident"], writes=[("ps", tbk)])
                    S.op("act", lambda e, kc=kc, w=w, nb=nb, pv=pv, mt=mt: e.activation(out=mt[:, kc * 4:kc * 4 + nb, :].rearrange("p a b -> p (a b)"),
                                                                                     in_=pv[:, 0:w], func=AF.Copy),
                         reads=[("ps", tbk)], writes=[("maskT", qb % 2)])

            def do_att(qb, hook=None):
                nkb, nk, nkc = geom(qb)
                q2 = qb % 2
                qsl = slice(q2 * 128, (q2 + 1) * 128)
                mt = maskT[qb % 2]
                meng = "pool" if hook is not None else "dve"
                for hq in range(4):
                    ws_ = hq % 2
                    S.dma("pool", lambda e, ws_=ws_, hq=hq: e.dma_start(max_dma_last_dim=4096, out=wuvb[ws_][:].rearrange("p a b -> p (a b)"),
                                                                         in_=wuv_d[:, hq * 1024:(hq + 1) * 1024]), "wuv%d" % ws_, writes=[("wuvb", ws_)])
                    if hook is not None:
                        hook(hq)
                    DEP = 2

                    def chunk_dma(kc):
                        w = min(512, nk - kc * 512)
                        nb = w // 128
                        cb_ = kc % 2
                        S.dma("sp", lambda e, kc=kc, nb=nb, cb_=cb_: e.dma_start(out=ckvc[cb_][:, 0:nb, :],
                                                                                  in_=ckv_d[kc * 4:kc * 4 + nb].rearrange("b s d -> s b d")),
                              "ckvc%d" % cb_, reads=[("ckv_d", kc)], writes=[("ckvc", cb_)])
                        S.dma("sp", lambda e, kc=kc, w=w, cb_=cb_: e.dma_start(out=ckvTc[cb_][:, :, 0:w], in_=ckvT_d[:, :, kc * 512:kc * 512 + w]),
                              "ckvTc%d" % cb_, reads=[("ckvT_d", kc)], writes=[("ckvTc", cb_)])

                    chunk_dma(0)
                    if nkc > 1:
                        chunk_dma(1)
                    for idx_ in range(nkb + DEP):
                        if idx_ < nkb:
                            kb = idx_
                            kc, j = kb // 4, kb % 4
                            cb_ = kc % 2
                            rel = kb - nkb_pair
                            slot = {qb - 1: 0, qb: 1, qb + 3: 2, qb + 4: 3}.get(rel)
                            qkb = 3 + (kb % 3)
                            pp = kb % 3

                            def fqk(e, j=j, qkb=qkb, slot=slot, hq=hq, qsl=qsl, cb_=cb_):
                                ins = None
                                for k in range(2):
                                    ins = e.matmul(ps[:, qkb, :], lhsT=ckvTc[cb_][:, k, j * 128:(j + 1) * 128],
                                                   rhs=qv[:, hq * 8 + k:hq * 8 + 8:2, qsl], start=(k == 0), stop=(k == 1 and slot is None))
                                if slot is not None:
                                    ins = e.matmul(ps[:, qkb, :], lhsT=ident[:], rhs=biasS[:, slot, hq * 4:hq * 4 + 4, :], start=False, stop=True)
                                return ins
                            S.op("pe", fqk, reads=[("ckvTc", cb_), "qT", "ident", "biasS"], writes=[("ps", qkb)], region=RA)
                            S.op("act", lambda e, qkb=qkb, pp=pp: e.activation(out=Pb[pp][:], in_=ps[:, qkb, :], func=AF.Exp, scale=1.0 / 16.0),
                                 reads=[("ps", qkb)], writes=[("Pb", pp)])
                            S.op(meng, lambda e, pp=pp, kb=kb, mt=mt: e.tensor_tensor(out=Pm[pp][:], in0=Pb[pp][:].rearrange("p (a b) -> p a b", a=4),
                                                                                   in1=mt[:, kb:kb + 1, :].to_broadcast([128, 4, 128]), op=ALU.mult),
                                 reads=[("Pb", pp), ("maskT", qb % 2)], writes=[("Pm", pp)])
                        if idx_ >= DEP:
                            kb = idx_ - DEP
                            kc, j = kb // 4, kb % 4
                            cb_ = kc % 2
                            pp = kb % 3

                            def fpv(e, j=j, pp=pp, kb=kb, nkb=nkb, cb_=cb_):
                                rhs = Pm[pp][:].rearrange("p a b -> p (a b)")
                                e.matmul(ps[:, 0, :], lhsT=ckvc[cb_][:, j, 0:128], rhs=rhs, start=(kb == 0), stop=(kb == nkb - 1))
                                e.matmul(ps[:, 1, :], lhsT=ckvc[cb_][:, j, 128:256], rhs=rhs, start=(kb == 0), stop=(kb == nkb - 1))
                                return e.matmul(ps[:, 2, :], lhsT=ones[:], rhs=rhs, start=(kb == 0), stop=(kb == nkb - 1))
                            S.op("pe", fpv, reads=[("ckvc", cb_), ("Pm", pp), "ones"], writes=[("ps", 0), ("ps", 1), ("ps", 2)])
                            if j == 3 and kc + 2 < nkc:
                                chunk_dma(kc + 2)
                    S.op("dve", lambda e: e.reciprocal(out=rden[:], in_=ps[:, 2, :]), reads=[("ps", 2)], writes=["rden"])
                    for k in range(2):
                        S.op("dve", lambda e, k=k: e.tensor_tensor(out=onT[:, k, :], in0=ps[:, k, :], in1=rden[:], op=ALU.mult),
                             reads=[("ps", k), "rden"], writes=["onT"])
                    ub = gp()

                    def fuv(e, ub=ub, ws_=ws_):
                        ins = None
                        for hh_ in range(4):
                            for k in range(2):
                                ins = e.matmul(ps[:, ub, hh_ * 128:(hh_ + 1) * 128], lhsT=wuvb[ws_][:, hh_ * 2 + k, :],
                                               rhs=onT[:, k, hh_ * 128:(hh_ + 1) * 128], start=(k == 0), stop=(k == 1))
                        return ins
                    S.op("pe", fuv, reads=["onT", ("wuvb", ws_)], writes=[("ps", ub)])
                    S.op("act", lambda e, ub=ub, hq=hq, qb=qb: e.activation(out=hg[:, hq * 4:hq * 4 + 4, qb * 128:(qb + 1) * 128],
                                                                          in_=ps[:, ub, :].rearrange("p (a b) -> p a b", a=4), func=AF.Copy),
                         reads=[("ps", ub)], writes=[("hg", hq * 4 + x) for x in range(4)])

            def bis_hook(qb):
                per = (NIT + 3) // 4

                def hk(hq):
                    do_bisect(qb, range(hq * per, min(NIT, (hq + 1) * per)))
                return hk

            proj_qi(0)
            proj_widx(0)
            proj_widx(1)
            do_idx(0)
            branch_b(lambda step: do_bisect(0, range(step, step + 1)) if step < NIT else None)
            proj_q(0)
            do_mask(0)
            do_idx(1)
            do_att(0, bis_hook(1))
            do_mask(1)
            proj_qi(1)
            proj_widx(2)
            proj_widx(3)
            do_idx(2)
            do_att(1, bis_hook(2))
            do_mask(2)
            proj_q(1)
            do_idx(3)
            do_att(2, bis_hook(3))
            do_mask(3)
            do_att(3)
            stage('attn')
            for c in range(16):
                bank = proj_fm(AG + c, xT_rhs, ["xT"], 512)
                t0 = ft[c % 2]
                S.op("act", lambda e, bank=bank, t0=t0: e.activation(out=t0[:], in_=ps[:, bank, :], func=AF.Tanh, scale=0.5),
                     reads=[("ps", bank)], writes=[("ft", c % 2)])
                S.op("dve", lambda e, bank=bank, t0=t0: e.scalar_tensor_tensor(out=t0[:], in0=t0[:], scalar=1.0, in1=ps[:, bank, :], op0=ALU.add, op1=ALU.mult),
                     reads=[("ps", bank), ("ft", c % 2)], writes=[("ft", c % 2)])
                S.op("dve", lambda e, t0=t0, c=c: e.scalar_tensor_tensor(out=hg[:, c, :], in0=t0[:], scalar=0.5, in1=hg[:, c, :], op0=ALU.mult, op1=ALU.mult),
                     reads=[("ft", c % 2), ("hg", c)], writes=[("hg", c)])
            if dbg and m == 0:
                for c in range(16):
                    S.op("act", lambda e, c=c: e.activation(out=ft[2][:], in_=hg[:, c, :], func=AF.Copy), reads=[("hg", c)], writes=[("ft", 2)])
                    S.dma("sp", lambda e, c=c: e.dma_start(out=dbg_out["d_ga"][:, c * 512:(c + 1) * 512], in_=ft[2][:]), "dbg4", reads=[("ft", 2)], final=True)
            stage('gate')
            for jc in range(16):
                ba = proj_fm(WA + jc, lambda k: hg[:, k, :], hgk, 512)
                bb = proj_fm(GA + jc, xT_rhs, ["xT"], 512)
                t0 = ft[jc % 2]
                S.op("act", lambda e, bb=bb, t0=t0: e.activation(out=t0[:], in_=ps[:, bb, :], func=AF.Tanh, scale=0.5),
                     reads=[("ps", bb)], writes=[("ft", jc % 2)])
                S.op("dve", lambda e, ba=ba, t0=t0: e.scalar_tensor_tensor(out=t0[:], in0=t0[:], scalar=1.0, in1=ps[:, ba, :], op0=ALU.add, op1=ALU.mult),
                     reads=[("ps", ba), ("ft", jc % 2)], writes=[("ft", jc % 2)])
                S.op("dve", lambda e, t0=t0, jc=jc: e.scalar_tensor_tensor(out=merged[:, jc, :], in0=t0[:], scalar=0.5, in1=merged[:, jc, :], op0=ALU.mult, op1=ALU.add),
                     reads=[("ft", jc % 2), ("mrg", jc)], writes=[("mrg", jc)])
            if dbg and m == 0:
                for c in range(16):
                    S.op("act", lambda e, c=c: e.activation(out=ft[2][:], in_=merged[:, c, :], func=AF.Copy), reads=[("mrg", c)], writes=[("ft", 2)])
                    S.dma("sp", lambda e, c=c: e.dma_start(out=dbg_out["d_mrg"][:, c * 512:(c + 1) * 512], in_=ft[2][:]), "dbg5", reads=[("ft", 2)], final=True)
            stage('brancha')
            mk_ = [("mrg", c) for c in range(16)]
            S.dma("sp", lambda e: e.dma_start(out=lng, in_=lng_d), "lng", writes=["lng"], region=RO)
            S.dma("sp", lambda e: e.dma_start(out=lnb, in_=lnb_d), "lnb", writes=["lnb"], region=RO)
            for tb in range(4):
                S.dma("sp", lambda e, tb=tb: e.dma_start(out=ysub[:, tb * 2048:(tb + 1) * 2048], in_=xtok_d[m, tb]), "xtk%d" % tb,
                      writes=[("ysub", tb)], region=RO)
            for jj in range(8):
                wsl = jj % 2
                S.dma("pool", lambda e, wsl=wsl, jj=jj: e.dma_start(out=wo[wsl], in_=wout_b[jj]), "wo%d" % wsl, reads=[("wob", jj)], writes=[("wo", wsl)], region=RO)
                for tb in range(4):
                    ob = gp()

                    def fo(e, ob=ob, wsl=wsl, tb=tb):
                        ins = None
                        for k in range(16):
                            ins = e.matmul(ps[:, ob, 0:256], lhsT=merged[:, k, tb * 128:(tb + 1) * 128], rhs=wo[wsl][:, k * 256:(k + 1) * 256],
                                           start=(k == 0), stop=(k == 15))
                        return ins
                    S.op("pe", fo, reads=mk_ + [("wo", wsl)], writes=[("ps", ob)], region=RO)
                    ys = ysub[:, tb * 2048 + jj * 256: tb * 2048 + (jj + 1) * 256]
                    S.op("dve", lambda e, ob=ob, ys=ys: e.scalar_tensor_tensor(out=ys, in0=ys, scalar=ALPHA, in1=ps[:, ob, 0:256], op0=ALU.mult, op1=ALU.add),
                         reads=[("ps", ob), ("ysub", tb)], writes=[("ysub", tb)], region=RO)
            for tb in range(4):
                yv = ysub[:, tb * 2048:(tb + 1) * 2048]
                mv = sm[:, 40:42]
                rs = sm[:, 42:43]
                for q in range(4):
                    S.op("dve", lambda e, yv=yv, q=q: e.bn_stats(out=bst[:, q, :], in_=yv[:, q * 512:(q + 1) * 512]),
                         reads=[("ysub", tb)], writes=[("bst", q)], region=RO)
                S.op("dve", lambda e, mv=mv: e.bn_aggr(out=mv, in_=bst[:].rearrange("p a b -> p (a b)")),
                     reads=[("bst", q) for q in range(4)], writes=["mv"])
                S.op("dve", lambda e, mv=mv, rs=rs: e.tensor_scalar(out=rs, in0=mv[:, 1:2], scalar1=EPS, scalar2=None, op0=ALU.add), reads=["mv"], writes=["rs"])
                S.op("pool", lambda e, rs=rs: e.tensor_tensor(out=rs, in0=rs, in1=mhalf[:], op=ALU.pow), reads=["rs", "mhalf"], writes=["rs"])
                S.op("dve", lambda e, yv=yv, mv=mv, rs=rs: e.tensor_scalar(out=yv, in0=yv, scalar1=mv[:, 0:1], scalar2=rs, op0=ALU.subtract, op1=ALU.mult),
                     reads=[("ysub", tb), "mv", "rs"], writes=[("ysub", tb)], region=RO)
                S.op("dve", lambda e, yv=yv: e.tensor_tensor(out=yv, in0=yv, in1=lng, op=ALU.mult), reads=[("ysub", tb), "lng"], writes=[("ysub", tb)], region=RO)
                S.op("dve", lambda e, yv=yv: e.tensor_tensor(out=yv, in0=yv, in1=lnb, op=ALU.add), reads=[("ysub", tb), "lnb"], writes=[("ysub", tb)], region=RO)
                S.dma("sp", lambda e, tb=tb, yv=yv: e.dma_start(out=y_d[m, tb], in_=yv), "yo%d" % tb, reads=[("ysub", tb)], region=RO, final=True)

        try:
            stage('init')
            for m in range(n_pairs):
                fullseq(2 * m, 0, m)
                if m == 0:
                    convert_rest()
                fullseq(2 * m + 1, 1, m)
                own(m)
        except _Stop:
            pass
        S.emit()
    return nc


def _t5_bucket_np(n):
    n = np.maximum(n, 0)
    nf = np.maximum(n, 1).astype(np.float32)
    large = 16 + (np.log(nf / np.float32(16)) / np.float32(math.log(128 / 16)) * np.float32(16)).astype(np.int32)
    large = np.minimum(large, 31)
    return np.where(n < 16, n, large)


def _fm(w):
    n = w.shape[1] // 128
    return np.ascontiguousarray(w.reshape(16, 128, n, 128).transpose(2, 1, 0, 3)).reshape(n, 128, 16 * 128)


def _vec(v):
    return np.ascontiguousarray(v.reshape(16, 128).T)


def make_inputs(x, w_in, kv_norm_g, w_uv, w_branch_a, conv_w, conv_b, w_gate_a, b_gate_a,
                w_gate_x, b_gate_x, lru_lambda, w_branch_b, rel_bias, w_out, ln_g, ln_b):
    w_in = w_in[0]
    chunks = []
    for base, n in ((QL, 32), (AG, 16), (QI, 8), (XR, 16), (RG, 16), (GA, 16), (GB, 16)):
        c0 = COL[base]
        chunks.append(_fm(w_in[:, c0:c0 + n * 128]))
    chunks.append(_fm(w_branch_a[0]))
    chunks.append(_fm(w_branch_b[0]))
    wfm = np.concatenate(chunks, axis=0)
    tokcols = np.concatenate([np.arange(4096, 4352), np.arange(7424, 7488), np.arange(7424, 7488), np.arange(7488, 7504)])
    wt = w_in[:, tokcols]
    wtok = np.ascontiguousarray(wt.reshape(4, 4, 128, 400).transpose(0, 2, 1, 3)).reshape(4, 128, 1600)
    wout = np.ascontiguousarray(w_out[0].reshape(16, 128, 8, 256).transpose(2, 1, 0, 3)).reshape(8, 128, 16 * 256)
    wuv = np.ascontiguousarray(w_uv[0].reshape(16, 2, 128, 128).transpose(2, 0, 1, 3)).reshape(128, 32 * 128)
    wg = np.ascontiguousarray(np.stack([w_gate_a[0], w_gate_x[0]], 0).transpose(2, 0, 1, 3)).reshape(128, 32 * 128)
    cvec = np.stack([_vec(conv_w[0, 0]), _vec(conv_w[0, 1]), _vec(conv_w[0, 2]), _vec(conv_w[0, 3]), _vec(conv_b[0]),
                     _vec(b_gate_a[0]), _vec(b_gate_x[0]), _vec(lru_lambda[0])], axis=1).reshape(128, 128)
    kvg = np.ascontiguousarray(np.broadcast_to(kv_norm_g[0][None, :], (128, 256)))
    lng = np.ascontiguousarray(np.broadcast_to(ln_g[0][None, :], (128, D)))
    lnb = np.ascontiguousarray(np.broadcast_to(ln_b[0][None, :], (128, D)))
    s_ = np.arange(128)[:, None]
    t_ = np.arange(128)[None, :]
    diag = rel_bias[_t5_bucket_np(t_ - s_)]
    prev = rel_bias[_t5_bucket_np(128 + t_ - s_)]
    diag = np.ascontiguousarray(diag.transpose(0, 2, 1))
    prev = np.ascontiguousarray(prev.transpose(0, 2, 1))
    b31 = np.ascontiguousarray(np.broadcast_to(rel_bias[31][None, :, None], (128, 16, 128)))
    pow2 = np.ascontiguousarray(np.broadcast_to((2.0 ** -np.arange(NIT, dtype=np.float64)).astype(np.float32)[None, :], (128, NIT)))
    shared = dict(wfm=wfm, wtok=wtok, wout=wout, wuv=wuv, wg=wg, cvec=np.ascontiguousarray(cvec), kvg=kvg, lng=lng, lnb=lnb,
                  b31=b31.reshape(128, 2048), pow2=pow2)
    in_maps = []
    for core in range(8):
        b, par = core // 2, core % 2
        xb = x[b]
        xT = np.ascontiguousarray(xb.reshape(NT, 512, 16, 128).transpose(0, 3, 2, 1)).reshape(NT, 128, 16 * 512)
        xTo = np.ascontiguousarray(xT[par::2])
        xtok = np.ascontiguousarray(xb.reshape(NP, 2, 4, 128, D)[:, par])
        sel = np.zeros((128, 2), np.float32)
        sel[:, par] = 1.0
        cb = np.zeros((4, 128, 1024), np.float32)
        for qb in range(4):
            tg = par * 512 + qb * 128 + np.arange(128)[:, None]
            sg = np.arange(1024)[None, :]
            cb[qb] = np.where(sg <= tg, 0.0, NEG)
        if par == 0:
            slots = [prev, diag, b31, b31]
        else:
            slots = [b31, b31, prev, diag]
        biasS = np.ascontiguousarray(np.stack(slots, axis=1)).reshape(128, 4 * 16 * 128).astype(np.float32)
        d = dict(shared)
        d.update(xT=xT, xTo=xTo, xtok=xtok, sel=sel, cbias=cb, biasS=biasS)
        in_maps.append(d)
    return in_maps


def kernel(**inputs):
    inputs = {k: np.asarray(v) for k, v in inputs.items()}
    in_maps = make_inputs(**inputs)
    nc = build_nc()
    res = run_bass_kernel_spmd(nc, in_maps, core_ids=list(range(8)))
    out = np.empty((4, T, D), np.float32)
    for core in range(8):
        b, par = core // 2, core % 2
        yv = res.results[core]["y"].reshape(NP, 512, D)
        out.reshape(4, NP, 2, 512, D)[b, :, par] = yv
    return out
```

```python
import math
from contextlib import ExitStack

import numpy as np
import concourse.bass as bass
import concourse.mybir as mybir
from concourse.bass_utils import run_bass_kernel_spmd

F32 = mybir.dt.float32
BF16 = mybir.dt.bfloat16
U8 = mybir.dt.uint8
AF = mybir.ActivationFunctionType
ALU = mybir.AluOpType
AX = mybir.AxisListType

D = 2048
T = 8192
NT = 16
NP = 8
TOPK = 256
NIT = 20
ALPHA = 2.0 ** 0.25
EPS = 1e-5
NEG = -1.0e30
QL, AG, QI, XR, RG, GA, GB, WA, WB = 0, 32, 48, 56, 72, 88, 104, 120, 136
NCH = 152
COL = {QL: 0, AG: 4352, QI: 6400, XR: 7504, RG: 9552, GA: 11600, GB: 13648}


class _Op:
    __slots__ = ("eng", "fn", "waits", "idx", "inc", "semval", "dma_sem", "dma_val")

    def __init__(self, eng, fn):
        self.eng = eng
        self.fn = fn
        self.waits = []
        self.idx = -1
        self.inc = False
        self.semval = 0
        self.dma_sem = None
        self.dma_val = 0


class Sched:
    ENG = ("pe", "act", "dve", "pool", "sp")
    SAME = ("act", "dve", "pool")

    def __init__(self, nc, stack):
        self.nc = nc
        self.stack = stack
        self.ops = {e: [] for e in self.ENG}
        self.last_w = {}
        self.readers = {}
        self.dma_sems = {}
        self.final_tokens = []
        self.reg_mode = {}
        self.reg_cur = {}
        self.reg_fence = {}

    def _dma_sem(self, name):
        if name not in self.dma_sems:
            h = self.stack.enter_context(self.nc.semaphore("d_" + name))
            self.dma_sems[name] = [h, 0]
        return self.dma_sems[name]

    @staticmethod
    def _compress(toks):
        be = {}
        bd = {}
        for t in toks:
            if t[0] == "e":
                p = t[1]
                if p.eng not in be or be[p.eng].idx < p.idx:
                    be[p.eng] = p
            else:
                if bd.get(t[1], 0) < t[2]:
                    bd[t[1]] = t[2]
        return [("e", p) for p in be.values()] + [("d", k, v) for k, v in bd.items()]

    def _deps(self, reads, writes, region):
        toks = []
        for k in reads:
            t = self.last_w.get(k)
            if t is not None:
                toks.append(t)
        for k in writes:
            t = self.last_w.get(k)
            if t is not None:
                toks.append(t)
            toks.extend(self.readers.get(k, ()))
        if region is not None:
            r, mode = region
            if self.reg_mode.get(r) != mode:
                self.reg_fence[r] = self._compress(self.reg_cur.get(r, []) + self.reg_fence.get(r, []))
                self.reg_cur[r] = []
                self.reg_mode[r] = mode
            toks.extend(self.reg_fence.get(r, ()))
        return self._compress(toks)

    def _commit(self, tok, reads, writes, region):
        for k in reads:
            lst = self.readers.setdefault(k, [])
            lst.append(tok)
            if len(lst) > 12:
                self.readers[k] = self._compress(lst)
        for k in writes:
            self.last_w[k] = tok
            self.readers[k] = []
        if region is not None:
            lst = self.reg_cur.setdefault(region[0], [])
            lst.append(tok)
            if len(lst) > 12:
                self.reg_cur[region[0]] = self._compress(lst)

    @staticmethod
    def _excl(reads, writes):
        r = [k for k in reads if not (isinstance(k, tuple) and k[0] == "ps")]
        w = list(writes) + [k for k in reads if isinstance(k, tuple) and k[0] == "ps" and k not in writes]
        return r, w

    def op(self, eng, fn, reads=(), writes=(), region=None):
        reads, writes = self._excl(reads, writes)
        o = _Op(eng, fn)
        o.idx = len(self.ops[eng])
        o.waits = self._deps(reads, writes, region)
        self.ops[eng].append(o)
        self._commit(("e", o), reads, writes, region)
        return o

    def dma(self, eng, fn, sem, reads=(), writes=(), region=None, final=False):
        o = _Op(eng, fn)
        o.idx = len(self.ops[eng])
        o.waits = self._deps(reads, writes, region)
        s = self._dma_sem(sem)
        s[1] += 16
        o.dma_sem = sem
        o.dma_val = s[1]
        self.ops[eng].append(o)
        tok = ("d", sem, s[1])
        self._commit(tok, reads, writes, region)
        if final:
            self.final_tokens.append(tok)
        return o

    def dma_group(self, eng, fns, sem, keys):
        s = self._dma_sem(sem)
        for fn in fns:
            o = _Op(eng, fn)
            o.idx = len(self.ops[eng])
            o.waits = []
            s[1] += 16
            o.dma_sem = sem
            o.dma_val = s[1]
            self.ops[eng].append(o)
        tok = ("d", sem, s[1])
        for k in keys:
            self.last_w[k] = tok
            self.readers[k] = []

    def emit(self, final_eng="sp"):
        nc = self.nc
        fo = _Op(final_eng, None)
        fo.idx = len(self.ops[final_eng])
        fo.waits = list(self.final_tokens) + [("d", k, v[1]) for k, v in self.dma_sems.items()]
        for e in self.ENG:
            if e != final_eng and self.ops[e]:
                fo.waits.append(("e", self.ops[e][-1]))
        self.ops[final_eng].append(fo)
        for e in self.ENG:
            for o in self.ops[e]:
                for t in o.waits:
                    if t[0] == "e":
                        p = t[1]
                        if p.eng != o.eng or o.eng in self.SAME:
                            p.inc = True
        esem = {}
        for e in self.ENG:
            esem[e] = self.stack.enter_context(nc.semaphore("e_" + e))
            c = 0
            for o in self.ops[e]:
                if o.inc:
                    c += 1
                    o.semval = c
        block = self.stack.enter_context(nc.Block())

        def run(e, eng):
            seen_e = {x: -1 for x in self.ENG}
            seen_d = {}
            for o in self.ops[e]:
                for t in o.waits:
                    if t[0] == "e":
                        p = t[1]
                        if p.eng == e and e not in self.SAME:
                            continue
                        if p.idx > seen_e[p.eng]:
                            eng.wait_ge(esem[p.eng], p.semval)
                            seen_e[p.eng] = p.idx
                    else:
                        if seen_d.get(t[1], 0) < t[2]:
                            eng.wait_ge(self.dma_sems[t[1]][0], t[2])
                            seen_d[t[1]] = t[2]
                if o.fn is None:
                    continue
                ins = o.fn(eng)
                if o.dma_sem is not None:
                    ins.then_inc(self.dma_sems[o.dma_sem][0], 16)
                elif o.inc:
                    ins.then_inc(esem[e], 1)

        @block.tensor
        def _(eng):
            run("pe", eng)

        @block.scalar
        def _(eng):
            run("act", eng)

        @block.vector
        def _(eng):
            run("dve", eng)

        @block.gpsimd
        def _(eng):
            run("pool", eng)

        @block.sync
        def _(eng):
            run("sp", eng)


class _Stop(Exception):
    pass


def build_nc(n_pairs=NP, dbg=False, stop=None):
    def stage(name):
        if stop is not None and name == stop:
            raise _Stop()

    nc = bass.Bass("TRN2", target_bir_lowering=False)

    def din(name, shape, dt=F32):
        return nc.dram_tensor(name, list(shape), dt, kind="ExternalInput").ap()

    xT_d = din("xT", [NT, 128, 16 * 512])
    xTo_d = din("xTo", [NP, 128, 16 * 512])
    xtok_d = din("xtok", [NP, 4, 128, D])
    wfm_d = din("wfm", [NCH, 128, 16 * 128])
    wtok_d = din("wtok", [4, 128, 4 * 400])
    wout_d = din("wout", [8, 128, 16 * 256])
    wuv_d = din("wuv", [128, 32 * 128])
    wg_d = din("wg", [128, 32 * 128])
    cvec_d = din("cvec", [128, 8 * 16])
    kvg_d = din("kvg", [128, 256])
    lng_d = din("lng", [128, D])
    lnb_d = din("lnb", [128, D])
    biasS_d = din("biasS", [128, 4 * 16 * 128])
    b31_d = din("b31", [128, 16 * 128])
    sel_d = din("sel", [128, 2])
    cbias_d = din("cbias", [4, 128, 1024])
    pow2_d = din("pow2", [128, NIT])
    y_d = nc.dram_tensor("y", [NP, 4, 128, D], F32, kind="ExternalOutput").ap()
    ckv_d = nc.dram_tensor("ckv_s", [64, 128, 256], BF16, kind="Internal").ap()
    ckvT_d = nc.dram_tensor("ckvT_s", [128, 2, T], BF16, kind="Internal").ap()
    kixT_d = nc.dram_tensor("kixT_s", [128, T], BF16, kind="Internal").ap()
    wfm_b = nc.dram_tensor("wfm_bf", [NCH, 128, 16 * 128], BF16, kind="Internal").ap()
    wout_b = nc.dram_tensor("wout_bf", [8, 128, 16 * 256], BF16, kind="Internal").ap()
    dbg_out = {}
    if dbg:
        for nm, shp in (("d_ckv", [128, 256]), ("d_sc", [128, 1024]), ("d_thr", [128, 4]),
                        ("d_hg", [128, 16 * 512]), ("d_mrg", [128, 16 * 512]), ("d_ga", [128, 16 * 512])):
            dbg_out[nm] = nc.dram_tensor(nm, shp, F32, kind="ExternalOutput").ap()

    st = ExitStack()
    with st:
        def sb(name, shape, dt):
            return st.enter_context(nc.sbuf_tensor("s_" + name, list(shape), dt))

        S = Sched(nc, st)
        biasS = sb("biasS", [128, 4, 16, 128], BF16)
        cvec = sb("cvec", [128, 8, 16], F32)
        clam = sb("clam", [128, 16], F32)
        hclam = sb("hclam", [128, 16], F32)
        hba = sb("hba", [128, 16], F32)
        hbx = sb("hbx", [128, 16], F32)
        kvg = sb("kvg", [128, 256], F32)
        selt = sb("selt", [128, 2], F32)
        pow2 = sb("pow2", [128, NIT], F32)
        mhalf = sb("mhalf", [128, 1], F32)
        ident = sb("ident", [128, 128], BF16)
        ones = sb("ones", [128, 128], BF16)
        tail = sb("tail", [128, 16, 3], F32)
        hcar = sb("hcar", [128, 16], F32)
        xT = sb("xT", [128, 16, 512], BF16)
        NW = 2 if dbg else 3
        wst = [sb("wst%d" % i, [128, 16 * 128], BF16) for i in range(NW)]
        wgb = [sb("wgb%d" % i, [128, 2, 128], BF16) for i in range(2)]
        wuvb = [sb("wuvb%d" % i, [128, 8, 128], BF16) for i in range(2)]
        merged = sb("merged", [128, 16, 512], BF16)
        hg = sb("hg", [128, 16, 512], BF16)
        maskT = [sb("maskT%d" % i, [128, 64, 128], U8) for i in range(2)]
        _qi0 = sb("qiT0", [128, 8, 256], BF16)
        qiT = [_qi0, _qi0]
        REG = sb("REG", [128, 16384], F32)
        kix = [sb("kix%d" % i, [128, 512], BF16) for i in range(2)]
        ckvc = [sb("ckvc%d" % i, [128, 4, 256], BF16) for i in range(2)]
        ckvTc = [sb("ckvTc%d" % i, [128, 2, 512], BF16) for i in range(2)]
        Pb = [sb("Pb%d" % i, [128, 512], BF16) for i in range(3)]
        Pm = [sb("Pm%d" % i, [128, 4, 128], BF16) for i in range(3)]
        onT = sb("onT", [128, 2, 512], BF16)
        rden = sb("rden", [128, 512], F32)
        itmp = [sb("itmp%d" % i, [128, 512], F32) for i in range(2)]
        mk = sb("mk", [128, 512], BF16)
        ckv_tok = [sb("ckvtok%d" % i, [128, 256], BF16) for i in range(2)]
        kix_tok = [sb("kixtok%d" % i, [128, 128], BF16) for i in range(2)]
        ckvT_st = sb("ckvT_st", [128, 2, 512], BF16)
        kixT_st = sb("kixT_st", [128, 512], BF16)
        absw = sb("absw", [128, 4, 16], F32)
        sgnw = sb("sgnw", [128, 4, 16], F32)
        sm = sb("sm", [128, 64], F32)
        ft = [sb("ft%d" % i, [128, 512], F32) for i in range(3 if dbg else 2)]
        cbt = sb("cbt", [128, 1024], F32)
        bst = sb("bst", [128, 4, 6], F32)
        ps = st.enter_context(nc.psum_tensor("ps", [128, 8, 512], F32))

        scores = REG[:, 0:8192]
        qT = REG[:, 8192:16384].bitcast(BF16)
        ysub = REG[:, 0:8192]
        wo = [REG[:, 8192 + i * 2048: 8192 + (i + 1) * 2048].bitcast(BF16) for i in range(2)]
        lng = REG[:, 12288:14336]
        lnb = REG[:, 14336:16384]

        def rv(i):
            return REG[:, i * 1024:(i + 1) * 1024]

        gpc = [0]

        def gp():
            b = 6 + gpc[0] % 2
            gpc[0] += 1
            return b

        gwc = [0]

        def gw():
            b = 3 + gwc[0] % 5
            gwc[0] += 1
            return b

        gqc = [0]

        def gq():
            b = 3 + gqc[0] % 3
            gqc[0] += 1
            return b

        wc = [0]

        def wslot():
            s = wc[0] % NW
            wc[0] += 1
            return s

        def psb(bank):
            return ps[:, bank, :]

        S.dma("sp", lambda e: e.dma_start(out=cvec[:].rearrange("p a b -> p (a b)"), in_=cvec_d), "c0", writes=["cvec"])
        S.dma("sp", lambda e: e.dma_start(out=kvg[:], in_=kvg_d), "c1", writes=["kvg"])
        S.dma("sp", lambda e: e.dma_start(out=selt[:], in_=sel_d), "c2", writes=["selt"])
        S.dma("sp", lambda e: e.dma_start(out=pow2[:], in_=pow2_d), "c3", writes=["pow2"])
        S.dma("sp", lambda e: e.dma_start(out=REG[:, 0:8192], in_=biasS_d), "c4", writes=["ibias"], region=("R", "init"))
        S.dma("sp", lambda e: e.dma_start(out=REG[:, 8192:10240], in_=b31_d), "c5", writes=["ib31"], region=("R", "init"))
        for s_ in range(4):
            S.op("dve", lambda e, s_=s_: e.tensor_tensor(out=REG[:, s_ * 2048:(s_ + 1) * 2048], in0=REG[:, s_ * 2048:(s_ + 1) * 2048],
                                                          in1=REG[:, 8192:10240], op=ALU.subtract),
                 reads=["ib31", "ibias"], writes=[("ibs", s_)], region=("R", "init"))
            S.op("dve", lambda e, s_=s_: e.tensor_scalar(out=biasS[:, s_, :, :].rearrange("p a b -> p (a b)"),
                                                          in0=REG[:, s_ * 2048:(s_ + 1) * 2048], scalar1=16.0, scalar2=None, op0=ALU.mult),
                 reads=[("ibs", s_)], writes=["biasS"], region=("R", "init"))
        S.op("pool", lambda e: e.memset(ones[:], 1.0), writes=["ones"])
        S.op("pool", lambda e: e.memset(mhalf[:], -0.5), writes=["mhalf"])
        S.op("pool", lambda e: e.memset(tail[:], 0.0), writes=["tail"])
        S.op("pool", lambda e: e.memset(hcar[:], 0.0), writes=["hcar"])
        S.op("pool", lambda e: e.affine_select(out=ident[:], in_=ones[:], pattern=[[-1, 128]], compare_op=ALU.is_equal,
                                                fill=0.0, base=0, channel_multiplier=1),
             reads=["ones"], writes=["ident"])
        S.op("act", lambda e: e.activation(out=clam[:], in_=cvec[:, 7, :], func=AF.Exp, scale=-1.0), reads=["cvec"], writes=["clam"])
        S.op("act", lambda e: e.activation(out=clam[:], in_=clam[:], func=AF.Ln, bias=1.0, scale=1.0), reads=["clam"], writes=["clam"])
        S.op("dve", lambda e: e.tensor_scalar(out=clam[:], in0=clam[:], scalar1=-8.0, scalar2=None, op0=ALU.mult), reads=["clam"], writes=["clam"])
        S.op("dve", lambda e: e.tensor_scalar(out=hclam[:], in0=clam[:], scalar1=0.5, scalar2=None, op0=ALU.mult), reads=["clam"], writes=["hclam"])
        S.op("dve", lambda e: e.tensor_scalar(out=hba[:], in0=cvec[:, 5, :], scalar1=0.5, scalar2=None, op0=ALU.mult), reads=["cvec"], writes=["hba"])
        S.op("dve", lambda e: e.tensor_scalar(out=hbx[:], in0=cvec[:, 6, :], scalar1=0.5, scalar2=None, op0=ALU.mult), reads=["cvec"], writes=["hbx"])

        def convert(groups):
            for gname, base, n in groups:
                S.dma_group("pool", [(lambda e, c=c: e.dma_start(max_dma_last_dim=4096, out=wfm_b[c], in_=wfm_d[c])) for c in range(base, base + n)],
                            "cv" + gname, [("wb", c) for c in range(base, base + n)])

        def convert_rest():
            convert((("RG", RG, 16), ("WB", WB, 16), ("GB", GB, 16), ("QI", QI, 8), ("QL", QL, 32), ("AG", AG, 16), ("WA", WA, 16), ("GA", GA, 16)))
            S.dma_group("pool", [(lambda e, c=c: e.dma_start(max_dma_last_dim=4096, out=wout_b[c], in_=wout_d[c])) for c in range(8)],
                        "cvWO", [("wob", c) for c in range(8)])

        convert((("XR", XR, 16),))

        def load_w(chunk):
            s = wslot()
            S.dma("pool", lambda e: e.dma_start(out=wst[s][:], in_=wfm_b[chunk]), "w%d" % s, reads=[("wb", chunk)], writes=[("wst", s)])
            return s

        def proj_fm(chunk, rhs_fn, rhs_keys, ncol, bank=None):
            s = load_w(chunk)
            if bank is None:
                bank = gw()

            def f(e):
                ins = None
                for k in range(16):
                    ins = e.matmul(ps[:, bank, 0:ncol], lhsT=wst[s][:, k * 128:(k + 1) * 128], rhs=rhs_fn(k),
                                   start=(k == 0), stop=(k == 15))
                return ins
            S.op("pe", f, reads=[("wst", s)] + list(rhs_keys), writes=[("ps", bank)])
            return bank

        def xT_rhs(k):
            return xT[:, k, :]

        def sig_half(bank, ncol, out_t, key, bias=None, rkeys=()):
            if bias is None:
                S.op("act", lambda e: e.activation(out=out_t, in_=ps[:, bank, 0:ncol], func=AF.Tanh, scale=0.5),
                     reads=[("ps", bank)], writes=[key])
            else:
                S.op("act", lambda e: e.activation(out=out_t, in_=ps[:, bank, 0:ncol], func=AF.Tanh, bias=bias, scale=0.5),
                     reads=[("ps", bank)] + list(rkeys), writes=[key])

        def fullseq(i, half, m):
            RM = ("R", "rnn%d" % i)
            S.dma("pool", lambda e: e.dma_start(max_dma_last_dim=4096, out=xT[:].rearrange("p a b -> p (a b)"), in_=xT_d[i]), "xT", writes=["xT"])
            stage('s1')
            for piece in range(4):
                s = wslot()
                S.dma("pool", lambda e, s=s, piece=piece: e.dma_start(max_dma_last_dim=4096, out=wst[s][:, 0:1600], in_=wtok_d[piece]), "w%d" % s,
                      writes=[("wst", s)])
                for tb in range(4):
                    def f(e, s=s, piece=piece, tb=tb):
                        ins = None
                        for kk in range(4):
                            k = piece * 4 + kk
                            ins = e.matmul(ps[:, tb, 0:400], lhsT=xT[:, k, tb * 128:(tb + 1) * 128],
                                           rhs=wst[s][:, kk * 400:(kk + 1) * 400], start=(k == 0), stop=(k == 15))
                        return ins
                    S.op("pe", f, reads=[("wst", s), "xT"], writes=[("ps", tb)])
            stage('s2')
            for tb in range(4):
                pb = tb % 2
                ss = sm[:, tb:tb + 1]
                rs = sm[:, 4 + tb:5 + tb]
                S.op("act", lambda e, tb=tb, ss=ss: e.activation(out=ft[0][:, 0:256], in_=ps[:, tb, 0:256], func=AF.Square, accum_out=ss),
                     reads=[("ps", tb)], writes=[("ft", 0), ("sm", tb)])
                S.op("dve", lambda e, ss=ss: e.tensor_scalar(out=ss, in0=ss, scalar1=1.0 / 256.0, scalar2=EPS, op0=ALU.mult, op1=ALU.add),
                     reads=[("sm", tb)], writes=[("sm", tb)])
                stage('s3_%d' % tb)
                S.op("pool", lambda e, ss=ss, rs=rs: e.tensor_tensor(out=rs, in0=ss, in1=mhalf[:], op=ALU.pow),
                     reads=[("sm", tb), "mhalf"], writes=[("sm", 4 + tb)])
                S.op("dve", lambda e, tb=tb, rs=rs, pb=pb: e.scalar_tensor_tensor(out=ckv_tok[pb][:], in0=ps[:, tb, 0:256], scalar=rs,
                                                                                 in1=kvg[:], op0=ALU.mult, op1=ALU.mult),
                     reads=[("ps", tb), ("sm", 4 + tb), "kvg"], writes=[("ckvtok", pb)])
                S.op("act", lambda e, tb=tb, pb=pb: e.activation(out=kix_tok[pb][:], in_=ps[:, tb, 256:384], func=AF.Copy),
                     reads=[("ps", tb)], writes=[("kixtok", pb)])
                if half == 1 or True:
                    pass
                if dbg and i == 0 and tb == 0:
                    S.op("act", lambda e: e.activation(out=ft[1][:, 0:256], in_=ckv_tok[0][:], func=AF.Copy),
                         reads=[("ckvtok", 0)], writes=[("ft", 1)])
                    S.dma("sp", lambda e: e.dma_start(out=dbg_out["d_ckv"], in_=ft[1][:, 0:256]), "dbg0", reads=[("ft", 1)], final=True)
                stage('s4_%d' % tb)
                S.dma("sp", lambda e, tb=tb, pb=pb: e.dma_start(out=ckv_d[i * 4 + tb], in_=ckv_tok[pb][:]), "skv%d" % pb,
                      reads=[("ckvtok", pb)], writes=[("ckv_d", i)])
                stage('s5_%d' % tb)
                tbk = gp()
                pv = ps[:, tbk, :].bitcast(BF16)

                def ftr(e, pb=pb, pv=pv):
                    e.transpose(out=pv[:, 0:128], in_=ckv_tok[pb][:, 0:128], identity=ident[:])
                    e.transpose(out=pv[:, 128:256], in_=ckv_tok[pb][:, 128:256], identity=ident[:])
                    return e.transpose(out=pv[:, 256:384], in_=kix_tok[pb][:], identity=ident[:])
                S.op("pe", ftr, reads=[("ckvtok", pb), ("kixtok", pb), "ident"], writes=[("ps", tbk)])
                stage('s6_%d' % tb)
                S.op("act", lambda e, tb=tb, pv=pv: e.activation(out=ckvT_st[:, :, tb * 128:(tb + 1) * 128],
                                                               in_=pv[:, 0:256].rearrange("p (a b) -> p a b", a=2), func=AF.Copy),
                     reads=[("ps", tbk)], writes=["ckvT_st"])
                S.op("dve", lambda e, tb=tb, pv=pv: e.tensor_copy(out=kixT_st[:, tb * 128:(tb + 1) * 128], in_=pv[:, 256:384]),
                     reads=[("ps", tbk)], writes=["kixT_st"])
                stage('s7_%d' % tb)
            stage('s8')
            S.dma("sp", lambda e: e.dma_start(out=ckvT_d[:, :, i * 512:(i + 1) * 512], in_=ckvT_st[:]), "skvT",
                  reads=["ckvT_st"], writes=[("ckvT_d", i)])
            S.dma("sp", lambda e: e.dma_start(out=kixT_d[:, i * 512:(i + 1) * 512], in_=kixT_st[:]), "skix",
                  reads=["kixT_st"], writes=[("kixT_d", i)])

            stage('tokproj')
            for c in range(16):
                par = c % 2
                o = par * 8
                xr = rv(o + 0)[:, 0:515]
                xc = rv(o + 1)[:, 0:512]
                xcb = rv(o + 1)[:, 512:768].bitcast(BF16)
                thr_ = rv(o + 2)[:, 0:512]
                thi = rv(o + 2)[:, 512:1024]
                a_ = rv(o + 3)[:, 0:512]
                a2 = rv(o + 3)[:, 512:1024]
                b_ = rv(o + 4)[:, 0:512]
                hh = rv(o + 4)[:, 512:1024]
                kk_ = lambda n: ("rt", par, n)
                bank = proj_fm(XR + c, xT_rhs, ["xT"], 512)
                S.op("dve", lambda e, xr=xr, c=c: e.tensor_copy(out=xr[:, 0:3], in_=tail[:, c, :]), reads=["tail"], writes=[kk_("xr0")], region=RM)
                S.op("act", lambda e, xr=xr, bank=bank: e.activation(out=xr[:, 3:515], in_=ps[:, bank, :], func=AF.Copy),
                     reads=[("ps", bank)], writes=[kk_("xr")], region=RM)
                S.op("dve", lambda e, xr=xr, c=c: e.tensor_copy(out=tail[:, c, :], in_=xr[:, 512:515]), reads=[kk_("xr")], writes=["tail"], region=RM)
                S.op("dve", lambda e, xr=xr, xc=xc, c=c: e.tensor_scalar(out=xc, in0=xr[:, 0:512], scalar1=cvec[:, 0, c:c + 1],
                                                                         scalar2=cvec[:, 4, c:c + 1], op0=ALU.mult, op1=ALU.add),
                     reads=[kk_("xr"), kk_("xr0"), "cvec"], writes=[kk_("xc")], region=RM)
                for k in range(1, 4):
                    S.op("dve", lambda e, xr=xr, xc=xc, c=c, k=k: e.scalar_tensor_tensor(out=xc, in0=xr[:, k:k + 512], scalar=cvec[:, k, c:c + 1],
                                                                                      in1=xc, op0=ALU.mult, op1=ALU.add),
                         reads=[kk_("xr"), kk_("xr0"), kk_("xc")], writes=[kk_("xc")], region=RM)
                S.op("act", lambda e, xc=xc, xcb=xcb: e.activation(out=xcb, in_=xc, func=AF.Copy), reads=[kk_("xc")], writes=[kk_("xcb")], region=RM)
                gs = c % 2
                S.dma("pool", lambda e, gs=gs, c=c: e.dma_start(max_dma_last_dim=4096, out=wgb[gs][:], in_=wg_d.rearrange("p (g n e) -> p g n e", g=2, n=16)[:, :, c, :]),
                      "wg%d" % gs, writes=[("wgb", gs)])
                br = gq()
                S.op("pe", lambda e, gs=gs, br=br, xcb=xcb: e.matmul(ps[:, br, :], lhsT=wgb[gs][:, 0, :], rhs=xcb, start=True, stop=True),
                     reads=[("wgb", gs), kk_("xcb")], writes=[("ps", br)], region=RM)
                bi = gq()
                S.op("pe", lambda e, gs=gs, bi=bi, xcb=xcb: e.matmul(ps[:, bi, :], lhsT=wgb[gs][:, 1, :], rhs=xcb, start=True, stop=True),
                     reads=[("wgb", gs), kk_("xcb")], writes=[("ps", bi)], region=RM)
                S.op("act", lambda e, br=br, thr_=thr_, c=c: e.activation(out=thr_, in_=ps[:, br, :], func=AF.Tanh, bias=hba[:, c:c + 1], scale=0.5),
                     reads=[("ps", br), "hba"], writes=[kk_("thr")], region=RM)
                S.op("act", lambda e, bi=bi, thi=thi, c=c: e.activation(out=thi, in_=ps[:, bi, :], func=AF.Tanh, bias=hbx[:, c:c + 1], scale=0.5),
                     reads=[("ps", bi), "hbx"], writes=[kk_("thi")], region=RM)
                S.op("act", lambda e, thr_=thr_, a_=a_, c=c: e.activation(out=a_, in_=thr_, func=AF.Exp, bias=hclam[:, c:c + 1], scale=hclam[:, c:c + 1]),
                     reads=[kk_("thr"), "hclam"], writes=[kk_("a")], region=RM)
                S.op("act", lambda e, thr_=thr_, a2=a2, c=c: e.activation(out=a2, in_=thr_, func=AF.Exp, bias=clam[:, c:c + 1], scale=clam[:, c:c + 1]),
                     reads=[kk_("thr"), "clam"], writes=[kk_("a2")], region=RM)
                S.op("act", lambda e, a2=a2: e.activation(out=a2, in_=a2, func=AF.Sqrt, bias=1.0, scale=-1.0),
                     reads=[kk_("a2")], writes=[kk_("a2")], region=RM)
                S.op("dve", lambda e, thi=thi, xc=xc, b_=b_: e.scalar_tensor_tensor(out=b_, in0=thi, scalar=1.0, in1=xc, op0=ALU.add, op1=ALU.mult),
                     reads=[kk_("thi"), kk_("xc")], writes=[kk_("b")], region=RM)
                if i == 0:
                    S.op("pool", lambda e, a2=a2: e.memset(a2[:, 0:1], 1.0), reads=[kk_("a2")], writes=[kk_("a2")], region=RM)
                S.op("dve", lambda e, b_=b_, a2=a2: e.scalar_tensor_tensor(out=b_, in0=b_, scalar=0.5, in1=a2, op0=ALU.mult, op1=ALU.mult),
                     reads=[kk_("b"), kk_("a2")], writes=[kk_("b")], region=RM)
                S.op("dve", lambda e, hh=hh, a_=a_, b_=b_, c=c: e.tensor_tensor_scan(out=hh, data0=a_, data1=b_, initial=hcar[:, c:c + 1],
                                                                                  op0=ALU.mult, op1=ALU.add),
                     reads=[kk_("a"), kk_("b"), "hcar"], writes=[kk_("h")], region=RM)
                S.op("dve", lambda e, hh=hh, c=c: e.tensor_copy(out=hcar[:, c:c + 1], in_=hh[:, 511:512]), reads=[kk_("h")], writes=["hcar"], region=RM)
                if half == 0:
                    S.op("dve", lambda e, hh=hh, c=c: e.tensor_scalar(out=hg[:, c, :], in0=hh, scalar1=selt[:, 0:1], scalar2=None, op0=ALU.mult),
                         reads=[kk_("h"), "selt"], writes=[("hg", c)], region=RM)
                else:
                    S.op("dve", lambda e, hh=hh, c=c: e.scalar_tensor_tensor(out=hg[:, c, :], in0=hh, scalar=selt[:, 1:2], in1=hg[:, c, :],
                                                                          op0=ALU.mult, op1=ALU.add),
                         reads=[kk_("h"), "selt", ("hg", c)], writes=[("hg", c)], region=RM)

        def own(m):
            stage('rnn')
            RA = ("R", "att%d" % m)
            RO = ("R", "out%d" % m)
            S.dma("pool", lambda e: e.dma_start(max_dma_last_dim=4096, out=xT[:].rearrange("p a b -> p (a b)"), in_=xTo_d[m]), "xT", writes=["xT"])
            hgk = [("hg", c) for c in range(16)]

            def branch_b(hook):
              if True:
                for c in range(16):
                    hook(c)
                    bank = proj_fm(RG + c, xT_rhs, ["xT"], 512)
                    t0 = ft[c % 2]
                    S.op("act", lambda e, bank=bank, t0=t0: e.activation(out=t0[:], in_=ps[:, bank, :], func=AF.Tanh, scale=0.5),
                         reads=[("ps", bank)], writes=[("ft", c % 2)])
                    S.op("dve", lambda e, bank=bank, t0=t0: e.scalar_tensor_tensor(out=t0[:], in0=t0[:], scalar=1.0, in1=ps[:, bank, :], op0=ALU.add, op1=ALU.mult),
                         reads=[("ps", bank), ("ft", c % 2)], writes=[("ft", c % 2)])
                    S.op("dve", lambda e, t0=t0, c=c: e.scalar_tensor_tensor(out=hg[:, c, :], in0=t0[:], scalar=0.5, in1=hg[:, c, :], op0=ALU.mult, op1=ALU.mult),
                         reads=[("ft", c % 2), ("hg", c)], writes=[("hg", c)])
                if dbg and m == 0:
                    for c in range(16):
                        S.op("act", lambda e, c=c: e.activation(out=ft[2][:], in_=hg[:, c, :], func=AF.Copy), reads=[("hg", c)], writes=[("ft", 2)])
                        S.dma("sp", lambda e, c=c: e.dma_start(out=dbg_out["d_hg"][:, c * 512:(c + 1) * 512], in_=ft[2][:]), "dbg1", reads=[("ft", 2)], final=True)
                for jc in range(16):
                    hook(16 + jc)
                    ba = proj_fm(WB + jc, lambda k: hg[:, k, :], hgk, 512)
                    bb = proj_fm(GB + jc, xT_rhs, ["xT"], 512)
                    t0 = ft[jc % 2]
                    S.op("act", lambda e, bb=bb, t0=t0: e.activation(out=t0[:], in_=ps[:, bb, :], func=AF.Tanh, scale=0.5),
                         reads=[("ps", bb)], writes=[("ft", jc % 2)])
                    S.op("dve", lambda e, ba=ba, t0=t0: e.scalar_tensor_tensor(out=t0[:], in0=t0[:], scalar=1.0, in1=ps[:, ba, :], op0=ALU.add, op1=ALU.mult),
                         reads=[("ps", ba), ("ft", jc % 2)], writes=[("ft", jc % 2)])
                    S.op("act", lambda e, t0=t0, jc=jc: e.activation(out=merged[:, jc, :], in_=t0[:], func=AF.Copy, scale=0.5),
                         reads=[("ft", jc % 2)], writes=[("mrg", jc)])

            stage('ownb')
            nkb_pair = 8 * m
            qv = qT.rearrange("p (c t) -> p c t", c=32)
            am = sm[:, 8:9]
            lo = sm[:, 9:10]
            mid = sm[:, 10:11]
            cnt = sm[:, 11:12]
            dl = sm[:, 12:13]
            wt = sm[:, 16:16 + NIT]

            def geom(qb):
                nkb = nkb_pair + 5 + qb
                return nkb, nkb * 128, (nkb + 3) // 4

            def proj_qi(hf):
                tsl = slice(hf * 256, (hf + 1) * 256)
                for c in range(8):
                    bank = proj_fm(QI + c, lambda k, tsl=tsl: xT[:, k, tsl], ["xT"], 256)
                    S.op("act", lambda e, bank=bank, c=c, hf=hf: e.activation(out=qiT[hf][:, c, :], in_=ps[:, bank, 0:256], func=AF.Copy),
                         reads=[("ps", bank)], writes=["qiT"])

            def proj_q():
                for c in range(32):
                    bank = proj_fm(QL + c, xT_rhs, ["xT"], 512)
                    if c % 2 == 0:
                        S.op("act", lambda e, bank=bank, c=c: e.activation(out=qT[:, c * 512:(c + 1) * 512], in_=ps[:, bank, :], func=AF.Copy),
                             reads=[("ps", bank)], writes=["qT"], region=RA)
                    else:
                        S.op("dve", lambda e, bank=bank, c=c: e.tensor_copy(out=qT[:, c * 512:(c + 1) * 512], in_=ps[:, bank, :]),
                             reads=[("ps", bank)], writes=["qT"], region=RA)

            def proj_widx(qb):
                s = wslot()
                bw = gp()
                for piece in range(4):
                    if piece > 0:
                        s = wslot()
                    S.dma("pool", lambda e, s=s, piece=piece: e.dma_start(max_dma_last_dim=4096, out=wst[s][:, 0:1600], in_=wtok_d[piece]), "w%d" % s, writes=[("wst", s)])

                    def f(e, s=s, piece=piece, qb=qb, bw=bw):
                        ins = None
                        for kk in range(4):
                            k = piece * 4 + kk
                            ins = e.matmul(ps[:, bw, 0:16], lhsT=xT[:, k, qb * 128:(qb + 1) * 128],
                                           rhs=wst[s][:, kk * 400 + 384:kk * 400 + 400], start=(k == 0), stop=(k == 15))
                        return ins
                    S.op("pe", f, reads=[("wst", s), "xT"], writes=[("ps", bw)])
                S.op("act", lambda e, bw=bw, qb=qb: e.activation(out=absw[:, qb, :], in_=ps[:, bw, 0:16], func=AF.Abs),
                     reads=[("ps", bw)], writes=[("absw", qb)])
                S.op("act", lambda e, bw=bw, qb=qb: e.activation(out=sgnw[:, qb, :], in_=ps[:, bw, 0:16], func=AF.Sign),
                     reads=[("ps", bw)], writes=[("sgnw", qb)])

            def do_idx(qb):
                nkb, nk, nkc = geom(qb)
                hf, q2 = qb // 2, qb % 2
                qsl = slice(q2 * 128, (q2 + 1) * 128)
                for kc in range(nkc):
                    w = min(512, nk - kc * 512)
                    kxs = kc % 2
                    S.dma("sp", lambda e, kc=kc, w=w, kxs=kxs: e.dma_start(out=kix[kxs][:, 0:w], in_=kixT_d[:, kc * 512:kc * 512 + w]), "kix%d" % kxs,
                          reads=[("kixT_d", kc)], writes=[("kix", kxs)])
                    accs = [gp(), gp()]
                    for h in range(16):
                        c = h // 2
                        po = (h % 2) * 64
                        accb = accs[h % 2]
                        zb = gq()
                        S.op("pe", lambda e, zb=zb, c=c, po=po, w=w, qsl=qsl, hf=hf, kxs=kxs: e.matmul(ps[:, zb, 0:w], lhsT=qiT[hf][po:po + 64, c, qsl],
                                                                                                     rhs=kix[kxs][po:po + 64, 0:w], start=True, stop=True),
                             reads=["qiT", ("kix", kxs)], writes=[("ps", zb)])
                        it = itmp[h % 2]
                        S.op("act", lambda e, zb=zb, it=it, h=h, w=w, qb=qb: e.activation(out=it[:, 0:w], in_=ps[:, zb, 0:w], func=AF.Relu,
                                                                                      scale=absw[:, qb, h:h + 1]),
                             reads=[("ps", zb), ("absw", qb)], writes=[("itmp", h % 2)])
                        if h < 2:
                            S.op("dve", lambda e, it=it, accb=accb, w=w, qb=qb, h=h: e.tensor_scalar(out=ps[:, accb, 0:w], in0=it[:, 0:w],
                                                                                                 scalar1=sgnw[:, qb, h:h + 1], scalar2=None, op0=ALU.mult),
                                 reads=[("itmp", h % 2), ("sgnw", qb)], writes=[("ps", accb)])
                        else:
                            S.op("dve", lambda e, it=it, accb=accb, w=w, h=h, qb=qb: e.scalar_tensor_tensor(out=ps[:, accb, 0:w], in0=it[:, 0:w],
                                                                                                      scalar=sgnw[:, qb, h:h + 1], in1=ps[:, accb, 0:w],
                                                                                                      op0=ALU.mult, op1=ALU.add),
                                 reads=[("itmp", h % 2), ("sgnw", qb), ("ps", accb)], writes=[("ps", accb)])
                    S.op("act", lambda e, accs=accs, kc=kc, w=w: e.activation(out=scores[:, kc * 512:kc * 512 + w], in_=ps[:, accs[0], 0:w], func=AF.Copy),
                         reads=[("ps", accs[0])], writes=["scores"], region=RA)
                    S.op("dve", lambda e, accs=accs, kc=kc, w=w: e.tensor_tensor(out=scores[:, kc * 512:kc * 512 + w], in0=scores[:, kc * 512:kc * 512 + w],
                                                                                in1=ps[:, accs[1], 0:w], op=ALU.add),
                         reads=[("ps", accs[1]), "scores"], writes=["scores"], region=RA)
                S.op("dve", lambda e, nk=nk: e.tensor_reduce(out=am, in_=scores[:, 0:nk], axis=AX.X, op=ALU.max, apply_absolute_value=True),
                     reads=["scores"], writes=["am"], region=RA)
                S.op("dve", lambda e: e.tensor_scalar(out=am, in0=am, scalar1=1.0, scalar2=None, op0=ALU.add), reads=["am"], writes=["am"])
                S.op("dve", lambda e: e.tensor_scalar(out=lo, in0=am, scalar1=-1.0, scalar2=None, op0=ALU.mult), reads=["am"], writes=["lo"])
                S.op("dve", lambda e: e.tensor_scalar(out=wt, in0=pow2[:], scalar1=am, scalar2=None, op0=ALU.mult),
                     reads=["am", "pow2"], writes=["wt"])
                S.op("dve", lambda e: e.tensor_tensor(out=mid, in0=lo, in1=wt[:, 0:1], op=ALU.add), reads=["lo", "wt"], writes=["mid"])
                S.dma("sp", lambda e, qb=qb: e.dma_start(out=cbt[:], in_=cbias_d[qb]), "cbt", writes=["cbt"])
                cw = (5 + qb) * 128
                S.op("dve", lambda e, cw=cw: e.tensor_tensor(out=scores[:, nkb_pair * 128:nkb_pair * 128 + cw],
                                                            in0=scores[:, nkb_pair * 128:nkb_pair * 128 + cw], in1=cbt[:, 0:cw], op=ALU.add),
                     reads=["scores", "cbt", "am"], writes=["scores"], region=RA)

            def do_bisect(qb, its):
                nkb, nk, nkc = geom(qb)
                mt = maskT[qb % 2]
                junk = mt[:].rearrange("p a b -> p (a b)")
                for it_ in its:
                    S.op("dve", lambda e, nk=nk, junk=junk: e.tensor_scalar(out=junk[:, 0:nk], in0=scores[:, 0:nk], scalar1=mid, scalar2=None,
                                                                             op0=ALU.is_ge, op1=ALU.add, accum_out=cnt),
                         reads=["scores", "mid"], writes=[("maskT", qb % 2), "cnt"], region=RA)
                    S.op("dve", lambda e: e.tensor_scalar(out=dl, in0=cnt, scalar1=TOPK - 0.5, scalar2=0.5, op0=ALU.is_ge, op1=ALU.subtract),
                         reads=["cnt"], writes=["dl"])
                    S.op("dve", lambda e, it_=it_: e.scalar_tensor_tensor(out=mid, in0=dl, scalar=wt[:, it_:it_ + 1], in1=mid, op0=ALU.mult, op1=ALU.add),
                         reads=["dl", "wt", "mid"], writes=["mid"])

            def do_mask(qb):
                nkb, nk, nkc = geom(qb)
                mt = maskT[qb % 2]
                S.op("dve", lambda e: e.scalar_tensor_tensor(out=lo, in0=wt[:, NIT - 1:NIT], scalar=-0.5, in1=mid, op0=ALU.mult, op1=ALU.add),
                     reads=["wt", "mid"], writes=["lo"])
                if dbg and m == 0:
                    S.dma("sp", lambda e, qb=qb: e.dma_start(out=dbg_out["d_thr"][:, qb:qb + 1], in_=lo, allow_slow_non_contiguous=True), "dbg2", reads=["lo"], final=True)
                    if qb == 3:
                        S.dma("sp", lambda e: e.dma_start(out=dbg_out["d_sc"], in_=scores[:, 0:1024]), "dbg3", reads=["scores"], region=RA, final=True)
                for kc in range(nkc):
                    w = min(512, nk - kc * 512)
                    nb = w // 128
                    S.op("dve", lambda e, kc=kc, w=w: e.tensor_scalar(out=mk[:, 0:w], in0=scores[:, kc * 512:kc * 512 + w], scalar1=lo, scalar2=None,
                                                                       op0=ALU.is_ge),
                         reads=["scores", "lo"], writes=["mk"], region=RA)
                    tbk = gp()
                    pv = ps[:, tbk, :].bitcast(BF16)

                    def ftr(e, pv=pv, nb=nb):
                        ins = None
                        for j in range(nb):
                            ins = e.transpose(out=pv[:, j * 128:(j + 1) * 128], in_=mk[:, j * 128:(j + 1) * 128], identity=ident[:])
                        return ins
                    S.op("pe", ftr, reads=["mk", "ident"], writes=[("ps", tbk)])
                    S.op("act", lambda e, kc=kc, w=w, nb=nb, pv=pv, mt=mt: e.activation(out=mt[:, kc * 4:kc * 4 + nb, :].rearrange("p a b -> p (a b)"),
                                                                                     in_=pv[:, 0:w], func=AF.Copy),
                         reads=[("ps", tbk)], writes=[("maskT", qb % 2)])

            def do_att(qb, hook=None):
                nkb, nk, nkc = geom(qb)
                qsl = slice(qb * 128, (qb + 1) * 128)
                mt = maskT[qb % 2]
                meng = "pool" if hook is not None else "dve"
                for hq in range(4):
                    ws_ = hq % 2
                    S.dma("pool", lambda e, ws_=ws_, hq=hq: e.dma_start(max_dma_last_dim=4096, out=wuvb[ws_][:].rearrange("p a b -> p (a b)"),
                                                                         in_=wuv_d[:, hq * 1024:(hq + 1) * 1024]), "wuv%d" % ws_, writes=[("wuvb", ws_)])
                    if hook is not None:
                        hook(hq)
                    DEP = 2

                    def chunk_dma(kc):
                        w = min(512, nk - kc * 512)
                        nb = w // 128
                        cb_ = kc % 2
                        S.dma("sp", lambda e, kc=kc, nb=nb, cb_=cb_: e.dma_start(out=ckvc[cb_][:, 0:nb, :],
                                                                                  in_=ckv_d[kc * 4:kc * 4 + nb].rearrange("b s d -> s b d")),
                              "ckvc%d" % cb_, reads=[("ckv_d", kc)], writes=[("ckvc", cb_)])
                        S.dma("sp", lambda e, kc=kc, w=w, cb_=cb_: e.dma_start(out=ckvTc[cb_][:, :, 0:w], in_=ckvT_d[:, :, kc * 512:kc * 512 + w]),
                              "ckvTc%d" % cb_, reads=[("ckvT_d", kc)], writes=[("ckvTc", cb_)])

                    chunk_dma(0)
                    if nkc > 1:
                        chunk_dma(1)
                    for idx_ in range(nkb + DEP):
                        if idx_ < nkb:
                            kb = idx_
                            kc, j = kb // 4, kb % 4
                            cb_ = kc % 2
                            rel = kb - nkb_pair
                            slot = {qb - 1: 0, qb: 1, qb + 3: 2, qb + 4: 3}.get(rel)
                            qkb = 3 + (kb % 3)
                            pp = kb % 3

                            def fqk(e, j=j, qkb=qkb, slot=slot, hq=hq, qsl=qsl, cb_=cb_):
                                ins = None
                                for k in range(2):
                                    ins = e.matmul(ps[:, qkb, :], lhsT=ckvTc[cb_][:, k, j * 128:(j + 1) * 128],
                                                   rhs=qv[:, hq * 8 + k:hq * 8 + 8:2, qsl], start=(k == 0), stop=(k == 1 and slot is None))
                                if slot is not None:
                                    ins = e.matmul(ps[:, qkb, :], lhsT=ident[:], rhs=biasS[:, slot, hq * 4:hq * 4 + 4, :], start=False, stop=True)
                                return ins
                            S.op("pe", fqk, reads=[("ckvTc", cb_), "qT", "ident", "biasS"], writes=[("ps", qkb)], region=RA)
                            S.op("act", lambda e, qkb=qkb, pp=pp: e.activation(out=Pb[pp][:], in_=ps[:, qkb, :], func=AF.Exp, scale=1.0 / 16.0),
                                 reads=[("ps", qkb)], writes=[("Pb", pp)])
                            S.op(meng, lambda e, pp=pp, kb=kb, mt=mt: e.tensor_tensor(out=Pm[pp][:], in0=Pb[pp][:].rearrange("p (a b) -> p a b", a=4),
                                                                                   in1=mt[:, kb:kb + 1, :].to_broadcast([128, 4, 128]), op=ALU.mult),
                                 reads=[("Pb", pp), ("maskT", qb % 2)], writes=[("Pm", pp)])
                        if idx_ >= DEP:
                            kb = idx_ - DEP
                            kc, j = kb // 4, kb % 4
                            cb_ = kc % 2
                            pp = kb % 3

                            def fpv(e, j=j, pp=pp, kb=kb, nkb=nkb, cb_=cb_):
                                rhs = Pm[pp][:].rearrange("p a b -> p (a b)")
                                e.matmul(ps[:, 0, :], lhsT=ckvc[cb_][:, j, 0:128], rhs=rhs, start=(kb == 0), stop=(kb == nkb - 1))
                                e.matmul(ps[:, 1, :], lhsT=ckvc[cb_][:, j, 128:256], rhs=rhs, start=(kb == 0), stop=(kb == nkb - 1))
                                return e.matmul(ps[:, 2, :], lhsT=ones[:], rhs=rhs, start=(kb == 0), stop=(kb == nkb - 1))
                            S.op("pe", fpv, reads=[("ckvc", cb_), ("Pm", pp), "ones"], writes=[("ps", 0), ("ps", 1), ("ps", 2)])
                            if j == 3 and kc + 2 < nkc:
                                chunk_dma(kc + 2)
                    S.op("dve", lambda e: e.reciprocal(out=rden[:], in_=ps[:, 2, :]), reads=[("ps", 2)], writes=["rden"])
                    for k in range(2):
                        S.op("dve", lambda e, k=k: e.tensor_tensor(out=onT[:, k, :], in0=ps[:, k, :], in1=rden[:], op=ALU.mult),
                             reads=[("ps", k), "rden"], writes=["onT"])
                    ub = gp()

                    def fuv(e, ub=ub, ws_=ws_):
                        ins = None
                        for hh_ in range(4):
                            for k in range(2):
                                ins = e.matmul(ps[:, ub, hh_ * 128:(hh_ + 1) * 128], lhsT=wuvb[ws_][:, hh_ * 2 + k, :],
                                               rhs=onT[:, k, hh_ * 128:(hh_ + 1) * 128], start=(k == 0), stop=(k == 1))
                        return ins
                    S.op("pe", fuv, reads=["onT", ("wuvb", ws_)], writes=[("ps", ub)])
                    S.op("act", lambda e, ub=ub, hq=hq, qb=qb: e.activation(out=hg[:, hq * 4:hq * 4 + 4, qb * 128:(qb + 1) * 128],
                                                                          in_=ps[:, ub, :].rearrange("p (a b) -> p a b", a=4), func=AF.Copy),
                         reads=[("ps", ub)], writes=[("hg", hq * 4 + x) for x in range(4)])

            def bis_hook(qb):
                per = (NIT + 3) // 4

                def hk(hq):
                    do_bisect(qb, range(hq * per, min(NIT, (hq + 1) * per)))
                return hk

            proj_qi(0)
            proj_widx(0)
            proj_widx(1)
            do_idx(0)
            branch_b(lambda step: do_bisect(0, range(step, step + 1)) if step < NIT else None)
            proj_q()
            do_mask(0)
            do_idx(1)
            do_att(0, bis_hook(1))
            do_mask(1)
            proj_qi(1)
            proj_widx(2)
            proj_widx(3)
            do_idx(2)
            do_att(1, bis_hook(2))
            do_mask(2)
            do_idx(3)
            do_att(2, bis_hook(3))
            do_mask(3)
            do_att(3)
            stage('attn')
            for c in range(16):
                bank = proj_fm(AG + c, xT_rhs, ["xT"], 512)
                t0 = ft[c % 2]
                S.op("act", lambda e, bank=bank, t0=t0: e.activation(out=t0[:], in_=ps[:, bank, :], func=AF.Tanh, scale=0.5),
                     reads=[("ps", bank)], writes=[("ft", c % 2)])
                S.op("dve", lambda e, bank=bank, t0=t0: e.scalar_tensor_tensor(out=t0[:], in0=t0[:], scalar=1.0, in1=ps[:, bank, :], op0=ALU.add, op1=ALU.mult),
                     reads=[("ps", bank), ("ft", c % 2)], writes=[("ft", c % 2)])
                S.op("dve", lambda e, t0=t0, c=c: e.scalar_tensor_tensor(out=hg[:, c, :], in0=t0[:], scalar=0.5, in1=hg[:, c, :], op0=ALU.mult, op1=ALU.mult),
                     reads=[("ft", c % 2), ("hg", c)], writes=[("hg", c)])
            if dbg and m == 0:
                for c in range(16):
                    S.op("act", lambda e, c=c: e.activation(out=ft[2][:], in_=hg[:, c, :], func=AF.Copy), reads=[("hg", c)], writes=[("ft", 2)])
                    S.dma("sp", lambda e, c=c: e.dma_start(out=dbg_out["d_ga"][:, c * 512:(c + 1) * 512], in_=ft[2][:]), "dbg4", reads=[("ft", 2)], final=True)
            stage('gate')
            for jc in range(16):
                ba = proj_fm(WA + jc, lambda k: hg[:, k, :], hgk, 512)
                bb = proj_fm(GA + jc, xT_rhs, ["xT"], 512)
                t0 = ft[jc % 2]
                S.op("act", lambda e, bb=bb, t0=t0: e.activation(out=t0[:], in_=ps[:, bb, :], func=AF.Tanh, scale=0.5),
                     reads=[("ps", bb)], writes=[("ft", jc % 2)])
                S.op("dve", lambda e, ba=ba, t0=t0: e.scalar_tensor_tensor(out=t0[:], in0=t0[:], scalar=1.0, in1=ps[:, ba, :], op0=ALU.add, op1=ALU.mult),
                     reads=[("ps", ba), ("ft", jc % 2)], writes=[("ft", jc % 2)])
                S.op("dve", lambda e, t0=t0, jc=jc: e.scalar_tensor_tensor(out=merged[:, jc, :], in0=t0[:], scalar=0.5, in1=merged[:, jc, :], op0=ALU.mult, op1=ALU.add),
                     reads=[("ft", jc % 2), ("mrg", jc)], writes=[("mrg", jc)])
            if dbg and m == 0:
                for c in range(16):
                    S.op("act", lambda e, c=c: e.activation(out=ft[2][:], in_=merged[:, c, :], func=AF.Copy), reads=[("mrg", c)], writes=[("ft", 2)])
                    S.dma("sp", lambda e, c=c: e.dma_start(out=dbg_out["d_mrg"][:, c * 512:(c + 1) * 512], in_=ft[2][:]), "dbg5", reads=[("ft", 2)], final=True)
            stage('brancha')
            mk_ = [("mrg", c) for c in range(16)]
            S.dma("sp", lambda e: e.dma_start(out=lng, in_=lng_d), "lng", writes=["lng"], region=RO)
            S.dma("sp", lambda e: e.dma_start(out=lnb, in_=lnb_d), "lnb", writes=["lnb"], region=RO)
            for tb in range(4):
                S.dma("sp", lambda e, tb=tb: e.dma_start(out=ysub[:, tb * 2048:(tb + 1) * 2048], in_=xtok_d[m, tb]), "xtk%d" % tb,
                      writes=[("ysub", tb)], region=RO)
            for jj in range(8):
                wsl = jj % 2
                S.dma("pool", lambda e, wsl=wsl, jj=jj: e.dma_start(out=wo[wsl], in_=wout_b[jj]), "wo%d" % wsl, reads=[("wob", jj)], writes=[("wo", wsl)], region=RO)
                for tb in range(4):
                    ob = gw()

                    def fo(e, ob=ob, wsl=wsl, tb=tb):
                        ins = None
                        for k in range(16):
                            ins = e.matmul(ps[:, ob, 0:256], lhsT=merged[:, k, tb * 128:(tb + 1) * 128], rhs=wo[wsl][:, k * 256:(k + 1) * 256],
                                           start=(k == 0), stop=(k == 15))
                        return ins
                    S.op("pe", fo, reads=mk_ + [("wo", wsl)], writes=[("ps", ob)], region=RO)
                    ys = ysub[:, tb * 2048 + jj * 256: tb * 2048 + (jj + 1) * 256]
                    S.op("dve", lambda e, ob=ob, ys=ys: e.scalar_tensor_tensor(out=ys, in0=ys, scalar=ALPHA, in1=ps[:, ob, 0:256], op0=ALU.mult, op1=ALU.add),
                         reads=[("ps", ob), ("ysub", tb)], writes=[("ysub", tb)], region=RO)
            for tb in range(4):
                yv = ysub[:, tb * 2048:(tb + 1) * 2048]
                mv = sm[:, 40:42]
                rs = sm[:, 42:43]
                for q in range(4):
                    S.op("dve", lambda e, yv=yv, q=q: e.bn_stats(out=bst[:, q, :], in_=yv[:, q * 512:(q + 1) * 512]),
                         reads=[("ysub", tb)], writes=[("bst", q)], region=RO)
                S.op("dve", lambda e, mv=mv: e.bn_aggr(out=mv, in_=bst[:].rearrange("p a b -> p (a b)")),
                     reads=[("bst", q) for q in range(4)], writes=["mv"])
                S.op("dve", lambda e, mv=mv, rs=rs: e.tensor_scalar(out=rs, in0=mv[:, 1:2], scalar1=EPS, scalar2=None, op0=ALU.add), reads=["mv"], writes=["rs"])
                S.op("pool", lambda e, rs=rs: e.tensor_tensor(out=rs, in0=rs, in1=mhalf[:], op=ALU.pow), reads=["rs", "mhalf"], writes=["rs"])
                S.op("dve", lambda e, yv=yv, mv=mv, rs=rs: e.tensor_scalar(out=yv, in0=yv, scalar1=mv[:, 0:1], scalar2=rs, op0=ALU.subtract, op1=ALU.mult),
                     reads=[("ysub", tb), "mv", "rs"], writes=[("ysub", tb)], region=RO)
                S.op("dve", lambda e, yv=yv: e.tensor_tensor(out=yv, in0=yv, in1=lng, op=ALU.mult), reads=[("ysub", tb), "lng"], writes=[("ysub", tb)], region=RO)
                S.op("dve", lambda e, yv=yv: e.tensor_tensor(out=yv, in0=yv, in1=lnb, op=ALU.add), reads=[("ysub", tb), "lnb"], writes=[("ysub", tb)], region=RO)
                S.dma("sp", lambda e, tb=tb, yv=yv: e.dma_start(out=y_d[m, tb], in_=yv), "yo%d" % tb, reads=[("ysub", tb)], region=RO, final=True)

        try:
            stage('init')
            for m in range(n_pairs):
                fullseq(2 * m, 0, m)
                if m == 0:
                    convert_rest()
                fullseq(2 * m + 1, 1, m)
                own(m)
        except _Stop:
            pass
        S.emit()
    return nc


def _t5_bucket_np(n):
    n = np.maximum(n, 0)
    nf = np.maximum(n, 1).astype(np.float32)
    large = 16 + (np.log(nf / np.float32(16)) / np.float32(math.log(128 / 16)) * np.float32(16)).astype(np.int32)
    large = np.minimum(large, 31)
    return np.where(n < 16, n, large)


def _fm(w):
    n = w.shape[1] // 128
    return np.ascontiguousarray(w.reshape(16, 128, n, 128).transpose(2, 1, 0, 3)).reshape(n, 128, 16 * 128)


def _vec(v):
    return np.ascontiguousarray(v.reshape(16, 128).T)


def make_inputs(x, w_in, kv_norm_g, w_uv, w_branch_a, conv_w, conv_b, w_gate_a, b_gate_a,
                w_gate_x, b_gate_x, lru_lambda, w_branch_b, rel_bias, w_out, ln_g, ln_b):
    w_in = w_in[0]
    chunks = []
    for base, n in ((QL, 32), (AG, 16), (QI, 8), (XR, 16), (RG, 16), (GA, 16), (GB, 16)):
        c0 = COL[base]
        chunks.append(_fm(w_in[:, c0:c0 + n * 128]))
    chunks.append(_fm(w_branch_a[0]))
    chunks.append(_fm(w_branch_b[0]))
    wfm = np.concatenate(chunks, axis=0)
    tokcols = np.concatenate([np.arange(4096, 4352), np.arange(7424, 7488), np.arange(7424, 7488), np.arange(7488, 7504)])
    wt = w_in[:, tokcols]
    wtok = np.ascontiguousarray(wt.reshape(4, 4, 128, 400).transpose(0, 2, 1, 3)).reshape(4, 128, 1600)
    wout = np.ascontiguousarray(w_out[0].reshape(16, 128, 8, 256).transpose(2, 1, 0, 3)).reshape(8, 128, 16 * 256)
    wuv = np.ascontiguousarray(w_uv[0].reshape(16, 2, 128, 128).transpose(2, 0, 1, 3)).reshape(128, 32 * 128)
    wg = np.ascontiguousarray(np.stack([w_gate_a[0], w_gate_x[0]], 0).transpose(2, 0, 1, 3)).reshape(128, 32 * 128)
    cvec = np.stack([_vec(conv_w[0, 0]), _vec(conv_w[0, 1]), _vec(conv_w[0, 2]), _vec(conv_w[0, 3]), _vec(conv_b[0]),
                     _vec(b_gate_a[0]), _vec(b_gate_x[0]), _vec(lru_lambda[0])], axis=1).reshape(128, 128)
    kvg = np.ascontiguousarray(np.broadcast_to(kv_norm_g[0][None, :], (128, 256)))
    lng = np.ascontiguousarray(np.broadcast_to(ln_g[0][None, :], (128, D)))
    lnb = np.ascontiguousarray(np.broadcast_to(ln_b[0][None, :], (128, D)))
    s_ = np.arange(128)[:, None]
    t_ = np.arange(128)[None, :]
    diag = rel_bias[_t5_bucket_np(t_ - s_)]
    prev = rel_bias[_t5_bucket_np(128 + t_ - s_)]
    diag = np.ascontiguousarray(diag.transpose(0, 2, 1))
    prev = np.ascontiguousarray(prev.transpose(0, 2, 1))
    b31 = np.ascontiguousarray(np.broadcast_to(rel_bias[31][None, :, None], (128, 16, 128)))
    pow2 = np.ascontiguousarray(np.broadcast_to((2.0 ** -np.arange(NIT, dtype=np.float64)).astype(np.float32)[None, :], (128, NIT)))
    shared = dict(wfm=wfm, wtok=wtok, wout=wout, wuv=wuv, wg=wg, cvec=np.ascontiguousarray(cvec), kvg=kvg, lng=lng, lnb=lnb,
                  b31=b31.reshape(128, 2048), pow2=pow2)
    in_maps = []
    for core in range(8):
        b, par = core // 2, core % 2
        xb = x[b]
        xT = np.ascontiguousarray(xb.reshape(NT, 512, 16, 128).transpose(0, 3, 2, 1)).reshape(NT, 128, 16 * 512)
        xTo = np.ascontiguousarray(xT[par::2])
        xtok = np.ascontiguousarray(xb.reshape(NP, 2, 4, 128, D)[:, par])
        sel = np.zeros((128, 2), np.float32)
        sel[:, par] = 1.0
        cb = np.zeros((4, 128, 1024), np.float32)
        for qb in range(4):
            tg = par * 512 + qb * 128 + np.arange(128)[:, None]
            sg = np.arange(1024)[None, :]
            cb[qb] = np.where(sg <= tg, 0.0, NEG)
        if par == 0:
            slots = [prev, diag, b31, b31]
        else:
            slots = [b31, b31, prev, diag]
        biasS = np.ascontiguousarray(np.stack(slots, axis=1)).reshape(128, 4 * 16 * 128).astype(np.float32)
        d = dict(shared)
        d.update(xT=xT, xTo=xTo, xtok=xtok, sel=sel, cbias=cb, biasS=biasS)
        in_maps.append(d)
    return in_maps


def kernel(**inputs):
    inputs = {k: np.asarray(v) for k, v in inputs.items()}
    in_maps = make_inputs(**inputs)
    nc = build_nc()
    res = run_bass_kernel_spmd(nc, in_maps, core_ids=list(range(8)))
    out = np.empty((4, T, D), np.float32)
    for core in range(8):
        b, par = core // 2, core % 2
        yv = res.results[core]["y"].reshape(NP, 512, D)
        out.reshape(4, NP, 2, 512, D)[b, :, par] = yv
    return out
```

```python
import math
from contextlib import ExitStack

import numpy as np
import concourse.bass as bass
import concourse.mybir as mybir
from concourse.bass_utils import run_bass_kernel_spmd

F32 = mybir.dt.float32
BF16 = mybir.dt.bfloat16
U8 = mybir.dt.uint8
AF = mybir.ActivationFunctionType
ALU = mybir.AluOpType
AX = mybir.AxisListType

D = 2048
T = 8192
NT = 16
NP = 8
TOPK = 256
NIT = 20
ALPHA = 2.0 ** 0.25
EPS = 1e-5
NEG = -1.0e30
QL, AG, QI, XR, RG, GA, GB, WA, WB = 0, 32, 48, 56, 72, 88, 104, 120, 136
NCH = 152
COL = {QL: 0, AG: 4352, QI: 6400, XR: 7504, RG: 9552, GA: 11600, GB: 13648}


class _Op:
    __slots__ = ("eng", "fn", "waits", "idx", "inc", "semval", "dma_sem", "dma_val")

    def __init__(self, eng, fn):
        self.eng = eng
        self.fn = fn
        self.waits = []
        self.idx = -1
        self.inc = False
        self.semval = 0
        self.dma_sem = None
        self.dma_val = 0


class Sched:
    ENG = ("pe", "act", "dve", "pool", "sp")
    SAME = ("act", "dve", "pool")

    def __init__(self, nc, stack):
        self.nc = nc
        self.stack = stack
        self.ops = {e: [] for e in self.ENG}
        self.last_w = {}
        self.readers = {}
        self.dma_sems = {}
        self.final_tokens = []
        self.reg_mode = {}
        self.reg_cur = {}
        self.reg_fence = {}

    def _dma_sem(self, name):
        if name not in self.dma_sems:
            h = self.stack.enter_context(self.nc.semaphore("d_" + name))
            self.dma_sems[name] = [h, 0]
        return self.dma_sems[name]

    @staticmethod
    def _compress(toks):
        be = {}
        bd = {}
        for t in toks:
            if t[0] == "e":
                p = t[1]
                if p.eng not in be or be[p.eng].idx < p.idx:
                    be[p.eng] = p
            else:
                if bd.get(t[1], 0) < t[2]:
                    bd[t[1]] = t[2]
        return [("e", p) for p in be.values()] + [("d", k, v) for k, v in bd.items()]

    def _deps(self, reads, writes, region):
        toks = []
        for k in reads:
            t = self.last_w.get(k)
            if t is not None:
                toks.append(t)
        for k in writes:
            t = self.last_w.get(k)
            if t is not None:
                toks.append(t)
            toks.extend(self.readers.get(k, ()))
        if region is not None:
            r, mode = region
            if self.reg_mode.get(r) != mode:
                self.reg_fence[r] = self._compress(self.reg_cur.get(r, []) + self.reg_fence.get(r, []))
                self.reg_cur[r] = []
                self.reg_mode[r] = mode
            toks.extend(self.reg_fence.get(r, ()))
        return self._compress(toks)

    def _commit(self, tok, reads, writes, region):
        for k in reads:
            lst = self.readers.setdefault(k, [])
            lst.append(tok)
            if len(lst) > 12:
                self.readers[k] = self._compress(lst)
        for k in writes:
            self.last_w[k] = tok
            self.readers[k] = []
        if region is not None:
            lst = self.reg_cur.setdefault(region[0], [])
            lst.append(tok)
            if len(lst) > 12:
                self.reg_cur[region[0]] = self._compress(lst)

    @staticmethod
    def _excl(reads, writes):
        r = [k for k in reads if not (isinstance(k, tuple) and k[0] == "ps")]
        w = list(writes) + [k for k in reads if isinstance(k, tuple) and k[0] == "ps" and k not in writes]
        return r, w

    def op(self, eng, fn, reads=(), writes=(), region=None):
        reads, writes = self._excl(reads, writes)
        o = _Op(eng, fn)
        o.idx = len(self.ops[eng])
        o.waits = self._deps(reads, writes, region)
        self.ops[eng].append(o)
        self._commit(("e", o), reads, writes, region)
        return o

    def dma(self, eng, fn, sem, reads=(), writes=(), region=None, final=False):
        o = _Op(eng, fn)
        o.idx = len(self.ops[eng])
        o.waits = self._deps(reads, writes, region)
        s = self._dma_sem(sem)
        s[1] += 16
        o.dma_sem = sem
        o.dma_val = s[1]
        self.ops[eng].append(o)
        tok = ("d", sem, s[1])
        self._commit(tok, reads, writes, region)
        if final:
            self.final_tokens.append(tok)
        return o

    def dma_group(self, eng, fns, sem, keys):
        s = self._dma_sem(sem)
        for fn in fns:
            o = _Op(eng, fn)
            o.idx = len(self.ops[eng])
            o.waits = []
            s[1] += 16
            o.dma_sem = sem
            o.dma_val = s[1]
            self.ops[eng].append(o)
        tok = ("d", sem, s[1])
        for k in keys:
            self.last_w[k] = tok
            self.readers[k] = []

    def emit(self, final_eng="sp"):
        nc = self.nc
        fo = _Op(final_eng, None)
        fo.idx = len(self.ops[final_eng])
        fo.waits = list(self.final_tokens) + [("d", k, v[1]) for k, v in self.dma_sems.items()]
        for e in self.ENG:
            if e != final_eng and self.ops[e]:
                fo.waits.append(("e", self.ops[e][-1]))
        self.ops[final_eng].append(fo)
        for e in self.ENG:
            for o in self.ops[e]:
                for t in o.waits:
                    if t[0] == "e":
                        p = t[1]
                        if p.eng != o.eng or o.eng in self.SAME:
                            p.inc = True
        esem = {}
        for e in self.ENG:
            esem[e] = self.stack.enter_context(nc.semaphore("e_" + e))
            c = 0
            for o in self.ops[e]:
                if o.inc:
                    c += 1
                    o.semval = c
        block = self.stack.enter_context(nc.Block())

        def run(e, eng):
            seen_e = {x: -1 for x in self.ENG}
            seen_d = {}
            for o in self.ops[e]:
                for t in o.waits:
                    if t[0] == "e":
                        p = t[1]
                        if p.eng == e and e not in self.SAME:
                            continue
                        if p.idx > seen_e[p.eng]:
                            eng.wait_ge(esem[p.eng], p.semval)
                            seen_e[p.eng] = p.idx
                    else:
                        if seen_d.get(t[1], 0) < t[2]:
                            eng.wait_ge(self.dma_sems[t[1]][0], t[2])
                            seen_d[t[1]] = t[2]
                if o.fn is None:
                    continue
                ins = o.fn(eng)
                if o.dma_sem is not None:
                    ins.then_inc(self.dma_sems[o.dma_sem][0], 16)
                elif o.inc:
                    ins.then_inc(esem[e], 1)

        @block.tensor
        def _(eng):
            run("pe", eng)

        @block.scalar
        def _(eng):
            run("act", eng)

        @block.vector
        def _(eng):
            run("dve", eng)

        @block.gpsimd
        def _(eng):
            run("pool", eng)

        @block.sync
        def _(eng):
            run("sp", eng)


class _Stop(Exception):
    pass


def build_nc(n_pairs=NP, dbg=False, stop=None):
    def stage(name):
        if stop is not None and name == stop:
            raise _Stop()

    nc = bass.Bass("TRN2", target_bir_lowering=False)

    def din(name, shape, dt=F32):
        return nc.dram_tensor(name, list(shape), dt, kind="ExternalInput").ap()

    xT_d = din("xT", [NT, 128, 16 * 512])
    xTo_d = din("xTo", [NP, 128, 16 * 512])
    xtok_d = din("xtok", [NP, 4, 128, D])
    wfm_d = din("wfm", [NCH, 128, 16 * 128])
    wtok_d = din("wtok", [4, 128, 4 * 400])
    wout_d = din("wout", [8, 128, 16 * 256])
    wuv_d = din("wuv", [128, 32 * 128])
    wg_d = din("wg", [128, 32 * 128])
    cvec_d = din("cvec", [128, 8 * 16])
    kvg_d = din("kvg", [128, 256])
    lng_d = din("lng", [128, D])
    lnb_d = din("lnb", [128, D])
    biasS_d = din("biasS", [128, 4 * 16 * 128])
    b31_d = din("b31", [128, 16 * 128])
    sel_d = din("sel", [128, 2])
    cbias_d = din("cbias", [4, 128, 1024])
    pow2_d = din("pow2", [128, NIT])
    y_d = nc.dram_tensor("y", [NP, 4, 128, D], F32, kind="ExternalOutput").ap()
    ckv_d = nc.dram_tensor("ckv_s", [64, 128, 256], BF16, kind="Internal").ap()
    ckvT_d = nc.dram_tensor("ckvT_s", [128, 2, T], BF16, kind="Internal").ap()
    kixT_d = nc.dram_tensor("kixT_s", [128, T], BF16, kind="Internal").ap()
    wfm_b = nc.dram_tensor("wfm_bf", [NCH, 128, 16 * 128], BF16, kind="Internal").ap()
    wout_b = nc.dram_tensor("wout_bf", [8, 128, 16 * 256], BF16, kind="Internal").ap()
    dbg_out = {}
    if dbg:
        for nm, shp in (("d_ckv", [128, 256]), ("d_sc", [128, 1024]), ("d_thr", [128, 4]),
                        ("d_hg", [128, 16 * 512]), ("d_mrg", [128, 16 * 512]), ("d_ga", [128, 16 * 512])):
            dbg_out[nm] = nc.dram_tensor(nm, shp, F32, kind="ExternalOutput").ap()

    st = ExitStack()
    with st:
        def sb(name, shape, dt):
            return st.enter_context(nc.sbuf_tensor("s_" + name, list(shape), dt))

        S = Sched(nc, st)
        biasS = sb("biasS", [128, 4, 16, 128], BF16)
        cvec = sb("cvec", [128, 8, 16], F32)
        clam = sb("clam", [128, 16], F32)
        hclam = sb("hclam", [128, 16], F32)
        hba = sb("hba", [128, 16], F32)
        hbx = sb("hbx", [128, 16], F32)
        kvg = sb("kvg", [128, 256], F32)
        selt = sb("selt", [128, 2], F32)
        pow2 = sb("pow2", [128, NIT], F32)
        mhalf = sb("mhalf", [128, 1], F32)
        ident = sb("ident", [128, 128], BF16)
        ones = sb("ones", [128, 128], BF16)
        tail = sb("tail", [128, 16, 3], F32)
        hcar = sb("hcar", [128, 16], F32)
        xT = sb("xT", [128, 16, 512], BF16)
        NW = 2 if dbg else 3
        wst = [sb("wst%d" % i, [128, 16 * 128], BF16) for i in range(NW)]
        wgb = [sb("wgb%d" % i, [128, 2, 128], BF16) for i in range(2)]
        wuvb = [sb("wuvb%d" % i, [128, 8, 128], BF16) for i in range(2)]
        merged = sb("merged", [128, 16, 512], BF16)
        hg = sb("hg", [128, 16, 512], BF16)
        maskT = [sb("maskT%d" % i, [128, 64, 128], U8) for i in range(2)]
        _qi0 = sb("qiT0", [128, 8, 256], BF16)
        qiT = [_qi0, _qi0]
        REG = sb("REG", [128, 16384], F32)
        kix = [sb("kix%d" % i, [128, 512], BF16) for i in range(2)]
        ckvc = [sb("ckvc%d" % i, [128, 4, 256], BF16) for i in range(2)]
        ckvTc = [sb("ckvTc%d" % i, [128, 2, 512], BF16) for i in range(2)]
        Pb = [sb("Pb%d" % i, [128, 512], BF16) for i in range(3)]
        Pm = [sb("Pm%d" % i, [128, 4, 128], BF16) for i in range(3)]
        onT = sb("onT", [128, 2, 512], BF16)
        rden = sb("rden", [128, 512], F32)
        itmp = [sb("itmp%d" % i, [128, 512], F32) for i in range(2)]
        mk = sb("mk", [128, 512], BF16)
        ckv_tok = [sb("ckvtok%d" % i, [128, 256], BF16) for i in range(2)]
        kix_tok = [sb("kixtok%d" % i, [128, 128], BF16) for i in range(2)]
        ckvT_st = sb("ckvT_st", [128, 2, 512], BF16)
        kixT_st = sb("kixT_st", [128, 512], BF16)
        absw = sb("absw", [128, 4, 16], F32)
        sgnw = sb("sgnw", [128, 4, 16], F32)
        sm = sb("sm", [128, 64], F32)
        ft = [sb("ft%d" % i, [128, 512], F32) for i in range(3 if dbg else 2)]
        cbt = sb("cbt", [128, 1024], F32)
        bst = sb("bst", [128, 4, 6], F32)
        ps = st.enter_context(nc.psum_tensor("ps", [128, 8, 512], F32))

        scores = REG[:, 0:8192]
        qT = REG[:, 8192:16384].bitcast(BF16)
        ysub = REG[:, 0:8192]
        wo = [REG[:, 8192 + i * 2048: 8192 + (i + 1) * 2048].bitcast(BF16) for i in range(2)]
        lng = REG[:, 12288:14336]
        lnb = REG[:, 14336:16384]

        def rv(i):
            return REG[:, i * 1024:(i + 1) * 1024]

        gpc = [0]

        def gp():
            b = 6 + gpc[0] % 2
            gpc[0] += 1
            return b

        gwc = [0]

        def gw():
            b = 3 + gwc[0] % 5
            gwc[0] += 1
            return b

        gqc = [0]

        def gq():
            b = 3 + gqc[0] % 3
            gqc[0] += 1
            return b

        wc = [0]

        def wslot():
            s = wc[0] % NW
            wc[0] += 1
            return s

        def psb(bank):
            return ps[:, bank, :]

        S.dma("sp", lambda e: e.dma_start(out=cvec[:].rearrange("p a b -> p (a b)"), in_=cvec_d), "c0", writes=["cvec"])
        S.dma("sp", lambda e: e.dma_start(out=kvg[:], in_=kvg_d), "c1", writes=["kvg"])
        S.dma("sp", lambda e: e.dma_start(out=selt[:], in_=sel_d), "c2", writes=["selt"])
        S.dma("sp", lambda e: e.dma_start(out=pow2[:], in_=pow2_d), "c3", writes=["pow2"])
        S.dma("sp", lambda e: e.dma_start(out=REG[:, 0:8192], in_=biasS_d), "c4", writes=["ibias"], region=("R", "init"))
        S.dma("sp", lambda e: e.dma_start(out=REG[:, 8192:10240], in_=b31_d), "c5", writes=["ib31"], region=("R", "init"))
        for s_ in range(4):
            S.op("dve", lambda e, s_=s_: e.tensor_tensor(out=REG[:, s_ * 2048:(s_ + 1) * 2048], in0=REG[:, s_ * 2048:(s_ + 1) * 2048],
                                                          in1=REG[:, 8192:10240], op=ALU.subtract),
                 reads=["ib31", "ibias"], writes=[("ibs", s_)], region=("R", "init"))
            S.op("dve", lambda e, s_=s_: e.tensor_scalar(out=biasS[:, s_, :, :].rearrange("p a b -> p (a b)"),
                                                          in0=REG[:, s_ * 2048:(s_ + 1) * 2048], scalar1=16.0, scalar2=None, op0=ALU.mult),
                 reads=[("ibs", s_)], writes=["biasS"], region=("R", "init"))
        S.op("pool", lambda e: e.memset(ones[:], 1.0), writes=["ones"])
        S.op("pool", lambda e: e.memset(mhalf[:], -0.5), writes=["mhalf"])
        S.op("pool", lambda e: e.memset(tail[:], 0.0), writes=["tail"])
        S.op("pool", lambda e: e.memset(hcar[:], 0.0), writes=["hcar"])
        S.op("pool", lambda e: e.affine_select(out=ident[:], in_=ones[:], pattern=[[-1, 128]], compare_op=ALU.is_equal,
                                                fill=0.0, base=0, channel_multiplier=1),
             reads=["ones"], writes=["ident"])
        S.op("act", lambda e: e.activation(out=clam[:], in_=cvec[:, 7, :], func=AF.Exp, scale=-1.0), reads=["cvec"], writes=["clam"])
        S.op("act", lambda e: e.activation(out=clam[:], in_=clam[:], func=AF.Ln, bias=1.0, scale=1.0), reads=["clam"], writes=["clam"])
        S.op("dve", lambda e: e.tensor_scalar(out=clam[:], in0=clam[:], scalar1=-8.0, scalar2=None, op0=ALU.mult), reads=["clam"], writes=["clam"])
        S.op("dve", lambda e: e.tensor_scalar(out=hclam[:], in0=clam[:], scalar1=0.5, scalar2=None, op0=ALU.mult), reads=["clam"], writes=["hclam"])
        S.op("dve", lambda e: e.tensor_scalar(out=hba[:], in0=cvec[:, 5, :], scalar1=0.5, scalar2=None, op0=ALU.mult), reads=["cvec"], writes=["hba"])
        S.op("dve", lambda e: e.tensor_scalar(out=hbx[:], in0=cvec[:, 6, :], scalar1=0.5, scalar2=None, op0=ALU.mult), reads=["cvec"], writes=["hbx"])

        def convert(groups):
            for gname, base, n in groups:
                S.dma_group("pool", [(lambda e, c=c: e.dma_start(max_dma_last_dim=4096, out=wfm_b[c], in_=wfm_d[c])) for c in range(base, base + n)],
                            "cv" + gname, [("wb", c) for c in range(base, base + n)])

        def convert_rest():
            convert((("RG", RG, 16), ("WB", WB, 16), ("GB", GB, 16), ("QI", QI, 8), ("QL", QL, 32), ("AG", AG, 16), ("WA", WA, 16), ("GA", GA, 16)))
            S.dma_group("pool", [(lambda e, c=c: e.dma_start(max_dma_last_dim=4096, out=wout_b[c], in_=wout_d[c])) for c in range(8)],
                        "cvWO", [("wob", c) for c in range(8)])

        convert((("XR", XR, 16),))

        def load_w(chunk):
            s = wslot()
            S.dma("pool", lambda e: e.dma_start(out=wst[s][:], in_=wfm_b[chunk]), "w%d" % s, reads=[("wb", chunk)], writes=[("wst", s)])
            return s

        def proj_fm(chunk, rhs_fn, rhs_keys, ncol, bank=None):
            s = load_w(chunk)
            if bank is None:
                bank = gw()

            def f(e):
                ins = None
                for k in range(16):
                    ins = e.matmul(ps[:, bank, 0:ncol], lhsT=wst[s][:, k * 128:(k + 1) * 128], rhs=rhs_fn(k),
                                   start=(k == 0), stop=(k == 15))
                return ins
            S.op("pe", f, reads=[("wst", s)] + list(rhs_keys), writes=[("ps", bank)])
            return bank

        def xT_rhs(k):
            return xT[:, k, :]

        def sig_half(bank, ncol, out_t, key, bias=None, rkeys=()):
            if bias is None:
                S.op("act", lambda e: e.activation(out=out_t, in_=ps[:, bank, 0:ncol], func=AF.Tanh, scale=0.5),
                     reads=[("ps", bank)], writes=[key])
            else:
                S.op("act", lambda e: e.activation(out=out_t, in_=ps[:, bank, 0:ncol], func=AF.Tanh, bias=bias, scale=0.5),
                     reads=[("ps", bank)] + list(rkeys), writes=[key])

        def fullseq(i, half, m):
            RM = ("R", "rnn%d" % i)
            S.dma("pool", lambda e: e.dma_start(max_dma_last_dim=4096, out=xT[:].rearrange("p a b -> p (a b)"), in_=xT_d[i]), "xT", writes=["xT"])
            stage('s1')
            for piece in range(4):
                s = wslot()
                S.dma("pool", lambda e, s=s, piece=piece: e.dma_start(max_dma_last_dim=4096, out=wst[s][:, 0:1600], in_=wtok_d[piece]), "w%d" % s,
                      writes=[("wst", s)])
                for tb in range(4):
                    def f(e, s=s, piece=piece, tb=tb):
                        ins = None
                        for kk in range(4):
                            k = piece * 4 + kk
                            ins = e.matmul(ps[:, tb, 0:400], lhsT=xT[:, k, tb * 128:(tb + 1) * 128],
                                           rhs=wst[s][:, kk * 400:(kk + 1) * 400], start=(k == 0), stop=(k == 15))
                        return ins
                    S.op("pe", f, reads=[("wst", s), "xT"], writes=[("ps", tb)])
            stage('s2')
            for tb in range(4):
                pb = tb % 2
                ss = sm[:, tb:tb + 1]
                rs = sm[:, 4 + tb:5 + tb]
                S.op("act", lambda e, tb=tb, ss=ss: e.activation(out=ft[0][:, 0:256], in_=ps[:, tb, 0:256], func=AF.Square, accum_out=ss),
                     reads=[("ps", tb)], writes=[("ft", 0), ("sm", tb)])
                S.op("dve", lambda e, ss=ss: e.tensor_scalar(out=ss, in0=ss, scalar1=1.0 / 256.0, scalar2=EPS, op0=ALU.mult, op1=ALU.add),
                     reads=[("sm", tb)], writes=[("sm", tb)])
                stage('s3_%d' % tb)
                S.op("pool", lambda e, ss=ss, rs=rs: e.tensor_tensor(out=rs, in0=ss, in1=mhalf[:], op=ALU.pow),
                     reads=[("sm", tb), "mhalf"], writes=[("sm", 4 + tb)])
                S.op("dve", lambda e, tb=tb, rs=rs, pb=pb: e.scalar_tensor_tensor(out=ckv_tok[pb][:], in0=ps[:, tb, 0:256], scalar=rs,
                                                                                 in1=kvg[:], op0=ALU.mult, op1=ALU.mult),
                     reads=[("ps", tb), ("sm", 4 + tb), "kvg"], writes=[("ckvtok", pb)])
                S.op("act", lambda e, tb=tb, pb=pb: e.activation(out=kix_tok[pb][:], in_=ps[:, tb, 256:384], func=AF.Copy),
                     reads=[("ps", tb)], writes=[("kixtok", pb)])
                if half == 1 or True:
                    pass
                if dbg and i == 0 and tb == 0:
                    S.op("act", lambda e: e.activation(out=ft[1][:, 0:256], in_=ckv_tok[0][:], func=AF.Copy),
                         reads=[("ckvtok", 0)], writes=[("ft", 1)])
                    S.dma("sp", lambda e: e.dma_start(out=dbg_out["d_ckv"], in_=ft[1][:, 0:256]), "dbg0", reads=[("ft", 1)], final=True)
                stage('s4_%d' % tb)
                S.dma("sp", lambda e, tb=tb, pb=pb: e.dma_start(out=ckv_d[i * 4 + tb], in_=ckv_tok[pb][:]), "skv%d" % pb,
                      reads=[("ckvtok", pb)], writes=[("ckv_d", i)])
                stage('s5_%d' % tb)
                tbk = gp()
                pv = ps[:, tbk, :].bitcast(BF16)

                def ftr(e, pb=pb, pv=pv):
                    e.transpose(out=pv[:, 0:128], in_=ckv_tok[pb][:, 0:128], identity=ident[:])
                    e.transpose(out=pv[:, 128:256], in_=ckv_tok[pb][:, 128:256], identity=ident[:])
                    return e.transpose(out=pv[:, 256:384], in_=kix_tok[pb][:], identity=ident[:])
                S.op("pe", ftr, reads=[("ckvtok", pb), ("kixtok", pb), "ident"], writes=[("ps", tbk)])
                stage('s6_%d' % tb)
                S.op("act", lambda e, tb=tb, pv=pv: e.activation(out=ckvT_st[:, :, tb * 128:(tb + 1) * 128],
                                                               in_=pv[:, 0:256].rearrange("p (a b) -> p a b", a=2), func=AF.Copy),
                     reads=[("ps", tbk)], writes=["ckvT_st"])
                S.op("dve", lambda e, tb=tb, pv=pv: e.tensor_copy(out=kixT_st[:, tb * 128:(tb + 1) * 128], in_=pv[:, 256:384]),
                     reads=[("ps", tbk)], writes=["kixT_st"])
                stage('s7_%d' % tb)
            stage('s8')
            S.dma("sp", lambda e: e.dma_start(out=ckvT_d[:, :, i * 512:(i + 1) * 512], in_=ckvT_st[:]), "skvT",
                  reads=["ckvT_st"], writes=[("ckvT_d", i)])
            S.dma("sp", lambda e: e.dma_start(out=kixT_d[:, i * 512:(i + 1) * 512], in_=kixT_st[:]), "skix",
                  reads=["kixT_st"], writes=[("kixT_d", i)])

            stage('tokproj')
            for c in range(16):
                par = c % 3
                o = par * 5
                xr = rv(o + 0)[:, 0:515]
                xc = rv(o + 1)[:, 0:512]
                xcb = rv(o + 1)[:, 512:768].bitcast(BF16)
                thr_ = rv(o + 2)[:, 0:512]
                thi = rv(o + 2)[:, 512:1024]
                a_ = rv(o + 3)[:, 0:512]
                a2 = rv(o + 3)[:, 512:1024]
                b_ = rv(o + 4)[:, 0:512]
                hh = rv(o + 4)[:, 512:1024]
                kk_ = lambda n: ("rt", par, n)
                bank = proj_fm(XR + c, xT_rhs, ["xT"], 512)
                S.op("dve", lambda e, xr=xr, c=c: e.tensor_copy(out=xr[:, 0:3], in_=tail[:, c, :]), reads=["tail"], writes=[kk_("xr0")], region=RM)
                S.op("act", lambda e, xr=xr, bank=bank: e.activation(out=xr[:, 3:515], in_=ps[:, bank, :], func=AF.Copy),
                     reads=[("ps", bank)], writes=[kk_("xr")], region=RM)
                S.op("dve", lambda e, xr=xr, c=c: e.tensor_copy(out=tail[:, c, :], in_=xr[:, 512:515]), reads=[kk_("xr")], writes=["tail"], region=RM)
                S.op("dve", lambda e, xr=xr, xc=xc, c=c: e.tensor_scalar(out=xc, in0=xr[:, 0:512], scalar1=cvec[:, 0, c:c + 1],
                                                                         scalar2=cvec[:, 4, c:c + 1], op0=ALU.mult, op1=ALU.add),
                     reads=[kk_("xr"), kk_("xr0"), "cvec"], writes=[kk_("xc")], region=RM)
                for k in range(1, 4):
                    S.op("dve", lambda e, xr=xr, xc=xc, c=c, k=k: e.scalar_tensor_tensor(out=xc, in0=xr[:, k:k + 512], scalar=cvec[:, k, c:c + 1],
                                                                                      in1=xc, op0=ALU.mult, op1=ALU.add),
                         reads=[kk_("xr"), kk_("xr0"), kk_("xc")], writes=[kk_("xc")], region=RM)
                S.op("act", lambda e, xc=xc, xcb=xcb: e.activation(out=xcb, in_=xc, func=AF.Copy), reads=[kk_("xc")], writes=[kk_("xcb")], region=RM)
                gs = c % 2
                S.dma("pool", lambda e, gs=gs, c=c: e.dma_start(max_dma_last_dim=4096, out=wgb[gs][:], in_=wg_d.rearrange("p (g n e) -> p g n e", g=2, n=16)[:, :, c, :]),
                      "wg%d" % gs, writes=[("wgb", gs)])
                br = gq()
                S.op("pe", lambda e, gs=gs, br=br, xcb=xcb: e.matmul(ps[:, br, :], lhsT=wgb[gs][:, 0, :], rhs=xcb, start=True, stop=True),
                     reads=[("wgb", gs), kk_("xcb")], writes=[("ps", br)], region=RM)
                bi = gq()
                S.op("pe", lambda e, gs=gs, bi=bi, xcb=xcb: e.matmul(ps[:, bi, :], lhsT=wgb[gs][:, 1, :], rhs=xcb, start=True, stop=True),
                     reads=[("wgb", gs), kk_("xcb")], writes=[("ps", bi)], region=RM)
                S.op("act", lambda e, br=br, thr_=thr_, c=c: e.activation(out=thr_, in_=ps[:, br, :], func=AF.Tanh, bias=hba[:, c:c + 1], scale=0.5),
                     reads=[("ps", br), "hba"], writes=[kk_("thr")], region=RM)
                S.op("act", lambda e, bi=bi, thi=thi, c=c: e.activation(out=thi, in_=ps[:, bi, :], func=AF.Tanh, bias=hbx[:, c:c + 1], scale=0.5),
                     reads=[("ps", bi), "hbx"], writes=[kk_("thi")], region=RM)
                S.op("act", lambda e, thr_=thr_, a_=a_, c=c: e.activation(out=a_, in_=thr_, func=AF.Exp, bias=hclam[:, c:c + 1], scale=hclam[:, c:c + 1]),
                     reads=[kk_("thr"), "hclam"], writes=[kk_("a")], region=RM)
                S.op("act", lambda e, thr_=thr_, a2=a2, c=c: e.activation(out=a2, in_=thr_, func=AF.Exp, bias=clam[:, c:c + 1], scale=clam[:, c:c + 1]),
                     reads=[kk_("thr"), "clam"], writes=[kk_("a2")], region=RM)
                S.op("act", lambda e, a2=a2: e.activation(out=a2, in_=a2, func=AF.Sqrt, bias=1.0, scale=-1.0),
                     reads=[kk_("a2")], writes=[kk_("a2")], region=RM)
                S.op("dve", lambda e, thi=thi, xc=xc, b_=b_: e.scalar_tensor_tensor(out=b_, in0=thi, scalar=1.0, in1=xc, op0=ALU.add, op1=ALU.mult),
                     reads=[kk_("thi"), kk_("xc")], writes=[kk_("b")], region=RM)
                if i == 0:
                    S.op("pool", lambda e, a2=a2: e.memset(a2[:, 0:1], 1.0), reads=[kk_("a2")], writes=[kk_("a2")], region=RM)
                S.op("dve", lambda e, b_=b_, a2=a2: e.scalar_tensor_tensor(out=b_, in0=b_, scalar=0.5, in1=a2, op0=ALU.mult, op1=ALU.mult),
                     reads=[kk_("b"), kk_("a2")], writes=[kk_("b")], region=RM)
                S.op("dve", lambda e, hh=hh, a_=a_, b_=b_, c=c: e.tensor_tensor_scan(out=hh, data0=a_, data1=b_, initial=hcar[:, c:c + 1],
                                                                                  op0=ALU.mult, op1=ALU.add),
                     reads=[kk_("a"), kk_("b"), "hcar"], writes=[kk_("h")], region=RM)
                S.op("dve", lambda e, hh=hh, c=c: e.tensor_copy(out=hcar[:, c:c + 1], in_=hh[:, 511:512]), reads=[kk_("h")], writes=["hcar"], region=RM)
                if half == 0:
                    S.op("dve", lambda e, hh=hh, c=c: e.tensor_scalar(out=hg[:, c, :], in0=hh, scalar1=selt[:, 0:1], scalar2=None, op0=ALU.mult),
                         reads=[kk_("h"), "selt"], writes=[("hg", c)], region=RM)
                else:
                    S.op("dve", lambda e, hh=hh, c=c: e.scalar_tensor_tensor(out=hg[:, c, :], in0=hh, scalar=selt[:, 1:2], in1=hg[:, c, :],
                                                                          op0=ALU.mult, op1=ALU.add),
                         reads=[kk_("h"), "selt", ("hg", c)], writes=[("hg", c)], region=RM)

        def own(m):
            stage('rnn')
            RA = ("R", "att%d" % m)
            RO = ("R", "out%d" % m)
            S.dma("pool", lambda e: e.dma_start(max_dma_last_dim=4096, out=xT[:].rearrange("p a b -> p (a b)"), in_=xTo_d[m]), "xT", writes=["xT"])
            hgk = [("hg", c) for c in range(16)]

            def branch_b(hook):
              if True:
                for c in range(16):
                    hook(c)
                    bank = proj_fm(RG + c, xT_rhs, ["xT"], 512)
                    t0 = ft[c % 2]
                    S.op("act", lambda e, bank=bank, t0=t0: e.activation(out=t0[:], in_=ps[:, bank, :], func=AF.Tanh, scale=0.5),
                         reads=[("ps", bank)], writes=[("ft", c % 2)])
                    S.op("dve", lambda e, bank=bank, t0=t0: e.scalar_tensor_tensor(out=t0[:], in0=t0[:], scalar=1.0, in1=ps[:, bank, :], op0=ALU.add, op1=ALU.mult),
                         reads=[("ps", bank), ("ft", c % 2)], writes=[("ft", c % 2)])
                    S.op("dve", lambda e, t0=t0, c=c: e.scalar_tensor_tensor(out=hg[:, c, :], in0=t0[:], scalar=0.5, in1=hg[:, c, :], op0=ALU.mult, op1=ALU.mult),
                         reads=[("ft", c % 2), ("hg", c)], writes=[("hg", c)])
                if dbg and m == 0:
                    for c in range(16):
                        S.op("act", lambda e, c=c: e.activation(out=ft[2][:], in_=hg[:, c, :], func=AF.Copy), reads=[("hg", c)], writes=[("ft", 2)])
                        S.dma("sp", lambda e, c=c: e.dma_start(out=dbg_out["d_hg"][:, c * 512:(c + 1) * 512], in_=ft[2][:]), "dbg1", reads=[("ft", 2)], final=True)
                for jc in range(16):
                    hook(16 + jc)
                    ba = proj_fm(WB + jc, lambda k: hg[:, k, :], hgk, 512)
                    bb = proj_fm(GB + jc, xT_rhs, ["xT"], 512)
                    t0 = ft[jc % 2]
                    S.op("act", lambda e, bb=bb, t0=t0: e.activation(out=t0[:], in_=ps[:, bb, :], func=AF.Tanh, scale=0.5),
                         reads=[("ps", bb)], writes=[("ft", jc % 2)])
                    S.op("dve", lambda e, ba=ba, t0=t0: e.scalar_tensor_tensor(out=t0[:], in0=t0[:], scalar=1.0, in1=ps[:, ba, :], op0=ALU.add, op1=ALU.mult),
                         reads=[("ps", ba), ("ft", jc % 2)], writes=[("ft", jc % 2)])
                    S.op("act", lambda e, t0=t0, jc=jc: e.activation(out=merged[:, jc, :], in_=t0[:], func=AF.Copy, scale=0.5),
                         reads=[("ft", jc % 2)], writes=[("mrg", jc)])

            stage('ownb')
            nkb_pair = 8 * m
            qv = qT.rearrange("p (c t) -> p c t", c=32)
            am = sm[:, 8:9]
            lo = sm[:, 9:10]
            mid = sm[:, 10:11]
            cnt = sm[:, 11:12]
            dl = sm[:, 12:13]
            wt = sm[:, 16:16 + NIT]

            def geom(qb):
                nkb = nkb_pair + 5 + qb
                return nkb, nkb * 128, (nkb + 3) // 4

            def proj_qi(hf):
                tsl = slice(hf * 256, (hf + 1) * 256)
                for c in range(8):
                    bank = proj_fm(QI + c, lambda k, tsl=tsl: xT[:, k, tsl], ["xT"], 256)
                    S.op("act", lambda e, bank=bank, c=c, hf=hf: e.activation(out=qiT[hf][:, c, :], in_=ps[:, bank, 0:256], func=AF.Copy),
                         reads=[("ps", bank)], writes=["qiT"])

            def proj_q():
                for c in range(32):
                    bank = proj_fm(QL + c, xT_rhs, ["xT"], 512)
                    if c % 2 == 0:
                        S.op("act", lambda e, bank=bank, c=c: e.activation(out=qT[:, c * 512:(c + 1) * 512], in_=ps[:, bank, :], func=AF.Copy),
                             reads=[("ps", bank)], writes=["qT"], region=RA)
                    else:
                        S.op("dve", lambda e, bank=bank, c=c: e.tensor_copy(out=qT[:, c * 512:(c + 1) * 512], in_=ps[:, bank, :]),
                             reads=[("ps", bank)], writes=["qT"], region=RA)

            def proj_widx(qb):
                s = wslot()
                bw = gp()
                for piece in range(4):
                    if piece > 0:
                        s = wslot()
                    S.dma("pool", lambda e, s=s, piece=piece: e.dma_start(max_dma_last_dim=4096, out=wst[s][:, 0:1600], in_=wtok_d[piece]), "w%d" % s, writes=[("wst", s)])

                    def f(e, s=s, piece=piece, qb=qb, bw=bw):
                        ins = None
                        for kk in range(4):
                            k = piece * 4 + kk
                            ins = e.matmul(ps[:, bw, 0:16], lhsT=xT[:, k, qb * 128:(qb + 1) * 128],
                                           rhs=wst[s][:, kk * 400 + 384:kk * 400 + 400], start=(k == 0), stop=(k == 15))
                        return ins
                    S.op("pe", f, reads=[("wst", s), "xT"], writes=[("ps", bw)])
                S.op("act", lambda e, bw=bw, qb=qb: e.activation(out=absw[:, qb, :], in_=ps[:, bw, 0:16], func=AF.Abs),
                     reads=[("ps", bw)], writes=[("absw", qb)])
                S.op("act", lambda e, bw=bw, qb=qb: e.activation(out=sgnw[:, qb, :], in_=ps[:, bw, 0:16], func=AF.Sign),
                     reads=[("ps", bw)], writes=[("sgnw", qb)])

            def do_idx(qb):
                nkb, nk, nkc = geom(qb)
                hf, q2 = qb // 2, qb % 2
                qsl = slice(q2 * 128, (q2 + 1) * 128)
                for kc in range(nkc):
                    w = min(512, nk - kc * 512)
                    kxs = kc % 2
                    S.dma("sp", lambda e, kc=kc, w=w, kxs=kxs: e.dma_start(out=kix[kxs][:, 0:w], in_=kixT_d[:, kc * 512:kc * 512 + w]), "kix%d" % kxs,
                          reads=[("kixT_d", kc)], writes=[("kix", kxs)])
                    accs = [gp(), gp()]
                    for h in range(16):
                        c = h // 2
                        po = (h % 2) * 64
                        accb = accs[h % 2]
                        zb = gq()
                        S.op("pe", lambda e, zb=zb, c=c, po=po, w=w, qsl=qsl, hf=hf, kxs=kxs: e.matmul(ps[:, zb, 0:w], lhsT=qiT[hf][po:po + 64, c, qsl],
                                                                                                     rhs=kix[kxs][po:po + 64, 0:w], start=True, stop=True),
                             reads=["qiT", ("kix", kxs)], writes=[("ps", zb)])
                        it, ikey = ((itmp[0], ("itmp", 0)), (itmp[1], ("itmp", 1)), (ft[0], ("ft", 0)), (ft[1], ("ft", 1)))[h % 4]
                        S.op("act", lambda e, zb=zb, it=it, h=h, w=w, qb=qb: e.activation(out=it[:, 0:w], in_=ps[:, zb, 0:w], func=AF.Relu,
                                                                                      scale=absw[:, qb, h:h + 1]),
                             reads=[("ps", zb), ("absw", qb)], writes=[ikey])
                        if h < 2:
                            S.op("dve", lambda e, it=it, accb=accb, w=w, qb=qb, h=h: e.tensor_scalar(out=ps[:, accb, 0:w], in0=it[:, 0:w],
                                                                                                 scalar1=sgnw[:, qb, h:h + 1], scalar2=None, op0=ALU.mult),
                                 reads=[ikey, ("sgnw", qb)], writes=[("ps", accb)])
                        else:
                            S.op("dve", lambda e, it=it, accb=accb, w=w, h=h, qb=qb: e.scalar_tensor_tensor(out=ps[:, accb, 0:w], in0=it[:, 0:w],
                                                                                                      scalar=sgnw[:, qb, h:h + 1], in1=ps[:, accb, 0:w],
                                                                                                      op0=ALU.mult, op1=ALU.add),
                                 reads=[ikey, ("sgnw", qb), ("ps", accb)], writes=[("ps", accb)])
                    S.op("act", lambda e, accs=accs, kc=kc, w=w: e.activation(out=scores[:, kc * 512:kc * 512 + w], in_=ps[:, accs[0], 0:w], func=AF.Copy),
                         reads=[("ps", accs[0])], writes=["scores"], region=RA)
                    S.op("dve", lambda e, accs=accs, kc=kc, w=w: e.tensor_tensor(out=scores[:, kc * 512:kc * 512 + w], in0=scores[:, kc * 512:kc * 512 + w],
                                                                                in1=ps[:, accs[1], 0:w], op=ALU.add),
                         reads=[("ps", accs[1]), "scores"], writes=["scores"], region=RA)
                S.op("dve", lambda e, nk=nk: e.tensor_reduce(out=am, in_=scores[:, 0:nk], axis=AX.X, op=ALU.max, apply_absolute_value=True),
                     reads=["scores"], writes=["am"], region=RA)
                S.op("dve", lambda e: e.tensor_scalar(out=am, in0=am, scalar1=1.0, scalar2=None, op0=ALU.add), reads=["am"], writes=["am"])
                S.op("dve", lambda e: e.tensor_scalar(out=lo, in0=am, scalar1=-1.0, scalar2=None, op0=ALU.mult), reads=["am"], writes=["lo"])
                S.op("dve", lambda e: e.tensor_scalar(out=wt, in0=pow2[:], scalar1=am, scalar2=None, op0=ALU.mult),
                     reads=["am", "pow2"], writes=["wt"])
                S.op("dve", lambda e: e.tensor_tensor(out=mid, in0=lo, in1=wt[:, 0:1], op=ALU.add), reads=["lo", "wt"], writes=["mid"])
                S.dma("sp", lambda e, qb=qb: e.dma_start(out=cbt[:], in_=cbias_d[qb]), "cbt", writes=["cbt"])
                cw = (5 + qb) * 128
                S.op("dve", lambda e, cw=cw: e.tensor_tensor(out=scores[:, nkb_pair * 128:nkb_pair * 128 + cw],
                                                            in0=scores[:, nkb_pair * 128:nkb_pair * 128 + cw], in1=cbt[:, 0:cw], op=ALU.add),
                     reads=["scores", "cbt", "am"], writes=["scores"], region=RA)

            def do_bisect(qb, its):
                nkb, nk, nkc = geom(qb)
                mt = maskT[qb % 2]
                junk = mt[:].rearrange("p a b -> p (a b)")
                for it_ in its:
                    S.op("dve", lambda e, nk=nk, junk=junk: e.tensor_scalar(out=junk[:, 0:nk], in0=scores[:, 0:nk], scalar1=mid, scalar2=None,
                                                                             op0=ALU.is_ge, op1=ALU.add, accum_out=cnt),
                         reads=["scores", "mid"], writes=[("maskT", qb % 2), "cnt"], region=RA)
                    S.op("dve", lambda e: e.tensor_scalar(out=dl, in0=cnt, scalar1=TOPK - 0.5, scalar2=0.5, op0=ALU.is_ge, op1=ALU.subtract),
                         reads=["cnt"], writes=["dl"])
                    S.op("dve", lambda e, it_=it_: e.scalar_tensor_tensor(out=mid, in0=dl, scalar=wt[:, it_:it_ + 1], in1=mid, op0=ALU.mult, op1=ALU.add),
                         reads=["dl", "wt", "mid"], writes=["mid"])

            def do_mask(qb):
                nkb, nk, nkc = geom(qb)
                mt = maskT[qb % 2]
                S.op("dve", lambda e: e.scalar_tensor_tensor(out=lo, in0=wt[:, NIT - 1:NIT], scalar=-0.5, in1=mid, op0=ALU.mult, op1=ALU.add),
                     reads=["wt", "mid"], writes=["lo"])
                if dbg and m == 0:
                    S.dma("sp", lambda e, qb=qb: e.dma_start(out=dbg_out["d_thr"][:, qb:qb + 1], in_=lo, allow_slow_non_contiguous=True), "dbg2", reads=["lo"], final=True)
                    if qb == 3:
                        S.dma("sp", lambda e: e.dma_start(out=dbg_out["d_sc"], in_=scores[:, 0:1024]), "dbg3", reads=["scores"], region=RA, final=True)
                for kc in range(nkc):
                    w = min(512, nk - kc * 512)
                    nb = w // 128
                    S.op("dve", lambda e, kc=kc, w=w: e.tensor_scalar(out=mk[:, 0:w], in0=scores[:, kc * 512:kc * 512 + w], scalar1=lo, scalar2=None,
                                                                       op0=ALU.is_ge),
                         reads=["scores", "lo"], writes=["mk"], region=RA)
                    tbk = gp()
                    pv = ps[:, tbk, :].bitcast(BF16)

                    def ftr(e, pv=pv, nb=nb):
                        ins = None
                        for j in range(nb):
                            ins = e.transpose(out=pv[:, j * 128:(j + 1) * 128], in_=mk[:, j * 128:(j + 1) * 128], identity=ident[:])
                        return ins
                    S.op("pe", ftr, reads=["mk", "ident"], writes=[("ps", tbk)])
                    S.op("act", lambda e, kc=kc, w=w, nb=nb, pv=pv, mt=mt: e.activation(out=mt[:, kc * 4:kc * 4 + nb, :].rearrange("p a b -> p (a b)"),
                                                                                     in_=pv[:, 0:w], func=AF.Copy),
                         reads=[("ps", tbk)], writes=[("maskT", qb % 2)])

            def do_att(qb, hook=None):
                nkb, nk, nkc = geom(qb)
                qsl = slice(qb * 128, (qb + 1) * 128)
                mt = maskT[qb % 2]
                meng = "pool" if hook is not None else "dve"
                for hq in range(4):
                    ws_ = hq % 2
                    S.dma("pool", lambda e, ws_=ws_, hq=hq: e.dma_start(max_dma_last_dim=4096, out=wuvb[ws_][:].rearrange("p a b -> p (a b)"),
                                                                         in_=wuv_d[:, hq * 1024:(hq + 1) * 1024]), "wuv%d" % ws_, writes=[("wuvb", ws_)])
                    if hook is not None:
                        hook(hq)
                    DEP = 2

                    def chunk_dma(kc):
                        w = min(512, nk - kc * 512)
                        nb = w // 128
                        cb_ = kc % 2
                        S.dma("sp", lambda e, kc=kc, nb=nb, cb_=cb_: e.dma_start(out=ckvc[cb_][:, 0:nb, :],
                                                                                  in_=ckv_d[kc * 4:kc * 4 + nb].rearrange("b s d -> s b d")),
                              "ckvc%d" % cb_, reads=[("ckv_d", kc)], writes=[("ckvc", cb_)])
                        S.dma("sp", lambda e, kc=kc, w=w, cb_=cb_: e.dma_start(out=ckvTc[cb_][:, :, 0:w], in_=ckvT_d[:, :, kc * 512:kc * 512 + w]),
                              "ckvTc%d" % cb_, reads=[("ckvT_d", kc)], writes=[("ckvTc", cb_)])

                    chunk_dma(0)
                    if nkc > 1:
                        chunk_dma(1)
                    for idx_ in range(nkb + DEP):
                        if idx_ < nkb:
                            kb = idx_
                            kc, j = kb // 4, kb % 4
                            cb_ = kc % 2
                            rel = kb - nkb_pair
                            slot = {qb - 1: 0, qb: 1, qb + 3: 2, qb + 4: 3}.get(rel)
                            qkb = 3 + (kb % 3)
                            pp = kb % 3

                            def fqk(e, j=j, qkb=qkb, slot=slot, hq=hq, qsl=qsl, cb_=cb_):
                                ins = None
                                for k in range(2):
                                    ins = e.matmul(ps[:, qkb, :], lhsT=ckvTc[cb_][:, k, j * 128:(j + 1) * 128],
                                                   rhs=qv[:, hq * 8 + k:hq * 8 + 8:2, qsl], start=(k == 0), stop=(k == 1 and slot is None))
                                if slot is not None:
                                    ins = e.matmul(ps[:, qkb, :], lhsT=ident[:], rhs=biasS[:, slot, hq * 4:hq * 4 + 4, :], start=False, stop=True)
                                return ins
                            S.op("pe", fqk, reads=[("ckvTc", cb_), "qT", "ident", "biasS"], writes=[("ps", qkb)], region=RA)
                            S.op("act", lambda e, qkb=qkb, pp=pp: e.activation(out=Pb[pp][:], in_=ps[:, qkb, :], func=AF.Exp, scale=1.0 / 16.0),
                                 reads=[("ps", qkb)], writes=[("Pb", pp)])
                            S.op(meng, lambda e, pp=pp, kb=kb, mt=mt: e.tensor_tensor(out=Pm[pp][:], in0=Pb[pp][:].rearrange("p (a b) -> p a b", a=4),
                                                                                   in1=mt[:, kb:kb + 1, :].to_broadcast([128, 4, 128]), op=ALU.mult),
                                 reads=[("Pb", pp), ("maskT", qb % 2)], writes=[("Pm", pp)])
                        if idx_ >= DEP:
                            kb = idx_ - DEP
                            kc, j = kb // 4, kb % 4
                            cb_ = kc % 2
                            pp = kb % 3

                            def fpv(e, j=j, pp=pp, kb=kb, nkb=nkb, cb_=cb_):
                                rhs = Pm[pp][:].rearrange("p a b -> p (a b)")
                                e.matmul(ps[:, 0, :], lhsT=ckvc[cb_][:, j, 0:128], rhs=rhs, start=(kb == 0), stop=(kb == nkb - 1))
                                e.matmul(ps[:, 1, :], lhsT=ckvc[cb_][:, j, 128:256], rhs=rhs, start=(kb == 0), stop=(kb == nkb - 1))
                                return e.matmul(ps[:, 2, :], lhsT=ones[:], rhs=rhs, start=(kb == 0), stop=(kb == nkb - 1))
                            S.op("pe", fpv, reads=[("ckvc", cb_), ("Pm", pp), "ones"], writes=[("ps", 0), ("ps", 1), ("ps", 2)])
                            if j == 3 and kc + 2 < nkc:
                                chunk_dma(kc + 2)
                    S.op("dve", lambda e: e.reciprocal(out=rden[:], in_=ps[:, 2, :]), reads=[("ps", 2)], writes=["rden"])
                    for k in range(2):
                        S.op("dve", lambda e, k=k: e.tensor_tensor(out=onT[:, k, :], in0=ps[:, k, :], in1=rden[:], op=ALU.mult),
                             reads=[("ps", k), "rden"], writes=["onT"])
                    ub = gp()

                    def fuv(e, ub=ub, ws_=ws_):
                        ins = None
                        for hh_ in range(4):
                            for k in range(2):
                                ins = e.matmul(ps[:, ub, hh_ * 128:(hh_ + 1) * 128], lhsT=wuvb[ws_][:, hh_ * 2 + k, :],
                                               rhs=onT[:, k, hh_ * 128:(hh_ + 1) * 128], start=(k == 0), stop=(k == 1))
                        return ins
                    S.op("pe", fuv, reads=["onT", ("wuvb", ws_)], writes=[("ps", ub)])
                    S.op("act", lambda e, ub=ub, hq=hq, qb=qb: e.activation(out=hg[:, hq * 4:hq * 4 + 4, qb * 128:(qb + 1) * 128],
                                                                          in_=ps[:, ub, :].rearrange("p (a b) -> p a b", a=4), func=AF.Copy),
                         reads=[("ps", ub)], writes=[("hg", hq * 4 + x) for x in range(4)])

            def bis_hook(qb):
                per = (NIT + 3) // 4

                def hk(hq):
                    do_bisect(qb, range(hq * per, min(NIT, (hq + 1) * per)))
                return hk

            proj_qi(0)
            proj_widx(0)
            proj_widx(1)
            do_idx(0)
            branch_b(lambda step: do_bisect(0, range(step, step + 1)) if step < NIT else None)
            proj_q()
            do_mask(0)
            do_idx(1)
            do_att(0, bis_hook(1))
            do_mask(1)
            proj_qi(1)
            proj_widx(2)
            proj_widx(3)
            do_idx(2)
            do_att(1, bis_hook(2))
            do_mask(2)
            do_idx(3)
            do_att(2, bis_hook(3))
            do_mask(3)
            do_att(3)
            stage('attn')
            for c in range(16):
                bank = proj_fm(AG + c, xT_rhs, ["xT"], 512)
                t0 = ft[c % 2]
                S.op("act", lambda e, bank=bank, t0=t0: e.activation(out=t0[:], in_=ps[:, bank, :], func=AF.Tanh, scale=0.5),
                     reads=[("ps", bank)], writes=[("ft", c % 2)])
                S.op("dve", lambda e, bank=bank, t0=t0: e.scalar_tensor_tensor(out=t0[:], in0=t0[:], scalar=1.0, in1=ps[:, bank, :], op0=ALU.add, op1=ALU.mult),
                     reads=[("ps", bank), ("ft", c % 2)], writes=[("ft", c % 2)])
                S.op("dve", lambda e, t0=t0, c=c: e.scalar_tensor_tensor(out=hg[:, c, :], in0=t0[:], scalar=0.5, in1=hg[:, c, :], op0=ALU.mult, op1=ALU.mult),
                     reads=[("ft", c % 2), ("hg", c)], writes=[("hg", c)])
            if dbg and m == 0:
                for c in range(16):
                    S.op("act", lambda e, c=c: e.activation(out=ft[2][:], in_=hg[:, c, :], func=AF.Copy), reads=[("hg", c)], writes=[("ft", 2)])
                    S.dma("sp", lambda e, c=c: e.dma_start(out=dbg_out["d_ga"][:, c * 512:(c + 1) * 512], in_=ft[2][:]), "dbg4", reads=[("ft", 2)], final=True)
            stage('gate')
            for jc in range(16):
                ba = proj_fm(WA + jc, lambda k: hg[:, k, :], hgk, 512)
                bb = proj_fm(GA + jc, xT_rhs, ["xT"], 512)
                t0 = ft[jc % 2]
                S.op("act", lambda e, bb=bb, t0=t0: e.activation(out=t0[:], in_=ps[:, bb, :], func=AF.Tanh, scale=0.5),
                     reads=[("ps", bb)], writes=[("ft", jc % 2)])
                S.op("dve", lambda e, ba=ba, t0=t0: e.scalar_tensor_tensor(out=t0[:], in0=t0[:], scalar=1.0, in1=ps[:, ba, :], op0=ALU.add, op1=ALU.mult),
                     reads=[("ps", ba), ("ft", jc % 2)], writes=[("ft", jc % 2)])
                S.op("dve", lambda e, t0=t0, jc=jc: e.scalar_tensor_tensor(out=merged[:, jc, :], in0=t0[:], scalar=0.5, in1=merged[:, jc, :], op0=ALU.mult, op1=ALU.add),
                     reads=[("ft", jc % 2), ("mrg", jc)], writes=[("mrg", jc)])
            if dbg and m == 0:
                for c in range(16):
                    S.op("act", lambda e, c=c: e.activation(out=ft[2][:], in_=merged[:, c, :], func=AF.Copy), reads=[("mrg", c)], writes=[("ft", 2)])
                    S.dma("sp", lambda e, c=c: e.dma_start(out=dbg_out["d_mrg"][:, c * 512:(c + 1) * 512], in_=ft[2][:]), "dbg5", reads=[("ft", 2)], final=True)
            stage('brancha')
            mk_ = [("mrg", c) for c in range(16)]
            S.dma("sp", lambda e: e.dma_start(out=lng, in_=lng_d), "lng", writes=["lng"], region=RO)
            S.dma("sp", lambda e: e.dma_start(out=lnb, in_=lnb_d), "lnb", writes=["lnb"], region=RO)
            for tb in range(4):
                S.dma("sp", lambda e, tb=tb: e.dma_start(out=ysub[:, tb * 2048:(tb + 1) * 2048], in_=xtok_d[m, tb]), "xtk%d" % tb,
                      writes=[("ysub", tb)], region=RO)
            for jj in range(8):
                wsl = jj % 2
                S.dma("pool", lambda e, wsl=wsl, jj=jj: e.dma_start(out=wo[wsl], in_=wout_b[jj]), "wo%d" % wsl, reads=[("wob", jj)], writes=[("wo", wsl)], region=RO)
                for tb in range(4):
                    ob = gw()

                    def fo(e, ob=ob, wsl=wsl, tb=tb):
                        ins = None
                        for k in range(16):
                            ins = e.matmul(ps[:, ob, 0:256], lhsT=merged[:, k, tb * 128:(tb + 1) * 128], rhs=wo[wsl][:, k * 256:(k + 1) * 256],
                                           start=(k == 0), stop=(k == 15))
                        return ins
                    S.op("pe", fo, reads=mk_ + [("wo", wsl)], writes=[("ps", ob)], region=RO)
                    ys = ysub[:, tb * 2048 + jj * 256: tb * 2048 + (jj + 1) * 256]
                    S.op("dve", lambda e, ob=ob, ys=ys: e.scalar_tensor_tensor(out=ys, in0=ys, scalar=ALPHA, in1=ps[:, ob, 0:256], op0=ALU.mult, op1=ALU.add),
                         reads=[("ps", ob), ("ysub", tb)], writes=[("ysub", tb)], region=RO)
            for tb in range(4):
                yv = ysub[:, tb * 2048:(tb + 1) * 2048]
                mv = sm[:, 40:42]
                rs = sm[:, 42:43]
                for q in range(4):
                    S.op("dve", lambda e, yv=yv, q=q: e.bn_stats(out=bst[:, q, :], in_=yv[:, q * 512:(q + 1) * 512]),
                         reads=[("ysub", tb)], writes=[("bst", q)], region=RO)
                S.op("dve", lambda e, mv=mv: e.bn_aggr(out=mv, in_=bst[:].rearrange("p a b -> p (a b)")),
                     reads=[("bst", q) for q in range(4)], writes=["mv"])
                S.op("dve", lambda e, mv=mv, rs=rs: e.tensor_scalar(out=rs, in0=mv[:, 1:2], scalar1=EPS, scalar2=None, op0=ALU.add), reads=["mv"], writes=["rs"])
                S.op("pool", lambda e, rs=rs: e.tensor_tensor(out=rs, in0=rs, in1=mhalf[:], op=ALU.pow), reads=["rs", "mhalf"], writes=["rs"])
                S.op("dve", lambda e, yv=yv, mv=mv, rs=rs: e.tensor_scalar(out=yv, in0=yv, scalar1=mv[:, 0:1], scalar2=rs, op0=ALU.subtract, op1=ALU.mult),
                     reads=[("ysub", tb), "mv", "rs"], writes=[("ysub", tb)], region=RO)
                S.op("dve", lambda e, yv=yv: e.tensor_tensor(out=yv, in0=yv, in1=lng, op=ALU.mult), reads=[("ysub", tb), "lng"], writes=[("ysub", tb)], region=RO)
                S.op("dve", lambda e, yv=yv: e.tensor_tensor(out=yv, in0=yv, in1=lnb, op=ALU.add), reads=[("ysub", tb), "lnb"], writes=[("ysub", tb)], region=RO)
                S.dma("sp", lambda e, tb=tb, yv=yv: e.dma_start(out=y_d[m, tb], in_=yv), "yo%d" % tb, reads=[("ysub", tb)], region=RO, final=True)

        try:
            stage('init')
            for m in range(n_pairs):
                fullseq(2 * m, 0, m)
                if m == 0:
                    convert_rest()
                fullseq(2 * m + 1, 1, m)
                own(m)
        except _Stop:
            pass
        S.emit()
    return nc


def _t5_bucket_np(n):
    n = np.maximum(n, 0)
    nf = np.maximum(n, 1).astype(np.float32)
    large = 16 + (np.log(nf / np.float32(16)) / np.float32(math.log(128 / 16)) * np.float32(16)).astype(np.int32)
    large = np.minimum(large, 31)
    return np.where(n < 16, n, large)


def _fm(w):
    n = w.shape[1] // 128
    return np.ascontiguousarray(w.reshape(16, 128, n, 128).transpose(2, 1, 0, 3)).reshape(n, 128, 16 * 128)


def _vec(v):
    return np.ascontiguousarray(v.reshape(16, 128).T)


def make_inputs(x, w_in, kv_norm_g, w_uv, w_branch_a, conv_w, conv_b, w_gate_a, b_gate_a,
                w_gate_x, b_gate_x, lru_lambda, w_branch_b, rel_bias, w_out, ln_g, ln_b):
    w_in = w_in[0]
    chunks = []
    for base, n in ((QL, 32), (AG, 16), (QI, 8), (XR, 16), (RG, 16), (GA, 16), (GB, 16)):
        c0 = COL[base]
        chunks.append(_fm(w_in[:, c0:c0 + n * 128]))
    chunks.append(_fm(w_branch_a[0]))
    chunks.append(_fm(w_branch_b[0]))
    wfm = np.concatenate(chunks, axis=0)
    tokcols = np.concatenate([np.arange(4096, 4352), np.arange(7424, 7488), np.arange(7424, 7488), np.arange(7488, 7504)])
    wt = w_in[:, tokcols]
    wtok = np.ascontiguousarray(wt.reshape(4, 4, 128, 400).transpose(0, 2, 1, 3)).reshape(4, 128, 1600)
    wout = np.ascontiguousarray(w_out[0].reshape(16, 128, 8, 256).transpose(2, 1, 0, 3)).reshape(8, 128, 16 * 256)
    wuv = np.ascontiguousarray(w_uv[0].reshape(16, 2, 128, 128).transpose(2, 0, 1, 3)).reshape(128, 32 * 128)
    wg = np.ascontiguousarray(np.stack([w_gate_a[0], w_gate_x[0]], 0).transpose(2, 0, 1, 3)).reshape(128, 32 * 128)
    cvec = np.stack([_vec(conv_w[0, 0]), _vec(conv_w[0, 1]), _vec(conv_w[0, 2]), _vec(conv_w[0, 3]), _vec(conv_b[0]),
                     _vec(b_gate_a[0]), _vec(b_gate_x[0]), _vec(lru_lambda[0])], axis=1).reshape(128, 128)
    kvg = np.ascontiguousarray(np.broadcast_to(kv_norm_g[0][None, :], (128, 256)))
    lng = np.ascontiguousarray(np.broadcast_to(ln_g[0][None, :], (128, D)))
    lnb = np.ascontiguousarray(np.broadcast_to(ln_b[0][None, :], (128, D)))
    s_ = np.arange(128)[:, None]
    t_ = np.arange(128)[None, :]
    diag = rel_bias[_t5_bucket_np(t_ - s_)]
    prev = rel_bias[_t5_bucket_np(128 + t_ - s_)]
    diag = np.ascontiguousarray(diag.transpose(0, 2, 1))
    prev = np.ascontiguousarray(prev.transpose(0, 2, 1))
    b31 = np.ascontiguousarray(np.broadcast_to(rel_bias[31][None, :, None], (128, 16, 128)))
    pow2 = np.ascontiguousarray(np.broadcast_to((2.0 ** -np.arange(NIT, dtype=np.float64)).astype(np.float32)[None, :], (128, NIT)))
    shared = dict(wfm=wfm, wtok=wtok, wout=wout, wuv=wuv, wg=wg, cvec=np.ascontiguousarray(cvec), kvg=kvg, lng=lng, lnb=lnb,
                  b31=b31.reshape(128, 2048), pow2=pow2)
    in_maps = []
    for core in range(8):
        b, par = core // 2, core % 2
        xb = x[b]
        xT = np.ascontiguousarray(xb.reshape(NT, 512, 16, 128).transpose(0, 3, 2, 1)).reshape(NT, 128, 16 * 512)
        xTo = np.ascontiguousarray(xT[par::2])
        xtok = np.ascontiguousarray(xb.reshape(NP, 2, 4, 128, D)[:, par])
        sel = np.zeros((128, 2), np.float32)
        sel[:, par] = 1.0
        cb = np.zeros((4, 128, 1024), np.float32)
        for qb in range(4):
            tg = par * 512 + qb * 128 + np.arange(128)[:, None]
            sg = np.arange(1024)[None, :]
            cb[qb] = np.where(sg <= tg, 0.0, NEG)
        if par == 0:
            slots = [prev, diag, b31, b31]
        else:
            slots = [b31, b31, prev, diag]
        biasS = np.ascontiguousarray(np.stack(slots, axis=1)).reshape(128, 4 * 16 * 128).astype(np.float32)
        d = dict(shared)
        d.update(xT=xT, xTo=xTo, xtok=xtok, sel=sel, cbias=cb, biasS=biasS)
        in_maps.append(d)
    return in_maps


def kernel(**inputs):
    inputs = {k: np.asarray(v) for k, v in inputs.items()}
    in_maps = make_inputs(**inputs)
    nc = build_nc()
    res = run_bass_kernel_spmd(nc, in_maps, core_ids=list(range(8)))
    out = np.empty((4, T, D), np.float32)
    for core in range(8):
        b, par = core // 2, core % 2
        yv = res.results[core]["y"].reshape(NP, 512, D)
        out.reshape(4, NP, 2, 512, D)[b, :, par] = yv
    return out
```
